# Optimizing a Trainium2 kernel written in Bass

```python
import jax, jax.numpy as jnp
from jax import lax
import numpy as np

D_MODEL = 1024
BATCH = 2
SEQ = 8192
DEPTH = 1

CHUNK = 64
Q_BLOCK = 128
EPS = 1e-6

SB_HEADS = 8
SB_HEAD_DIM = 64
SB_WIDTH = SB_HEADS * SB_HEAD_DIM

MLA_HEADS = 8
MLA_NOPE_DIM = 64
MLA_ROPE_DIM = 32
MLA_V_DIM = 64
MLA_Q_LORA = 384
MLA_KV_LORA = 256
MLA_QK_DIM = MLA_NOPE_DIM + MLA_ROPE_DIM
MLA_WIDTH = MLA_HEADS * MLA_V_DIM
ROPE_THETA = 10000.0

N_BRANCH = 2
IN_SPLIT_SIZES = (SB_WIDTH, SB_WIDTH, SB_WIDTH, SB_WIDTH,
                  MLA_Q_LORA, MLA_KV_LORA, MLA_ROPE_DIM, MLA_WIDTH,
                  N_BRANCH * D_MODEL)
IN_COLS = sum(IN_SPLIT_SIZES)

kernel_name = "hybrid_stickbreaking_mla_gated_block"


def rmsnorm(x, g):
    x32 = x.astype(jnp.float32)
    inv = lax.rsqrt(jnp.mean(x32 * x32, axis=-1, keepdims=True) + EPS)
    return (x32 * inv).astype(x.dtype) * g


def rope_tables(seq_len):
    half = MLA_ROPE_DIM // 2
    inv_freq = ROPE_THETA ** (-jnp.arange(half, dtype=jnp.float32) / half)
    ang = jnp.arange(seq_len, dtype=jnp.float32)[:, None] * inv_freq[None, :]
    return jnp.cos(ang), jnp.sin(ang)


def apply_rope(x, cos, sin):
    half = x.shape[-1] // 2
    x32 = x.astype(jnp.float32)
    x1, x2 = x32[..., :half], x32[..., half:]
    c, s = cos[None, :, None, :], sin[None, :, None, :]
    return jnp.concatenate([x1 * c - x2 * s, x1 * s + x2 * c], axis=-1).astype(x.dtype)


def to_query_blocks(q):
    b, s, h, d = q.shape
    return q.reshape(b, s // Q_BLOCK, Q_BLOCK, h, d).transpose(1, 0, 3, 2, 4)


def from_query_blocks(o):
    nb, b, h, qb, d = o.shape
    return o.transpose(1, 0, 3, 2, 4).reshape(b, nb * qb, h, d)


def stick_breaking_attention(q, k, v):
    seq_len = q.shape[1]
    scale = q.shape[-1] ** -0.5
    key_pos = jnp.arange(seq_len)
    qb = to_query_blocks(q)

    def block(args):
        q_blk, i = args
        q_pos = i * Q_BLOCK + jnp.arange(Q_BLOCK)
        z = jnp.einsum("bhqd,bshd->bhqs", q_blk, k).astype(jnp.float32) * scale
        strict = key_pos[None, :] < q_pos[:, None]
        log_beta = jax.nn.log_sigmoid(z)
        log_1m = jnp.where(strict, jax.nn.log_sigmoid(-z), 0.0)
        later = lax.cumsum(log_1m, axis=3, reverse=True) - log_1m
        w = jnp.where(strict, jnp.exp(log_beta + later), 0.0)
        return jnp.einsum("bhqs,bshd->bhqd", w.astype(v.dtype), v)

    o = lax.map(block, (qb, jnp.arange(qb.shape[0])))
    return from_query_blocks(o)


def chunk_causal_softmax_attention(q, k, v):
    seq_len = q.shape[1]
    scale = q.shape[-1] ** -0.5
    key_chunk = jnp.arange(seq_len) // CHUNK
    qb = to_query_blocks(q)

    def block(args):
        q_blk, i = args
        q_chunk = (i * Q_BLOCK + jnp.arange(Q_BLOCK)) // CHUNK
        allowed = key_chunk[None, :] <= q_chunk[:, None]
        z = jnp.einsum("bhqd,bshd->bhqs", q_blk, k).astype(jnp.float32) * scale
        p = jax.nn.softmax(jnp.where(allowed, z, -jnp.inf), axis=-1)
        return jnp.einsum("bhqs,bshd->bhqd", p.astype(v.dtype), v)

    o = lax.map(block, (qb, jnp.arange(qb.shape[0])))
    return from_query_blocks(o)


def setup_inputs(seed: int = 0) -> dict:
    key = jax.random.key(seed)
    ks = jax.random.split(key, 16)
    f32 = jnp.float32

    def nrm(k, shape, fan_in):
        return jax.random.normal(k, shape, f32) * (fan_in ** -0.5)

    def gain(k, n):
        return 1.0 + 0.01 * jax.random.normal(k, (DEPTH, n), f32)

    return {
        "x": jax.random.normal(ks[0], (BATCH, SEQ, D_MODEL), f32),
        "norm_in_g": gain(ks[1], D_MODEL),
        "w_in": nrm(ks[2], (DEPTH, D_MODEL, IN_COLS), D_MODEL),
        "b_gate": 0.1 * jax.random.normal(ks[3], (DEPTH, N_BRANCH * D_MODEL), f32),
        "q_norm_g": gain(ks[4], MLA_Q_LORA),
        "w_q_up": nrm(ks[5], (DEPTH, MLA_Q_LORA, MLA_HEADS * MLA_QK_DIM), MLA_Q_LORA),
        "kv_norm_g": gain(ks[6], MLA_KV_LORA),
        "w_kv_up": nrm(ks[7], (DEPTH, MLA_KV_LORA, MLA_HEADS * (MLA_NOPE_DIM + MLA_V_DIM)), MLA_KV_LORA),
        "w_o_sb": nrm(ks[8], (DEPTH, SB_WIDTH, D_MODEL), SB_WIDTH),
        "w_o_mla": nrm(ks[9], (DEPTH, MLA_WIDTH, D_MODEL), MLA_WIDTH),
        "w_out": nrm(ks[10], (DEPTH, D_MODEL, D_MODEL), D_MODEL),
        "norm_f_g": 1.0 + 0.01 * jax.random.normal(ks[11], (D_MODEL,), f32),
    }


def reference(x, norm_in_g, w_in, b_gate, q_norm_g, w_q_up, kv_norm_g, w_kv_up,
              w_o_sb, w_o_mla, w_out, norm_f_g):
    b, s, _ = x.shape
    cos, sin = rope_tables(s)
    offsets = [int(o) for o in np.cumsum(IN_SPLIT_SIZES)[:-1]]

    for l in range(DEPTH):
        h = rmsnorm(x, norm_in_g[l])
        proj = jnp.einsum("bsd,de->bse", h, w_in[l])
        (q_sb, k_sb, v_sb, gate_sb, c_q, c_kv, k_rope,
         gate_mla, gate_logits) = jnp.split(proj, offsets, axis=-1)

        o_sb = stick_breaking_attention(
            q_sb.reshape(b, s, SB_HEADS, SB_HEAD_DIM),
            k_sb.reshape(b, s, SB_HEADS, SB_HEAD_DIM),
            v_sb.reshape(b, s, SB_HEADS, SB_HEAD_DIM)).reshape(b, s, SB_WIDTH)
        y_sb = jnp.einsum("bsc,cd->bsd", o_sb * jax.nn.silu(gate_sb), w_o_sb[l])

        q = jnp.einsum("bsr,re->bse", rmsnorm(c_q, q_norm_g[l]), w_q_up[l])
        q = q.reshape(b, s, MLA_HEADS, MLA_QK_DIM)
        q_nope, q_rope = q[..., :MLA_NOPE_DIM], q[..., MLA_NOPE_DIM:]
        q_rope = apply_rope(q_rope, cos, sin)
        kv = jnp.einsum("bsr,re->bse", rmsnorm(c_kv, kv_norm_g[l]), w_kv_up[l])
        kv = kv.reshape(b, s, MLA_HEADS, MLA_NOPE_DIM + MLA_V_DIM)
        k_nope, v_mla = kv[..., :MLA_NOPE_DIM], kv[..., MLA_NOPE_DIM:]
        k_rope = apply_rope(k_rope[:, :, None, :], cos, sin)
        q_full = jnp.concatenate([q_nope, q_rope], axis=-1)
        k_full = jnp.concatenate(
            [k_nope, jnp.broadcast_to(k_rope, (b, s, MLA_HEADS, MLA_ROPE_DIM))], axis=-1)
        o_mla = chunk_causal_softmax_attention(q_full, k_full, v_mla).reshape(b, s, MLA_WIDTH)
        y_mla = jnp.einsum("bsc,cd->bsd", o_mla * jax.nn.silu(gate_mla), w_o_mla[l])

        g = jax.nn.sigmoid(gate_logits + b_gate[l]).reshape(b, s, N_BRANCH, D_MODEL)
        merged = g[:, :, 0, :] * y_sb + g[:, :, 1, :] * y_mla
        x = x + jnp.einsum("bsd,de->bse", merged, w_out[l])

    return rmsnorm(x, norm_f_g)
```

```python
import numpy as np
from contextlib import ExitStack
import concourse.bass as bass
import concourse.mybir as mybir
from concourse.bass_utils import run_bass_kernel_spmd

F32 = mybir.dt.float32
BF16 = mybir.dt.bfloat16
AF = mybir.ActivationFunctionType
ALU = mybir.AluOpType

D = 1024
EPS = 1e-6
NEG = -30000.0
C_QSB, C_KSB, C_VSB, C_GSB, C_CQ, C_CKV, C_KR, C_GM, C_GL = 0, 512, 1024, 1536, 2048, 2432, 2688, 2720, 3232

ENGS = ["pe", "act", "dve", "pool", "sp"]
NDS = 16


class Dummy:
    def __getitem__(self, k):
        return self

    def __getattr__(self, k):
        return self

    def __call__(self, *a, **k):
        return self


DUMMY = Dummy()


class Tok:
    __slots__ = ("eng", "idx", "dma", "clock", "dclock")

    def __init__(self, eng, idx, dma, clock, dclock):
        self.eng = eng
        self.idx = idx
        self.dma = dma
        self.clock = clock
        self.dclock = dclock


class Res:
    __slots__ = ("w", "rs", "excl")

    def __init__(self, excl=False):
        self.w = None
        self.rs = []
        self.excl = excl


class Prog:
    def __init__(self, nc, needed=None):
        self.nc = nc
        self.dry = nc is None
        self.needed = needed
        self.count = {e: 0 for e in ENGS}
        self.sig = {e: 0 for e in ENGS}
        self.sigval = {}
        self.ndma = 0
        self.ndk = {"h": 0, "s": 0}
        self.dtoks = {"h": [], "s": []}
        self.known = {e: {f: 0 for f in ENGS} for e in ENGS}
        self.dknown = {e: {} for e in ENGS}
        self.used = set()
        self.res = {}
        self.nwaits = 0
        if not self.dry:
            self.engobj = {"pe": nc.tensor, "act": nc.scalar, "dve": nc.vector,
                           "pool": nc.gpsimd, "sp": nc.sync}
            self.sems = {}

    def alloc_sems(self, stack):
        if self.dry:
            return
        for e in ENGS:
            self.sems[e] = stack.enter_context(self.nc.semaphore("s_" + e))
        self.dsem = {"h": [], "s": []}
        for i in range(NDS):
            self.dsem["h"].append(stack.enter_context(self.nc.semaphore("d_%d" % i)))
            self.dsem["s"].append(stack.enter_context(self.nc.semaphore("q_%d" % i)))

    def R(self, name):
        r = self.res.get(name)
        if r is None:
            excl = isinstance(name, str) and name.startswith("ps")
            r = Res(excl)
            self.res[name] = r
        return r

    def _knows(self, e, t):
        if t.dma:
            return self.dknown[e].get((t.dma, t.idx % NDS), -1) >= t.idx
        return self.known[e][t.eng] >= t.idx + 1

    def _merge(self, e, t):
        k = self.known[e]
        for f, v in t.clock.items():
            if v > k[f]:
                k[f] = v
        dk = self.dknown[e]
        for s, v in t.dclock.items():
            if v > dk.get(s, -1):
                dk[s] = v
        if t.dma:
            s = (t.dma, t.idx % NDS)
            if t.idx > dk.get(s, -1):
                dk[s] = t.idx
        else:
            if t.idx + 1 > k[t.eng]:
                k[t.eng] = t.idx + 1

    def _wait(self, e, t):
        if self._knows(e, t):
            return
        self.nwaits += 1
        self.used.add(("dma" + t.dma, t.idx) if t.dma else (t.eng, t.idx))
        if not self.dry:
            if t.dma:
                sem = self.dsem[t.dma][t.idx % NDS]
                val = 16 * (t.idx // NDS + 1)
            else:
                sem = self.sems[t.eng]
                val = self.sigval[(t.eng, t.idx)]
            self.engobj[e].wait_ge(sem, val)
        self._merge(e, t)

    def op(self, eng, fn, reads=(), writes=(), dma=False):
        deps = []
        rl = [self.R(n) for n in reads]
        wl = [self.R(n) for n in writes]
        for r in rl:
            if r.w is not None:
                deps.append((r.w, "raw"))
            if r.excl:
                for t in r.rs:
                    deps.append((t, "rr"))
        for r in wl:
            if r.w is not None:
                deps.append((r.w, "waw"))
            for t in r.rs:
                deps.append((t, "war"))
        best = {}
        for t, kind in deps:
            if (not t.dma) and (not dma) and t.eng == eng and (eng == "pe" or kind == "rr"):
                continue
            key = ("dma" + t.dma, t.idx % NDS) if t.dma else ("eng", t.eng)
            b = best.get(key)
            if b is None or t.idx > b.idx:
                best[key] = t
        for key in sorted(best.keys()):
            self._wait(eng, best[key])
        if dma:
            kind = "s" if eng == "pool" else "h"
            idx = self.ndk[kind]
            if idx >= NDS:
                self._wait(eng, self.dtoks[kind][idx - NDS])
            self.ndk[kind] += 1
            self.ndma += 1
            tok = Tok(eng, idx, kind, dict(self.known[eng]), dict(self.dknown[eng]))
            self.dtoks[kind].append(tok)
            if not self.dry:
                ins = fn(self.engobj[eng])
                ins.then_inc(self.dsem[kind][idx % NDS], 16)
        else:
            idx = self.count[eng]
            self.count[eng] += 1
            tok = Tok(eng, idx, None, dict(self.known[eng]), dict(self.dknown[eng]))
            if not self.dry:
                ins = fn(self.engobj[eng])
                if (eng, idx) in self.needed:
                    self.sig[eng] += 1
                    self.sigval[(eng, idx)] = self.sig[eng]
                    ins.then_inc(self.sems[eng], 1)
        for r in rl:
            r.rs.append(tok)
        for r in wl:
            r.w = tok
            r.rs = []
        return tok

    def wait_all(self, eng):
        toks = []
        for r in self.res.values():
            if r.w is not None:
                toks.append(r.w)
            toks.extend(r.rs)
        best = {}
        for t in toks:
            if (not t.dma) and t.eng == eng:
                continue
            key = ("dma" + t.dma, t.idx % NDS) if t.dma else ("eng", t.eng)
            b = best.get(key)
            if b is None or t.idx > b.idx:
                best[key] = t
        for key in sorted(best.keys()):
            self._wait(eng, best[key])

    def barrier(self):
        for e in ENGS:
            self.wait_all(e)


def build(P, S, debug=False):
    nc = P.nc
    dry = P.dry
    NT = S // 512
    NB = S // 128
    SO = S // 4
    NQ = SO // 512
    NBO = SO // 128

    def dram(name, shape, dt, kind):
        if dry:
            return DUMMY
        return nc.dram_tensor(name, list(shape), dt, kind=kind).ap()

    xf = dram("xf", [S, D], F32, "ExternalInput")
    xo = dram("xo", [SO, D], F32, "ExternalInput")
    w_in = dram("w_in", [D, 5280], F32, "ExternalInput")
    gin_d = dram("gin", [128, 8], F32, "ExternalInput")
    bg_d = dram("bg", [128, 16], F32, "ExternalInput")
    gq_d = dram("gq", [128, 3], F32, "ExternalInput")
    wq_d = dram("wq", [384, 768], F32, "ExternalInput")
    gkv_d = dram("gkv", [128, 2], F32, "ExternalInput")
    wkv_d = dram("wkv", [256, 1024], F32, "ExternalInput")
    wosb_d = dram("wosb", [512, 1024], F32, "ExternalInput")
    womla_d = dram("womla", [512, 1024], F32, "ExternalInput")
    wout_d = dram("wout", [1024, 1024], F32, "ExternalInput")
    gf_d = dram("gf", [128, 1024], F32, "ExternalInput")
    cosf_d = dram("cosf", [128, NB, 16], F32, "ExternalInput")
    sinf_d = dram("sinf", [128, NB, 16], F32, "ExternalInput")
    coso_d = dram("coso", [128, NBO, 4, 16], F32, "ExternalInput")
    sino_d = dram("sino", [128, NBO, 4, 16], F32, "ExternalInput")
    cst_d = dram("cst", [128, 5, 128], F32, "ExternalInput")
    msb_d = dram("msb", [128, 4, 128], F32, "ExternalInput")
    mmla_d = dram("mmla", [128, 4, 128], F32, "ExternalInput")
    esel_d = dram("esel", [128, 128], F32, "ExternalInput")
    out_d = dram("out", [SO, D], F32, "ExternalOutput")
    skind = "ExternalOutput" if debug else "Internal"
    kTsb_d = dram("kTsb", [4, 128, S], BF16, skind)
    vsb_d = dram("vsb", [S, 512], BF16, skind)
    kfT_d = dram("kfT", [8, 96, S], BF16, skind)
    vmla_d = dram("vmla", [S, 512], BF16, skind)
    hoT_d = dram("hoT", [128, 8, SO], BF16, skind)

    root = ExitStack()
    P.alloc_sems(root)

    ARENA_BYTES = 206 * 1024
    arena = None if dry else root.enter_context(nc.sbuf_tensor("arena", [128, ARENA_BYTES // 2], BF16))
    freelist = [[0, ARENA_BYTES]]
    peak = [0]

    def a_free(blk):
        freelist.append(list(blk))
        freelist.sort()
        i = 0
        while i + 1 < len(freelist):
            if freelist[i][0] + freelist[i][1] == freelist[i + 1][0]:
                freelist[i][1] += freelist[i + 1][1]
                del freelist[i + 1]
            else:
                i += 1

    def sb(stack, name, shape, dt):
        isz = 4 if dt == F32 else 2
        n = 1
        for d in shape[1:]:
            n *= d
        nb = (n * isz + 63) // 64 * 64
        for fb in freelist:
            if fb[1] >= nb:
                off = fb[0]
                fb[0] += nb
                fb[1] -= nb
                break
        else:
            raise RuntimeError("SBUF arena full allocating %s (%d B); free=%s" % (name, nb, freelist))
        if fb[1] == 0:
            freelist.remove(fb)
        stack.callback(a_free, (off, nb))
        used = ARENA_BYTES - sum(f[1] for f in freelist)
        peak[0] = max(peak[0], used)
        if dry:
            return DUMMY
        v = arena[0:shape[0], off // 2: off // 2 + n * isz // 2]
        if dt == F32:
            v = v.bitcast(F32)
        if len(shape) == 2:
            return v
        names = " ".join("d%d" % i for i in range(1, len(shape)))
        kw = {"d%d" % i: shape[i] for i in range(1, len(shape) - 1)}
        return v.rearrange("p (%s) -> p %s" % (names, names), **kw)

    pf = []
    pbv = []
    for i in range(8):
        if dry:
            pf.append(DUMMY)
            pbv.append(DUMMY)
        else:
            t = root.enter_context(nc.psum_tensor("pb%d" % i, [128, 512], F32))
            pf.append(t)
            pbv.append(t.bitcast(BF16))
    PS = ["ps%d" % i for i in range(8)]

    cstb = sb(root, "cstb", [128, 5, 128], BF16)
    mskb = sb(root, "mskb", [128, 8, 128], BF16)
    eselb = sb(root, "eselb", [128, 128], BF16)
    scst = ExitStack()
    cstf = sb(scst, "cstf", [128, 5, 128], F32)
    mskf = sb(scst, "mskf", [128, 8, 128], F32)
    eself = sb(scst, "eself", [128, 128], F32)
    gin = sb(root, "gin_s", [128, 8], F32)
    bg = sb(root, "bg_s", [128, 16], F32)
    gq = sb(root, "gq_s", [128, 3], F32)
    gkv = sb(root, "gkv_s", [128, 2], F32)
    junk = sb(root, "junk", [128, 1024], BF16)

    P.op("sp", lambda e: e.dma_start(out=cstf[:], in_=cst_d), writes=["cstf"], dma=True)
    P.op("sp", lambda e: e.dma_start(out=mskf[:, 0:4, :], in_=msb_d), writes=["mskf"], dma=True)
    P.op("sp", lambda e: e.dma_start(out=mskf[:, 4:8, :], in_=mmla_d), writes=["mskf2"], dma=True)
    P.op("sp", lambda e: e.dma_start(out=eself[:], in_=esel_d), writes=["eself"], dma=True)
    P.op("sp", lambda e: e.dma_start(out=gin[:], in_=gin_d), writes=["gin"], dma=True)
    P.op("sp", lambda e: e.dma_start(out=bg[:], in_=bg_d), writes=["bg"], dma=True)
    P.op("sp", lambda e: e.dma_start(out=gq[:], in_=gq_d), writes=["gq"], dma=True)
    P.op("sp", lambda e: e.dma_start(out=gkv[:], in_=gkv_d), writes=["gkv"], dma=True)
    P.op("dve", lambda e: e.tensor_copy(out=cstb[:], in_=cstf[:]), reads=["cstf"], writes=["cstb"])
    P.op("dve", lambda e: e.tensor_copy(out=mskb[:], in_=mskf[:]), reads=["mskf", "mskf2"], writes=["mskb"])
    P.op("dve", lambda e: e.tensor_copy(out=eselb[:], in_=eself[:]), reads=["eself"], writes=["eselb"])
    identb = cstb[:, 0, :]
    trib = cstb[:, 1, :]
    monesb = cstb[:, 2, :]
    onesb = cstb[:, 3, :]
    swapb = cstb[:, 4, :]
    P.barrier()
    scst.close()

    wgl_s = dram("wgl_s", [128, 8, 2048], BF16, "Internal")
    wos_s = dram("wos_s", [128, 4, 1024], BF16, "Internal")
    wom_s = dram("wom_s", [128, 4, 1024], BF16, "Internal")
    wout_s = dram("wout_s", [128, 8, 1024], BF16, "Internal")
    spp = ExitStack()
    pst = [sb(spp, "pst%d" % i, [128, 2048], F32) for i in range(2)]
    pob = [sb(spp, "pob%d" % i, [128, 2048], BF16) for i in range(2)]
    pcnt = [0]

    def prep_piece(src2d, nk, c0, c1, g=None, dst_fn=None, dst_name=None, dram_dst=None, dram_name=None):
        i = pcnt[0] % 2
        pcnt[0] += 1
        n = c1 - c0
        st = pst[i][:, 0:nk * n].rearrange("p (k n) -> p k n", k=nk)
        ob = pob[i][:, 0:nk * n].rearrange("p (k n) -> p k n", k=nk)
        P.op("sp", lambda e: e.dma_start(out=st, in_=src2d[:, c0:c1].rearrange("(k p) n -> p k n", p=128)),
             writes=["pst%d" % i], dma=True)
        for k in range(nk):
            if dram_dst is not None:
                o = ob[:, k, :]
                oname = "pob%d" % i
            else:
                o = dst_fn(k, c0, c1)
                oname = dst_name
            if g is not None:
                P.op("dve", lambda e, k=k, o=o: e.tensor_scalar(out=o, in0=st[:, k, :], scalar1=g[:, k:k + 1],
                                                               scalar2=None, op0=ALU.mult),
                     reads=["pst%d" % i, "gin", "gq", "gkv"], writes=[oname])
            else:
                P.op("act", lambda e, k=k, o=o: e.copy(out=o, in_=st[:, k, :]), reads=["pst%d" % i], writes=[oname])
        if dram_dst is not None:
            P.op("pool", lambda e: e.dma_start(out=dram_dst[:, :, c0:c1], in_=ob), reads=["pob%d" % i],
                 writes=[dram_name], dma=True)

    def prep_tasks(src2d, nk, ncols, **kw):
        return [(lambda c0=c0: prep_piece(src2d, nk, c0, min(c0 + 256, ncols), **kw)) for c0 in range(0, ncols, 256)]

    def make_front(xsrc, xbuf, ss8, inv8, hnb, hTb, tag, tbanks):
        def load(ti):
            for a in range(4):
                n = ti * 4 + a
                xb = xbuf[a]
                P.op("sp", lambda e, xb=xb, n=n: e.dma_start(out=xb[:], in_=xsrc[n * 128:(n + 1) * 128, :]),
                     writes=["x%s%d" % (tag, a)], dma=True)

        def f1(ti, a):
            p = ti % 2
            xb = xbuf[a]
            xn = "x%s%d" % (tag, a)
            c = p * 4 + a
            sn = "ss%s%d" % (tag, c)
            P.op("act", lambda e: e.activation(out=junk[:], in_=xb[:], func=AF.Square, accum_out=ss8[:, c:c + 1]),
                 reads=[xn], writes=["junk", sn])
            P.op("act", lambda e: e.activation(out=inv8[:, c:c + 1], in_=ss8[:, c:c + 1], func=AF.Ln,
                                               scale=1.0 / D, bias=EPS), reads=[sn], writes=[sn + "i"])
            P.op("act", lambda e: e.activation(out=inv8[:, c:c + 1], in_=inv8[:, c:c + 1], func=AF.Exp,
                                               scale=-0.5), reads=[sn + "i"], writes=[sn + "i"])
            P.op("dve", lambda e: e.tensor_scalar(out=hnb[p][:, a, :], in0=xb[:], scalar1=inv8[:, c:c + 1],
                                                  scalar2=None, op0=ALU.mult),
                 reads=[xn, sn + "i"], writes=["hn%s%d_%d" % (tag, p, a)])

        def f2(ti, a):
            p = ti % 2
            hT = hTb[p]
            bk = tbanks[a % len(tbanks)]
            hname = "hn%s%d_%d" % (tag, p, a)
            for k in range(8):
                P.op("pe", lambda e, k=k: e.transpose(out=pbv[bk][:, k * 128:(k + 1) * 128],
                                                      in_=hnb[p][:, a, k * 128:(k + 1) * 128], identity=identb),
                     reads=[hname, "cstb"], writes=[PS[bk]])
            src = pbv[bk][:, :].rearrange("p (k t) -> p k t", k=8)
            dst = hT[:, :, a * 128:(a + 1) * 128]
            if a % 2 == 0:
                P.op("act", lambda e: e.copy(out=dst, in_=src), reads=[PS[bk]], writes=["hT%s%d" % (tag, p)])
            else:
                P.op("dve", lambda e: e.tensor_copy(out=dst, in_=src), reads=[PS[bk]], writes=["hT%s%d" % (tag, p)])
        return load, f1, f2

    evc = [0]

    def evac(src, dst, reads, writes, scale=None, func=None, bias=None, eng=None):
        if func is not None or scale is not None or bias is not None:
            kw = {}
            if scale is not None:
                kw["scale"] = scale
            if bias is not None:
                kw["bias"] = bias
            f = func if func is not None else AF.Copy
            P.op("act", lambda e: e.activation(out=dst, in_=src, func=f, **kw), reads=reads, writes=writes)
            return
        if eng is None:
            eng = "act" if evc[0] % 2 == 0 else "dve"
            evc[0] += 1
        if eng == "act":
            P.op("act", lambda e: e.copy(out=dst, in_=src), reads=reads, writes=writes)
        else:
            P.op("dve", lambda e: e.tensor_copy(out=dst, in_=src), reads=reads, writes=writes)

    def rms_small(src_fn, ncols, ssq, invs, name):
        for a in range(4):
            P.op("act", lambda e, a=a: e.activation(out=junk[:, 0:ncols], in_=src_fn(a), func=AF.Square,
                                                    accum_out=ssq[:, a:a + 1]),
                 reads=[name], writes=["junk", name + "_ss"])
        P.op("act", lambda e: e.activation(out=invs[:], in_=ssq[:], func=AF.Ln, scale=1.0 / ncols, bias=EPS),
             reads=[name + "_ss"], writes=[name + "_inv"])
        P.op("act", lambda e: e.activation(out=invs[:], in_=invs[:], func=AF.Exp, scale=-0.5),
             reads=[name + "_inv"], writes=[name + "_inv"])

    swb = ExitStack()
    WBq = sb(swb, "WBq", [128, 8, 512], BF16)
    WBg = sb(swb, "WBg", [128, 8, 512], BF16)
    WBc = sb(swb, "WBc", [128, 8, 384], BF16)
    WBm = sb(swb, "WBm", [128, 8, 512], BF16)
    WQ = sb(swb, "WQ", [128, 3, 768], BF16)
    later = []
    later += prep_tasks(w_in[:, C_QSB:C_QSB + 512], 8, 512, g=gin, dst_fn=lambda k, a, b: WBq[:, k, a:b], dst_name="WBq")
    later += prep_tasks(w_in[:, C_GSB:C_GSB + 512], 8, 512, g=gin, dst_fn=lambda k, a, b: WBg[:, k, a:b], dst_name="WBg")
    later += prep_tasks(w_in[:, C_CQ:C_CQ + 384], 8, 384, g=gin, dst_fn=lambda k, a, b: WBc[:, k, a:b], dst_name="WBc")
    later += prep_tasks(w_in[:, C_GM:C_GM + 512], 8, 512, g=gin, dst_fn=lambda k, a, b: WBm[:, k, a:b], dst_name="WBm")
    later += prep_tasks(wq_d, 3, 768, g=gq, dst_fn=lambda k, a, b: WQ[:, k, a:b], dst_name="WQ")
    later += prep_tasks(w_in[:, C_GL:C_GL + 2048], 8, 2048, g=gin, dram_dst=wgl_s, dram_name="wgl_s")
    later += prep_tasks(wosb_d, 4, 1024, dram_dst=wos_s, dram_name="wos_s")
    later += prep_tasks(womla_d, 4, 1024, dram_dst=wom_s, dram_name="wom_s")
    later += prep_tasks(wout_d, 8, 1024, dram_dst=wout_s, dram_name="wout_s")
    per_tile = (len(later) + NT - 1) // NT

    with ExitStack() as sa:
        WAk = sb(sa, "WAk", [128, 8, 512], BF16)
        WAv = sb(sa, "WAv", [128, 8, 512], BF16)
        WAc = sb(sa, "WAc", [128, 8, 288], BF16)
        WKVK = sb(sa, "WKVK", [128, 2, 8, 128], BF16)
        WKVV = sb(sa, "WKVV", [128, 2, 512], BF16)
        xbuf = [sb(sa, "xA%d" % i, [128, 1024], F32) for i in range(4)]
        ss8 = sb(sa, "ss8A", [128, 8], F32)
        inv8 = sb(sa, "inv8A", [128, 8], F32)
        hnb = [sb(sa, "hnA%d" % i, [128, 4, 1024], BF16) for i in range(2)]
        hTb = [sb(sa, "hTA%d" % i, [128, 8, 512], BF16) for i in range(2)]
        loadA, f1A, f2A = make_front(xf, xbuf, ss8, inv8, hnb, hTb, "A", [0, 1, 6, 7])
        kst = [sb(sa, "kst%d" % i, [128, 4, 512], BF16) for i in range(2)]
        vst = [sb(sa, "vst%d" % i, [128, 4, 512], BF16) for i in range(2)]
        ckvs = sb(sa, "ckvs", [128, 4, 288], F32)
        ssk = sb(sa, "ssk", [128, 4], F32)
        invk = sb(sa, "invk", [128, 4], F32)
        kvn = sb(sa, "kvn", [128, 4, 256], BF16)
        kr = sb(sa, "kr", [128, 4, 32], BF16)
        rt1 = sb(sa, "rt1", [128, 4, 16], F32)
        rt2 = sb(sa, "rt2", [128, 4, 16], F32)
        rt3 = sb(sa, "rt3", [128, 4, 16], F32)
        rt4 = sb(sa, "rt4", [128, 4, 16], F32)
        kvnT = [sb(sa, "kvnT%d" % i, [128, 2, 512], BF16) for i in range(2)]
        krT = [sb(sa, "krT%d" % i, [128, 512], BF16) for i in range(2)]
        for i in range(2):
            P.op("pool", lambda e, i=i: e.memset(krT[i][32:64, :], 0.0), writes=["krT%d" % i])
            P.op("pool", lambda e, i=i: e.memset(krT[i][64:128, :], 0.0), writes=["krT%d" % i])
        kfst = [sb(sa, "kfst%d" % i, [96, 8, 512], BF16) for i in range(2)]
        vmst = [sb(sa, "vmst%d" % i, [128, 4, 512], BF16) for i in range(2)]
        cosf = sb(sa, "cosf", [128, NB, 16], F32)
        sinf = sb(sa, "sinf", [128, NB, 16], F32)
        P.op("sp", lambda e: e.dma_start(out=cosf[:], in_=cosf_d), writes=["cosf"], dma=True)
        P.op("sp", lambda e: e.dma_start(out=sinf[:], in_=sinf_d), writes=["sinf"], dma=True)

        xc = [0]
        pj = [0]

        def pbank():
            b = 2 + (pj[0] % 4)
            pj[0] += 1
            return b

        loadA(0)
        for a in range(4):
            f1A(0, a)
        for t in prep_tasks(w_in[:, C_KSB:C_KSB + 512], 8, 512, g=gin, dst_fn=lambda k, a, b: WAk[:, k, a:b], dst_name="WAk"):
            t()
        for t in prep_tasks(w_in[:, C_VSB:C_VSB + 512], 8, 512, g=gin, dst_fn=lambda k, a, b: WAv[:, k, a:b], dst_name="WAv"):
            t()
        for t in prep_tasks(w_in[:, C_CKV:C_CKV + 288], 8, 288, g=gin, dst_fn=lambda k, a, b: WAc[:, k, a:b], dst_name="WAc"):
            t()
        i = pcnt[0] % 2
        pcnt[0] += 1
        stv = pst[i]
        P.op("sp", lambda e: e.dma_start(out=stv[:, 0:2048].rearrange("p (k n) -> p k n", k=2),
                                         in_=wkv_d.rearrange("(k p) n -> p k n", p=128)),
             writes=["pst%d" % i], dma=True)
        P.op("dve", lambda e: e.memset(WKVK[:], 0.0), writes=["WKVK"])
        for rc in range(2):
            src = stv[:, rc * 1024:(rc + 1) * 1024].rearrange("p (h c) -> p h c", h=8)
            P.op("dve", lambda e, rc=rc, src=src: e.tensor_scalar(out=WKVK[:, rc, :, 0:64], in0=src[:, :, 0:64],
                                                                 scalar1=gkv[:, rc:rc + 1], scalar2=None,
                                                                 op0=ALU.mult),
                 reads=["pst%d" % i, "gkv"], writes=["WKVK"])
            P.op("dve", lambda e, rc=rc, src=src: e.tensor_scalar(
                out=WKVV[:, rc, :].rearrange("p (h c) -> p h c", h=8), in0=src[:, :, 64:128],
                scalar1=gkv[:, rc:rc + 1], scalar2=None, op0=ALU.mult),
                 reads=["pst%d" % i, "gkv"], writes=["WKVV"])
        for a in range(4):
            f2A(0, a)
        for ti in range(NT):
            bi = ti % 2
            hT = hTb[bi]
            hTn = "hTA%d" % bi
            if ti + 1 < NT:
                loadA(ti + 1)
            for a in range(4):
                b = pbank()
                for k in range(8):
                    P.op("pe", lambda e, b=b, k=k, a=a: e.matmul(pf[b][:, 0:288], lhsT=hT[:, k, a * 128:(a + 1) * 128],
                                                                 rhs=WAc[:, k, :], start=(k == 0), stop=(k == 7)),
                         reads=["WAc", hTn], writes=[PS[b]])
                evac(pf[b][:, 0:288], ckvs[:, a, :], [PS[b]], ["ckvs"])
            rms_small(lambda a: ckvs[:, a, 0:256], 256, ssk, invk, "ckvs")
            for a in range(4):
                P.op("dve", lambda e, a=a: e.tensor_scalar(out=kvn[:, a, :], in0=ckvs[:, a, 0:256],
                                                          scalar1=invk[:, a:a + 1], scalar2=None, op0=ALU.mult),
                     reads=["ckvs", "ckvs_inv"], writes=["kvn"])
            cs = cosf[:, ti * 4:(ti + 1) * 4, :]
            sn = sinf[:, ti * 4:(ti + 1) * 4, :]
            x1 = ckvs[:, :, 256:272]
            x2 = ckvs[:, :, 272:288]
            P.op("dve", lambda e: e.tensor_tensor(out=rt1[:], in0=x1, in1=cs, op=ALU.mult), reads=["ckvs", "cosf"], writes=["rt1"])
            P.op("dve", lambda e: e.tensor_tensor(out=rt2[:], in0=x2, in1=sn, op=ALU.mult), reads=["ckvs", "sinf"], writes=["rt2"])
            P.op("dve", lambda e: e.tensor_tensor(out=kr[:, :, 0:16], in0=rt1[:], in1=rt2[:], op=ALU.subtract),
                 reads=["rt1", "rt2"], writes=["kr1"])
            P.op("dve", lambda e: e.tensor_tensor(out=rt3[:], in0=x1, in1=sn, op=ALU.mult), reads=["ckvs", "sinf"], writes=["rt3"])
            P.op("dve", lambda e: e.tensor_tensor(out=rt4[:], in0=x2, in1=cs, op=ALU.mult), reads=["ckvs", "cosf"], writes=["rt4"])
            P.op("dve", lambda e: e.tensor_tensor(out=kr[:, :, 16:32], in0=rt3[:], in1=rt4[:], op=ALU.add),
                 reads=["rt3", "rt4"], writes=["kr2"])
            for hp in range(4):
                b = pbank()
                for k in range(8):
                    P.op("pe", lambda e, b=b, k=k, hp=hp: e.matmul(pf[b][:, :], lhsT=WAk[:, k, hp * 128:(hp + 1) * 128],
                                                                  rhs=hT[:, k, :], start=(k == 0), stop=(k == 7)),
                         reads=["WAk", hTn], writes=[PS[b]])
                evac(pf[b][:, :], kst[bi][:, hp, :], [PS[b]], ["kst%d" % bi])
                if ti + 1 < NT:
                    f1A(ti + 1, hp)
            P.op("pool", lambda e, ti=ti, bi=bi: e.dma_start(
                out=kTsb_d[:, :, ti * 512:(ti + 1) * 512].rearrange("h p s -> p h s"), in_=kst[bi][:]),
                 reads=["kst%d" % bi], writes=["kTsb_d"], dma=True)
            for a in range(4):
                for rc in range(2):
                    P.op("pe", lambda e, a=a, rc=rc: e.transpose(
                        out=pbv[6][:, rc * 512 + a * 128: rc * 512 + (a + 1) * 128],
                        in_=kvn[:, a, rc * 128:(rc + 1) * 128], identity=identb),
                         reads=["kvn", "cstb"], writes=[PS[6]])
                P.op("pe", lambda e, a=a: e.transpose(out=pbv[7][0:32, a * 128:(a + 1) * 128], in_=kr[:, a, :],
                                                      identity=identb),
                     reads=["kr1", "kr2", "cstb"], writes=[PS[7]])
            evac(pbv[6][:, :].rearrange("p (k t) -> p k t", k=2), kvnT[bi][:], [PS[6]], ["kvnT%d" % bi])
            evac(pbv[7][0:32, 0:512], krT[bi][0:32, :], [PS[7]], ["krT%d" % bi])
            for a in range(4):
                b = pbank()
                for k in range(8):
                    P.op("pe", lambda e, b=b, k=k, a=a: e.matmul(pf[b][:, :], lhsT=hT[:, k, a * 128:(a + 1) * 128],
                                                                 rhs=WAv[:, k, :], start=(k == 0), stop=(k == 7)),
                         reads=["WAv", hTn], writes=[PS[b]])
                evac(pf[b][:, :], vst[bi][:, a, :], [PS[b]], ["vst%d" % bi])
            P.op("pool", lambda e, ti=ti, bi=bi: e.dma_start(
                out=vsb_d[ti * 512:(ti + 1) * 512, :].rearrange("(a p) c -> p a c", p=128), in_=vst[bi][:]),
                 reads=["vst%d" % bi], writes=["vsb_d"], dma=True)
            for _ in range(per_tile):
                if later:
                    later.pop(0)()
            for h in range(8):
                b = pbank()
                P.op("pe", lambda e, b=b, h=h: e.matmul(pf[b][:, :], lhsT=WKVK[:, 0, h, :], rhs=kvnT[bi][:, 0, :],
                                                        start=True, stop=False),
                     reads=["WKVK", "kvnT%d" % bi], writes=[PS[b]])
                P.op("pe", lambda e, b=b, h=h: e.matmul(pf[b][:, :], lhsT=WKVK[:, 1, h, :], rhs=kvnT[bi][:, 1, :],
                                                        start=False, stop=False),
                     reads=["WKVK", "kvnT%d" % bi], writes=[PS[b]])
                P.op("pe", lambda e, b=b: e.matmul(pf[b][:, :], lhsT=eselb[:, :], rhs=krT[bi][:, :],
                                                   start=False, stop=True),
                     reads=["eselb", "krT%d" % bi], writes=[PS[b]])
                evac(pf[b][0:96, :], kfst[bi][:, h, :], [PS[b]], ["kfst%d" % bi])
            P.op("pool", lambda e, ti=ti, bi=bi: e.dma_start(
                out=kfT_d[:, :, ti * 512:(ti + 1) * 512].rearrange("h p s -> p h s"), in_=kfst[bi][:]),
                 reads=["kfst%d" % bi], writes=["kfT_d"], dma=True)
            for a in range(4):
                b = pbank()
                for rc in range(2):
                    P.op("pe", lambda e, b=b, rc=rc, a=a: e.matmul(pf[b][:, :], lhsT=kvnT[bi][:, rc, a * 128:(a + 1) * 128],
                                                                   rhs=WKVV[:, rc, :], start=(rc == 0), stop=(rc == 1)),
                         reads=["WKVV", "kvnT%d" % bi], writes=[PS[b]])
                evac(pf[b][:, :], vmst[bi][:, a, :], [PS[b]], ["vmst%d" % bi])
            P.op("pool", lambda e, ti=ti, bi=bi: e.dma_start(
                out=vmla_d[ti * 512:(ti + 1) * 512, :].rearrange("(a p) c -> p a c", p=128), in_=vmst[bi][:]),
                 reads=["vmst%d" % bi], writes=["vmla_d"], dma=True)
            if ti + 1 < NT:
                for a in range(4):
                    f2A(ti + 1, a)
        while later:
            later.pop(0)()
        P.barrier()
    spp.close()

    SGsb = sb(root, "SGsb", [128, 4, SO], BF16)
    SGmla = sb(root, "SGmla", [128, 4, SO], BF16)
    sq = ExitStack()
    QTsb = sb(sq, "QTsb", [128, 4, SO], BF16)
    QFT = sb(sq, "QFT", [128, 8, SO], BF16)
    P.op("pool", lambda e: e.memset(QFT[64:128, :, :], 0.0), writes=["QFT"])

    with ExitStack() as sbk:
        xbuf = [sb(sbk, "xB%d" % i, [128, 1024], F32) for i in range(4)]
        ss8 = sb(sbk, "ss8B", [128, 8], F32)
        inv8 = sb(sbk, "inv8B", [128, 8], F32)
        hnb = [sb(sbk, "hnB%d" % i, [128, 4, 1024], BF16) for i in range(2)]
        hTb = [sb(sbk, "hTB%d" % i, [128, 8, 512], BF16) for i in range(2)]
        loadB, f1B, f2B = make_front(xo, xbuf, ss8, inv8, hnb, hTb, "B", [0, 1, 6, 7])
        cqs = sb(sbk, "cqs", [128, 4, 384], F32)
        ssq_ = sb(sbk, "ssq", [128, 4], F32)
        invq = sb(sbk, "invq", [128, 4], F32)
        cqn = sb(sbk, "cqn", [128, 4, 384], BF16)
        cqnT = sb(sbk, "cqnT", [128, 3, 512], BF16)
        qf = [sb(sbk, "qf%d" % i, [128, 8, 96], BF16) for i in range(2)]
        qt1 = sb(sbk, "qt1", [128, 4, 16], F32)
        qt2 = sb(sbk, "qt2", [128, 4, 16], F32)
        qt3 = sb(sbk, "qt3", [128, 4, 16], F32)
        qt4 = sb(sbk, "qt4", [128, 4, 16], F32)
        coso = sb(sbk, "coso", [128, NBO, 4, 16], F32)
        sino = sb(sbk, "sino", [128, NBO, 4, 16], F32)
        P.op("sp", lambda e: e.dma_start(out=coso[:], in_=coso_d), writes=["coso"], dma=True)
        P.op("sp", lambda e: e.dma_start(out=sino[:], in_=sino_d), writes=["sino"], dma=True)

        xc = [0]
        pj = [0]

        def pbank():
            b = 2 + (pj[0] % 4)
            pj[0] += 1
            return b

        loadB(0)
        for a in range(4):
            f1B(0, a)
        for a in range(4):
            f2B(0, a)
        for tq in range(NQ):
            bi = tq % 2
            hT = hTb[bi]
            hTn = "hTB%d" % bi
            cols = slice(tq * 512, (tq + 1) * 512)
            if tq + 1 < NQ:
                loadB(tq + 1)
            P.op("pool", lambda e, tq=tq, hT=hT: e.dma_start(out=hoT_d[:, :, tq * 512:(tq + 1) * 512], in_=hT[:]),
                 reads=[hTn], writes=["hoT_d"], dma=True)
            for a in range(4):
                b = pbank()
                for k in range(8):
                    P.op("pe", lambda e, b=b, k=k, a=a: e.matmul(pf[b][:, 0:384], lhsT=hT[:, k, a * 128:(a + 1) * 128],
                                                                 rhs=WBc[:, k, :], start=(k == 0), stop=(k == 7)),
                         reads=["WBc", hTn], writes=[PS[b]])
                evac(pf[b][:, 0:384], cqs[:, a, :], [PS[b]], ["cqs"])
            rms_small(lambda a: cqs[:, a, :], 384, ssq_, invq, "cqs")
            for a in range(4):
                P.op("dve", lambda e, a=a: e.tensor_scalar(out=cqn[:, a, :], in0=cqs[:, a, :], scalar1=invq[:, a:a + 1],
                                                          scalar2=None, op0=ALU.mult),
                     reads=["cqs", "cqs_inv"], writes=["cqn"])
            for hp in range(4):
                b = pbank()
                for k in range(8):
                    P.op("pe", lambda e, b=b, k=k, hp=hp: e.matmul(pf[b][:, :], lhsT=WBq[:, k, hp * 128:(hp + 1) * 128],
                                                                  rhs=hT[:, k, :], start=(k == 0), stop=(k == 7)),
                         reads=["WBq", hTn], writes=[PS[b]])
                P.op("dve", lambda e, b=b, hp=hp, cols=cols: e.tensor_scalar(out=QTsb[:, hp, cols], in0=pf[b][:, :],
                                                                            scalar1=0.125, scalar2=None, op0=ALU.mult),
                     reads=[PS[b]], writes=["QTsb"])
                if tq + 1 < NQ:
                    f1B(tq + 1, hp)
            for (W, Wn, dstT, dn) in ((WBg, "WBg", SGsb, "SGsb"), (WBm, "WBm", SGmla, "SGmla")):
                for hp in range(4):
                    b = pbank()
                    for k in range(8):
                        P.op("pe", lambda e, b=b, k=k, hp=hp, W=W: e.matmul(pf[b][:, :], lhsT=W[:, k, hp * 128:(hp + 1) * 128],
                                                                           rhs=hT[:, k, :], start=(k == 0), stop=(k == 7)),
                             reads=[Wn, hTn], writes=[PS[b]])
                    P.op("act", lambda e, b=b, hp=hp, dstT=dstT, cols=cols: e.activation(out=dstT[:, hp, cols], in_=pf[b][:, :],
                                                                                        func=AF.Silu),
                         reads=[PS[b]], writes=[dn])
            for a in range(4):
                for rc in range(3):
                    bk = 6 if rc < 2 else 7
                    off = (rc % 2) * 512 + a * 128
                    P.op("pe", lambda e, a=a, rc=rc, bk=bk, off=off: e.transpose(
                        out=pbv[bk][:, off:off + 128], in_=cqn[:, a, rc * 128:(rc + 1) * 128], identity=identb),
                         reads=["cqn", "cstb"], writes=[PS[bk]])
            evac(pbv[6][:, :].rearrange("p (k t) -> p k t", k=2), cqnT[:, 0:2, :], [PS[6]], ["cqnT"])
            evac(pbv[7][:, 0:512], cqnT[:, 2, :], [PS[7]], ["cqnT"])
            def qX(a, tq=tq):
                n = tq * 4 + a
                qfb = qf[a % 2]
                qfn = "qf%d" % (a % 2)
                for half in range(2):
                    b = pbank()
                    for rc in range(3):
                        P.op("pe", lambda e, b=b, rc=rc, a=a, half=half: e.matmul(
                            pf[b][:, 0:384], lhsT=cqnT[:, rc, a * 128:(a + 1) * 128],
                            rhs=WQ[:, rc, half * 384:(half + 1) * 384], start=(rc == 0), stop=(rc == 2)),
                             reads=["WQ", "cqnT"], writes=[PS[b]])
                    pv = pf[b][:, 0:384].rearrange("p (h c) -> p h c", h=4)
                    dst = qfb[:, half * 4:(half + 1) * 4, :]
                    x1 = pv[:, :, 64:80]
                    x2 = pv[:, :, 80:96]
                    cs = coso[:, n, :, :]
                    sn = sino[:, n, :, :]
                    P.op("act", lambda e, pv=pv, dst=dst: e.copy(out=dst[:, :, 0:64], in_=pv[:, :, 0:64]),
                         reads=[PS[b]], writes=[qfn])
                    P.op("dve", lambda e, x1=x1, cs=cs: e.tensor_tensor(out=qt1[:], in0=x1, in1=cs, op=ALU.mult),
                         reads=[PS[b], "coso"], writes=["qt1"])
                    P.op("dve", lambda e, x2=x2, sn=sn: e.tensor_tensor(out=qt2[:], in0=x2, in1=sn, op=ALU.mult),
                         reads=[PS[b], "sino"], writes=["qt2"])
                    P.op("dve", lambda e, dst=dst: e.tensor_tensor(out=dst[:, :, 64:80], in0=qt1[:], in1=qt2[:], op=ALU.subtract),
                         reads=["qt1", "qt2"], writes=[qfn])
                    P.op("dve", lambda e, x1=x1, sn=sn: e.tensor_tensor(out=qt3[:], in0=x1, in1=sn, op=ALU.mult),
                         reads=[PS[b], "sino"], writes=["qt3"])
                    P.op("dve", lambda e, x2=x2, cs=cs: e.tensor_tensor(out=qt4[:], in0=x2, in1=cs, op=ALU.mult),
                         reads=[PS[b], "coso"], writes=["qt4"])
                    P.op("dve", lambda e, dst=dst: e.tensor_tensor(out=dst[:, :, 80:96], in0=qt3[:], in1=qt4[:], op=ALU.add),
                         reads=["qt3", "qt4"], writes=[qfn])

            def qY(a, tq=tq):
                qfb = qf[a % 2]
                qfn = "qf%d" % (a % 2)
                bk = a % 2
                for h in range(8):
                    P.op("pe", lambda e, h=h, bk=bk, qfb=qfb: e.transpose(out=pbv[bk][0:96, h * 128:(h + 1) * 128],
                                                                         in_=qfb[:, h, :], identity=identb),
                         reads=[qfn, "cstb"], writes=[PS[bk]])
                c0 = tq * 512 + a * 128
                evac(pbv[bk][0:96, :].rearrange("p (h t) -> p h t", h=8), QFT[0:96, :, c0:c0 + 128], [PS[bk]], ["QFT"])

            qX(0)
            for a in range(4):
                if a + 1 < 4:
                    qX(a + 1)
                qY(a)
            if tq + 1 < NQ:
                for a in range(4):
                    f2B(tq + 1, a)
        P.barrier()
    swb.close()

    if debug:
        dq1 = dram("dbg_qtsb", [128, 4, SO], BF16, "ExternalOutput")
        dq2 = dram("dbg_qft", [96, 8, SO], BF16, "ExternalOutput")
        dq3 = dram("dbg_sgsb", [128, 4, SO], BF16, "ExternalOutput")
        dq4 = dram("dbg_sgmla", [128, 4, SO], BF16, "ExternalOutput")
        P.op("sp", lambda e: e.dma_start(out=dq1, in_=QTsb[:]), reads=["QTsb"], writes=["dq1"], dma=True)
        P.op("sp", lambda e: e.dma_start(out=dq2, in_=QFT[0:96]), reads=["QFT"], writes=["dq2"], dma=True)
        P.op("sp", lambda e: e.dma_start(out=dq3, in_=SGsb[:]), reads=["SGsb"], writes=["dq3"], dma=True)
        P.op("sp", lambda e: e.dma_start(out=dq4, in_=SGmla[:]), reads=["SGmla"], writes=["dq4"], dma=True)
        P.barrier()

    def blocks(I):
        out = []
        for kb in range(16 * I + 15, -1, -1):
            m = kb - 16 * I
            if m >= 0:
                out.append((kb, 128 * (m // 4) + 32 * (m % 4), m % 4))
            else:
                out.append((kb, 0, None))
        return out

    with ExitStack() as sc:
        NCH = 4
        CB = NB // NCH
        KT = sb(sc, "KT", [128, S], BF16)
        VV = sb(sc, "VV", [128, NB, 128], BF16)
        KF = sb(sc, "KF", [128, S], BF16)
        VMx = [sb(sc, "VMe", [128, NB, 128], BF16), sb(sc, "VMo", [128, NB, 128], BF16)]
        rhi = sb(sc, "rhi", [128, 512], BF16)
        rlo = sb(sc, "rlo", [128, 512], BF16)
        ocp = sb(sc, "ocp", [128, 512], F32)
        eb = [sb(sc, "eb%d" % i, [128, 512], F32) for i in range(2)]
        spb = [sb(sc, "spb%d" % i, [128, 512], BF16) for i in range(3)]
        wb = [sb(sc, "wb%d" % i, [128, 512], BF16) for i in range(3)]
        ssum = [sb(sc, "ssum%d" % i, [128, 512], BF16) for i in range(3)]
        pbuf = [sb(sc, "pbuf%d" % i, [128, 512], BF16) for i in range(3)]
        rec = sb(sc, "rec", [128, 512], F32)
        otmp = sb(sc, "otmp", [128, 512], F32)
        SCALE = float(96 ** -0.5)
        B_SSB = [0, 1, 2, 3]
        B_SML = [4]
        B_OSB = 5
        B_OML = 6
        B_DML = 7

        def ld_kt(hp, c):
            P.op("sp", lambda e: e.dma_start(out=KT[:, c * CB * 128:(c + 1) * CB * 128],
                                             in_=kTsb_d[hp][:, c * CB * 128:(c + 1) * CB * 128]),
                 reads=["kTsb_d"], writes=["KTc%d" % c], dma=True)

        def ld_vv(hp, c):
            P.op("sp", lambda e: e.dma_start(
                out=VV[:, c * CB:(c + 1) * CB, :],
                in_=vsb_d[c * CB * 128:(c + 1) * CB * 128, hp * 128:(hp + 1) * 128].rearrange("(n p) c -> p n c", p=128)),
                 reads=["vsb_d"], writes=["VVc%d" % c], dma=True)

        def ld_kf(h, c):
            P.op("sp", lambda e: e.dma_start(out=KF[0:96, c * CB * 128:(c + 1) * CB * 128],
                                             in_=kfT_d[h][:, c * CB * 128:(c + 1) * CB * 128]),
                 reads=["kfT_d"], writes=["KFc%d" % c], dma=True)

        def ld_vm(h, c):
            par = h % 2
            vo = 0 if par == 0 else 64
            P.op("sp", lambda e: e.dma_start(
                out=VMx[par][:, c * CB:(c + 1) * CB, vo:vo + 64],
                in_=vmla_d[c * CB * 128:(c + 1) * CB * 128, h * 64:(h + 1) * 64].rearrange("(n p) c -> p n c", p=128)),
                 reads=["vmla_d"], writes=["VM%dc%d" % (par, c)], dma=True)

        jobs = []
        gidx = -1
        for h in range(8):
            for I in range(NQ - 1, -1, -1):
                bl = blocks(I)
                for j, (kb, c0, mi) in enumerate(bl):
                    if j == 0:
                        gidx += 1
                    jobs.append(dict(hp=h // 2, h=h, I=I, kb=kb, c0=c0, mi=mi, first=(j == 0), last=(j == len(bl) - 1),
                                     g=gidx, k=j, ch=kb // CB))
        nj = len(jobs)
        pc0 = 512
        for jb in jobs:
            if jb["first"]:
                pc0 = 512
            jb["pc0"] = pc0
            pc0 = jb["c0"]
        trig = {}
        PD = 5
        for idx, jb in enumerate(jobs):
            nxt = jobs[idx + 1] if idx + 1 < nj else None
            h, c = jb["h"], jb["ch"]
            if c * CB // 16 == jb["I"] or True:
                pass
        lastread = {}
        for idx, jb in enumerate(jobs):
            lastread[(jb["h"], jb["ch"])] = idx
        for (h, c), idx in lastread.items():
            trig.setdefault(idx + PD, []).append((h, c))

        P.op("dve", lambda e: e.memset(VMx[0][:, :, 64:128], 1.0), writes=["VM0c%d" % c for c in range(NCH)])
        P.op("dve", lambda e: e.memset(VMx[1][:, :, 0:64], 1.0), writes=["VM1c%d" % c for c in range(NCH)])
        P.op("pool", lambda e: e.memset(KF[64:128, :], 0.0), writes=["KFc%d" % c for c in range(NCH)])
        P.op("dve", lambda e: e.memset(rhi[:], 0.0), writes=["rhi"])
        P.op("dve", lambda e: e.memset(rlo[:], 0.0), writes=["rlo"])
        for c in range(NCH - 1, -1, -1):
            ld_kt(0, c)
            ld_vv(0, c)
            ld_kf(0, c)
            ld_vm(0, c)
        for c in range(NCH - 1, -1, -1):
            ld_vm(1, c)

        def sb1(j, jb):
            hp, h, I, kb, c0, mi, ch = jb["hp"], jb["h"], jb["I"], jb["kb"], jb["c0"], jb["mi"], jb["ch"]
            po = (h % 2) * 64
            bk = B_SSB[j % 4]
            kt = KT[po:po + 64, kb * 128:(kb + 1) * 128]
            qt = QTsb[po:po + 64, hp, I * 512 + c0:(I + 1) * 512]
            P.op("pe", lambda e: e.matmul(pf[bk][:, c0:512], lhsT=kt, rhs=qt, start=True, stop=(mi is None)),
                 reads=["KTc%d" % ch, "QTsb"], writes=[PS[bk]])
            if mi is not None:
                mo = 32 * mi
                P.op("pe", lambda e: e.matmul(pf[bk][:, c0:c0 + 128 - mo], lhsT=identb, rhs=mskb[:, mi, mo:128], start=False, stop=True),
                     reads=["cstb", "mskb"], writes=[PS[bk]])
            P.op("act", lambda e: e.activation(out=eb[j % 2][:, c0:512], in_=pf[bk][:, c0:512], func=AF.Exp),
                 reads=[PS[bk]], writes=["eb%d" % (j % 2)])

        def sb2(j, jb):
            c0 = jb["c0"]
            s3 = j % 3
            P.op("act", lambda e: e.activation(out=spb[s3][:, c0:512], in_=eb[j % 2][:, c0:512], func=AF.Ln, bias=1.0),
                 reads=["eb%d" % (j % 2)], writes=["spb%d" % s3])
            if not jb["last"]:
                rb = j % 3
                wbf = (j + 1) % 3
                if jb["first"]:
                    P.op("dve", lambda e: e.tensor_copy(out=ssum[wbf][:, c0:512], in_=spb[s3][:, c0:512]),
                         reads=["spb%d" % s3], writes=["ssum%d" % wbf])
                else:
                    pc0 = jb["pc0"]
                    P.op("dve", lambda e: e.tensor_tensor(out=ssum[wbf][:, pc0:512], in0=ssum[rb][:, pc0:512],
                                                          in1=spb[s3][:, pc0:512], op=ALU.add),
                         reads=["ssum%d" % rb, "spb%d" % s3], writes=["ssum%d" % wbf])
                    if pc0 > c0:
                        P.op("dve", lambda e: e.tensor_copy(out=ssum[wbf][:, c0:pc0], in_=spb[s3][:, c0:pc0]),
                             reads=["spb%d" % s3], writes=["ssum%d" % wbf])

        def sb3(j, jb):
            c0 = jb["c0"]
            ab = B_SSB[j % 4]
            s3 = j % 3
            gpar = j % 3
            if not jb["first"]:
                pc0 = jb["pc0"]
                P.op("pe", lambda e: e.matmul(pf[ab][:, pc0:512], lhsT=monesb, rhs=ssum[gpar][:, pc0:512],
                                              start=False, stop=False, skip_group_check=True),
                     reads=["cstb", "ssum%d" % gpar], writes=[PS[ab]])
            P.op("pe", lambda e: e.matmul(pf[ab][:, c0:512], lhsT=trib, rhs=spb[s3][:, c0:512],
                                          start=False, stop=True, skip_group_check=True),
                 reads=["cstb", "spb%d" % s3], writes=[PS[ab]])
            P.op("act", lambda e: e.activation(out=wb[s3][:, c0:512], in_=pf[ab][:, c0:512], func=AF.Exp),
                 reads=[PS[ab]], writes=["wb%d" % s3])

        def sb4(j, jb):
            hp, h, I, kb, c0, ch = jb["hp"], jb["h"], jb["I"], jb["kb"], jb["c0"], jb["ch"]
            s3 = j % 3
            ob = B_OSB
            po = (h % 2) * 64
            if jb["first"]:
                P.op("pe", lambda e: e.matmul(pf[ob][:, 0:512], lhsT=VV[:, kb, :], rhs=wb[s3][:, 0:512],
                                              start=True, stop=jb["last"], skip_group_check=True),
                     reads=["VVc%d" % ch, "wb%d" % s3], writes=[PS[ob]])
            else:
                P.op("pe", lambda e: e.matmul(pf[ob][:, c0:512], lhsT=VV[:, kb, :], rhs=wb[s3][:, c0:512],
                                              start=False, stop=jb["last"], skip_group_check=True),
                     reads=["VVc%d" % ch, "wb%d" % s3], writes=[PS[ob]])
            if jb["last"]:
                cols = slice(I * 512, (I + 1) * 512)
                P.op("dve", lambda e: e.tensor_tensor(out=SGsb[po:po + 64, hp, cols], in0=pf[ob][po:po + 64, :],
                                                      in1=SGsb[po:po + 64, hp, cols], op=ALU.mult),
                     reads=[PS[ob], "SGsb"], writes=["SGsb"])

        def ml1(j, jb):
            hp, h, I, kb, c0, mi, ch = jb["hp"], jb["h"], jb["I"], jb["kb"], jb["c0"], jb["mi"], jb["ch"]
            bk = B_SML[0]
            s3 = j % 3
            P.op("pe", lambda e: e.matmul(pf[bk][:, c0:512], lhsT=KF[:, kb * 128:(kb + 1) * 128],
                                          rhs=QFT[:, h, I * 512 + c0:(I + 1) * 512], start=True, stop=(mi is None)),
                 reads=["KFc%d" % ch, "QFT"], writes=[PS[bk]])
            if mi is not None:
                mo = 32 * mi
                P.op("pe", lambda e: e.matmul(pf[bk][:, c0:c0 + 128 - mo], lhsT=identb, rhs=mskb[:, 4 + mi, mo:128], start=False, stop=True),
                     reads=["cstb", "mskb"], writes=[PS[bk]])
            P.op("act", lambda e: e.activation(out=pbuf[s3][:, c0:512], in_=pf[bk][:, c0:512], func=AF.Exp, scale=SCALE),
                 reads=[PS[bk]], writes=["pbuf%d" % s3])

        def ml2(j, jb):
            hp, h, I, kb, c0, ch = jb["hp"], jb["h"], jb["I"], jb["kb"], jb["c0"], jb["ch"]
            s3 = j % 3
            ob = B_OML
            db = B_DML
            par = h % 2
            po = par * 64
            dq = 64 - po
            if jb["first"]:
                P.op("pe", lambda e: e.matmul(pf[ob][:, 0:512], lhsT=VMx[par][:, kb, :], rhs=pbuf[s3][:, 0:512],
                                              start=True, stop=jb["last"], skip_group_check=True),
                     reads=["VM%dc%d" % (par, ch), "pbuf%d" % s3], writes=[PS[ob]])
            else:
                P.op("pe", lambda e: e.matmul(pf[ob][:, c0:512], lhsT=VMx[par][:, kb, :], rhs=pbuf[s3][:, c0:512],
                                              start=False, stop=jb["last"], skip_group_check=True),
                     reads=["VM%dc%d" % (par, ch), "pbuf%d" % s3], writes=[PS[ob]])
            if jb["last"]:
                cols = slice(I * 512, (I + 1) * 512)
                P.op("dve", lambda e: e.tensor_copy(out=ocp[:, :], in_=pf[ob][:, :]), reads=[PS[ob]], writes=["ocp"])
                def fin1(q):
                    if q == 0:
                        P.op("act", lambda e: e.activation(out=rec[dq:dq + 64, :], in_=ocp[dq:dq + 64, :], func=AF.Ln),
                             reads=["ocp"], writes=["rec"])
                    else:
                        P.op("act", lambda e: e.activation(out=rec[dq:dq + 64, :], in_=rec[dq:dq + 64, :], func=AF.Exp,
                                                           scale=-1.0),
                             reads=["rec"], writes=["rec"])

                def fin1b():
                    P.op("dve", lambda e: e.tensor_copy(out=rhi[dq:dq + 64, :], in_=rec[dq:dq + 64, :]),
                         reads=["rec"], writes=["rhi"])
                    P.op("dve", lambda e: e.tensor_tensor(out=rlo[dq:dq + 64, :], in0=rec[dq:dq + 64, :],
                                                          in1=rhi[dq:dq + 64, :], op=ALU.subtract),
                         reads=["rec", "rhi"], writes=["rlo"])
                deferred.setdefault(cur[0] + 2, []).append(lambda: fin1(0))
                deferred.setdefault(cur[0] + 3, []).append(lambda: fin1(1))
                deferred.setdefault(cur[0] + 4, []).append(fin1b)
                def fin2():
                    P.op("pe", lambda e: e.matmul(pf[db][:, :], lhsT=swapb, rhs=rhi[:, :], start=True, stop=False),
                         reads=["cstb", "rhi"], writes=[PS[db]])
                    P.op("pe", lambda e: e.matmul(pf[db][:, :], lhsT=swapb, rhs=rlo[:, :], start=False, stop=True),
                         reads=["cstb", "rlo"], writes=[PS[db]])
                    P.op("dve", lambda e: e.tensor_tensor(out=otmp[po:po + 64, :], in0=pf[db][po:po + 64, :],
                                                          in1=ocp[po:po + 64, :], op=ALU.mult),
                         reads=[PS[db], "ocp"], writes=["otmp"])
                    P.op("dve", lambda e: e.tensor_tensor(out=SGmla[po:po + 64, hp, cols], in0=otmp[po:po + 64, :],
                                                          in1=SGmla[po:po + 64, hp, cols], op=ALU.mult),
                         reads=["otmp", "SGmla"], writes=["SGmla"])
                deferred.setdefault(cur[0] + 7, []).append(fin2)

        deferred = {}
        cur = [0]
        for j, jb in enumerate(jobs):
            if jb["first"] and jb["c0"] > 0:
                def zp(j=j, c0=jb["c0"]):
                    P.op("pool", lambda e: e.memset(pbuf[j % 3][:, 0:c0], 0.0), writes=["pbuf%d" % (j % 3)])

                def zw(j=j, c0=jb["c0"]):
                    P.op("pool", lambda e: e.memset(wb[j % 3][:, 0:c0], 0.0), writes=["wb%d" % (j % 3)])
                deferred.setdefault(max(j - 1, -1), []).append(zp)
                deferred.setdefault(j + 1, []).append(zw)
        for f in deferred.pop(-1, []):
            f()
        for step in range(nj + 3 + PD):
            cur[0] = step
            for (h, c) in trig.get(step, []):
                if h + 1 < 8:
                    ld_kf(h + 1, c)
                if h + 2 < 8:
                    ld_vm(h + 2, c)
                if h % 2 == 1 and h // 2 + 1 < 4:
                    ld_kt(h // 2 + 1, c)
                    ld_vv(h // 2 + 1, c)
            if step < nj:
                sb1(step, jobs[step])
                ml1(step, jobs[step])
            if 0 <= step - 1 < nj:
                sb2(step - 1, jobs[step - 1])
            if 0 <= step - 2 < nj:
                sb3(step - 2, jobs[step - 2])
            if 0 <= step - 1 < nj:
                ml2(step - 1, jobs[step - 1])
            if 0 <= step - 3 < nj:
                sb4(step - 3, jobs[step - 3])
            for f in deferred.pop(step, []):
                f()
        for k in sorted(deferred):
            for f in deferred[k]:
                f()
        P.barrier()

    if debug:
        dq5 = dram("dbg_ogsb", [128, 4, SO], BF16, "ExternalOutput")
        dq6 = dram("dbg_ogmla", [128, 4, SO], BF16, "ExternalOutput")
        P.op("sp", lambda e: e.dma_start(out=dq5, in_=SGsb[:]), reads=["SGsb"], writes=["dq5"], dma=True)
        P.op("sp", lambda e: e.dma_start(out=dq6, in_=SGmla[:]), reads=["SGmla"], writes=["dq6"], dma=True)
        P.barrier()
    sq.close()

    with ExitStack() as se:
        WGL = sb(se, "WGL", [128, 8, 2048], BF16)
        WOS = sb(se, "WOS", [128, 4, 1024], BF16)
        WOM = sb(se, "WOM", [128, 4, 1024], BF16)
        WOUT = sb(se, "WOUT", [128, 8, 1024], BF16)
        P.op("sp", lambda e: e.dma_start(out=WGL[:], in_=wgl_s), reads=["wgl_s"], writes=["WGL"], dma=True)
        P.op("sp", lambda e: e.dma_start(out=WOS[:], in_=wos_s), reads=["wos_s"], writes=["WOS"], dma=True)
        P.op("sp", lambda e: e.dma_start(out=WOM[:], in_=wom_s), reads=["wom_s"], writes=["WOM"], dma=True)
        P.op("sp", lambda e: e.dma_start(out=WOUT[:], in_=wout_s), reads=["wout_s"], writes=["WOUT"], dma=True)
        hoT = [sb(se, "hoT%d" % i, [128, 8, 512], BF16) for i in range(2)]
        G = sb(se, "G", [128, 16, 512], BF16)
        mg = [sb(se, "mg%d" % i, [128, 8, 512], BF16) for i in range(2)]
        mt1 = [sb(se, "mt1_%d" % i, [128, 512], F32) for i in range(2)]
        mt2 = [sb(se, "mt2_%d" % i, [128, 512], F32) for i in range(2)]
        xob = [sb(se, "xob%d" % i, [128, 1024], F32) for i in range(4)]
        resb = [sb(se, "resb%d" % i, [128, 1024], F32) for i in range(2)]
        gfs = sb(se, "gfs", [128, 1024], F32)
        ssf = sb(se, "ssf", [128, 2], F32)
        invf = sb(se, "invf", [128, 2], F32)
        pend = []
        P.op("sp", lambda e: e.dma_start(out=gfs[:], in_=gf_d), writes=["gfs"], dma=True)
        pj = [0]

        def ld_hoT(tq):
            hT = hoT[tq % 2]
            P.op("sp", lambda e: e.dma_start(out=hT[:], in_=hoT_d[:, :, tq * 512:(tq + 1) * 512]),
                 reads=["hoT_d"], writes=["hoT%d" % (tq % 2)], dma=True)

        def e_gl(tq):
            hT = hoT[tq % 2]
            hTn = "hoT%d" % (tq % 2)
            for mt in range(16):
                b = pj[0] % 2
                pj[0] += 1
                for k in range(8):
                    P.op("pe", lambda e, b=b, k=k, mt=mt: e.matmul(pf[b][:, :], lhsT=WGL[:, k, mt * 128:(mt + 1) * 128],
                                                                  rhs=hT[:, k, :], start=(k == 0), stop=(k == 7)),
                         reads=["WGL", hTn], writes=[PS[b]])
                P.op("act", lambda e, b=b, mt=mt: e.activation(out=G[:, mt, :], in_=pf[b][:, :], func=AF.Sigmoid,
                                                               bias=bg[:, mt:mt + 1]),
                     reads=[PS[b], "bg"], writes=["G"])

        def e_y(tq):
            cols = slice(tq * 512, (tq + 1) * 512)
            mgb = mg[tq % 2]
            for et in range(8):
                b1 = 2 + (et % 2) * 2
                b2 = b1 + 1
                m1 = mt1[et % 2]
                m2 = mt2[et % 2]
                for hp in range(4):
                    P.op("pe", lambda e, b1=b1, hp=hp, et=et: e.matmul(pf[b1][:, :], lhsT=WOS[:, hp, et * 128:(et + 1) * 128],
                                                                      rhs=SGsb[:, hp, cols], start=(hp == 0), stop=(hp == 3)),
                         reads=["WOS", "SGsb"], writes=[PS[b1]])
                for hp in range(4):
                    P.op("pe", lambda e, b2=b2, hp=hp, et=et: e.matmul(pf[b2][:, :], lhsT=WOM[:, hp, et * 128:(et + 1) * 128],
                                                                      rhs=SGmla[:, hp, cols], start=(hp == 0), stop=(hp == 3)),
                         reads=["WOM", "SGmla"], writes=[PS[b2]])
                P.op("dve", lambda e, b1=b1, et=et, m1=m1: e.tensor_tensor(out=m1[:], in0=pf[b1][:, :], in1=G[:, et, :], op=ALU.mult),
                     reads=[PS[b1], "G"], writes=["mt1_%d" % (et % 2)])
                P.op("dve", lambda e, b2=b2, et=et, m2=m2: e.tensor_tensor(out=m2[:], in0=pf[b2][:, :], in1=G[:, 8 + et, :], op=ALU.mult),
                     reads=[PS[b2], "G"], writes=["mt2_%d" % (et % 2)])
                P.op("pool", lambda e, et=et, m1=m1, m2=m2, mgb=mgb: e.tensor_tensor(out=mgb[:, et, :], in0=m1[:], in1=m2[:], op=ALU.add),
                     reads=["mt1_%d" % (et % 2), "mt2_%d" % (et % 2)], writes=["mg%d" % (tq % 2)])

        def ld_x(tq):
            for a in range(4):
                n = tq * 4 + a
                P.op("sp", lambda e, n=n, a=a: e.dma_start(out=xob[a][:], in_=xo[n * 128:(n + 1) * 128, :]),
                     writes=["xob%d" % a], dma=True)

        def e_out(tq):
            mgb = mg[tq % 2]
            for a in range(4):
                n = tq * 4 + a
                xi = n % 2
                for half in range(2):
                    b = 6 + half
                    for k in range(8):
                        P.op("pe", lambda e, b=b, k=k, a=a, half=half: e.matmul(
                            pf[b][:, :], lhsT=mgb[:, k, a * 128:(a + 1) * 128], rhs=WOUT[:, k, half * 512:(half + 1) * 512],
                            start=(k == 0), stop=(k == 7)),
                             reads=["mg%d" % (tq % 2), "WOUT"], writes=[PS[b]])
                    P.op("dve", lambda e, b=b, xi=xi, a=a, half=half: e.tensor_tensor(
                        out=resb[xi][:, half * 512:(half + 1) * 512], in0=pf[b][:, :],
                        in1=xob[a][:, half * 512:(half + 1) * 512], op=ALU.add),
                         reads=[PS[b], "xob%d" % a], writes=["resb%d" % xi])
                P.op("act", lambda e, xi=xi: e.activation(out=junk[:], in_=resb[xi][:], func=AF.Square,
                                                          accum_out=ssf[:, xi:xi + 1]),
                     reads=["resb%d" % xi], writes=["junk", "ssf%d" % xi])
                P.op("act", lambda e, xi=xi: e.activation(out=invf[:, xi:xi + 1], in_=ssf[:, xi:xi + 1], func=AF.Ln,
                                                          scale=1.0 / D, bias=EPS),
                     reads=["ssf%d" % xi], writes=["invf%d" % xi])
                P.op("act", lambda e, xi=xi: e.activation(out=invf[:, xi:xi + 1], in_=invf[:, xi:xi + 1], func=AF.Exp,
                                                          scale=-0.5),
                     reads=["invf%d" % xi], writes=["invf%d" % xi])

                def tail(n=n, xi=xi):
                    P.op("dve", lambda e: e.scalar_tensor_tensor(out=resb[xi][:], in0=resb[xi][:], scalar=invf[:, xi:xi + 1],
                                                                 in1=gfs[:], op0=ALU.mult, op1=ALU.mult),
                         reads=["resb%d" % xi, "invf%d" % xi, "gfs"], writes=["resb%d" % xi])
                    P.op("sp", lambda e: e.dma_start(out=out_d[n * 128:(n + 1) * 128, :], in_=resb[xi][:]),
                         reads=["resb%d" % xi], writes=["out_d"], dma=True)
                if pend:
                    pend.pop(0)()
                pend.append(tail)

        ld_hoT(0)
        for tq in range(NQ + 1):
            if tq + 1 < NQ:
                ld_hoT(tq + 1)
            if tq >= 1:
                ld_x(tq - 1)
            if tq < NQ:
                e_gl(tq)
            if tq >= 1:
                e_out(tq - 1)
            if tq < NQ:
                e_y(tq)
        while pend:
            pend.pop(0)()
        P.barrier()
    root.close()


def host_consts(S, c):
    NB = S // 128
    SO = S // 4
    NBO = SO // 128
    half = 16
    inv_freq = (np.float32(10000.0) ** (-np.arange(half, dtype=np.float32) / np.float32(half))).astype(np.float32)
    pos = np.arange(S, dtype=np.float32)
    ang = (pos[:, None] * inv_freq[None, :]).astype(np.float32)
    cosf = np.cos(ang).astype(np.float32)
    sinf = np.sin(ang).astype(np.float32)
    cf = np.ascontiguousarray(cosf.reshape(NB, 128, 16).transpose(1, 0, 2))
    sf = np.ascontiguousarray(sinf.reshape(NB, 128, 16).transpose(1, 0, 2))
    co = cosf[c::4].reshape(NBO, 128, 16).transpose(1, 0, 2)
    so = sinf[c::4].reshape(NBO, 128, 16).transpose(1, 0, 2)
    co4 = np.ascontiguousarray(np.broadcast_to(co[:, :, None, :], (128, NBO, 4, 16))).astype(np.float32)
    so4 = np.ascontiguousarray(np.broadcast_to(so[:, :, None, :], (128, NBO, 4, 16))).astype(np.float32)
    ident = np.eye(128, dtype=np.float32)
    jj = np.arange(128)
    tri = -(jj[:, None] >= jj[None, :]).astype(np.float32)
    swap = np.zeros((128, 128), np.float32)
    swap[(jj + 64) % 128, jj] = 1.0
    cst = np.ascontiguousarray(np.stack([ident, tri, -np.ones((128, 128), np.float32),
                                         np.ones((128, 128), np.float32), swap], axis=1))
    ss = np.arange(128)[:, None]
    qq = np.arange(128)[None, :]
    msb = np.zeros((128, 4, 128), np.float32)
    mmla = np.zeros((128, 4, 128), np.float32)
    for m in range(4):
        msb[:, m, :] = np.where(128 * m + ss < 4 * qq + c, 0.0, NEG)
        mmla[:, m, :] = np.where(2 * m + ss // 64 <= qq // 16, 0.0, NEG)
    esel = np.zeros((128, 128), np.float32)
    esel[np.arange(32), 64 + np.arange(32)] = 1.0
    return dict(cosf=cf, sinf=sf, coso=co4, sino=so4, cst=cst, msb=msb, mmla=mmla, esel=esel)


def make_in_maps(S, x, norm_in_g, w_in, b_gate, q_norm_g, w_q_up, kv_norm_g, w_kv_up,
                 w_o_sb, w_o_mla, w_out, norm_f_g):
    f = lambda a: np.ascontiguousarray(np.asarray(a, dtype=np.float32))
    B = x.shape[0]
    shared = dict(
        w_in=f(w_in[0]),
        gin=f(np.asarray(norm_in_g[0]).reshape(8, 128).T),
        bg=f(np.asarray(b_gate[0]).reshape(16, 128).T),
        gq=f(np.asarray(q_norm_g[0]).reshape(3, 128).T),
        wq=f(w_q_up[0]),
        gkv=f(np.asarray(kv_norm_g[0]).reshape(2, 128).T),
        wkv=f(w_kv_up[0]),
        wosb=f(w_o_sb[0]), womla=f(w_o_mla[0]), wout=f(w_out[0]),
        gf=f(np.broadcast_to(np.asarray(norm_f_g)[None, :], (128, 1024))),
    )
    consts = [host_consts(S, c) for c in range(4)]
    maps = []
    for b in range(B):
        xb = f(x[b])
        for c in range(4):
            m = dict(shared)
            m.update(consts[c])
            m["xf"] = xb
            m["xo"] = f(xb[c::4])
            maps.append(m)
    return maps


_CACHE = {}


def get_program(S, debug=False):
    key = (S, debug)
    if key not in _CACHE:
        P0 = Prog(None)
        build(P0, S, debug)
        nc = bass.Bass("TRN2", target_bir_lowering=False)
        P1 = Prog(nc, needed=P0.used)
        build(P1, S, debug)
        _CACHE[key] = nc
    return _CACHE[key]


def run(S, inputs, debug=False):
    x = np.asarray(inputs["x"])
    B = x.shape[0]
    maps = make_in_maps(S, **{k: np.asarray(v) for k, v in inputs.items()})
    nc = get_program(S, debug)
    ncores = 4 * B
    res = run_bass_kernel_spmd(nc, maps, core_ids=list(range(ncores)))
    out = np.empty((B, S, D), np.float32)
    for b in range(B):
        for c in range(4):
            out[b, c::4, :] = res.results[b * 4 + c]["out"]
    return out, res


def kernel(x, norm_in_g, w_in, b_gate, q_norm_g, w_q_up, kv_norm_g, w_kv_up,
           w_o_sb, w_o_mla, w_out, norm_f_g):
    inputs = dict(x=x, norm_in_g=norm_in_g, w_in=w_in, b_gate=b_gate, q_norm_g=q_norm_g, w_q_up=w_q_up,
                  kv_norm_g=kv_norm_g, w_kv_up=w_kv_up, w_o_sb=w_o_sb, w_o_mla=w_o_mla, w_out=w_out,
                  norm_f_g=norm_f_g)
    S = np.asarray(x).shape[1]
    out, _ = run(S, inputs)
    return out
```

```python
import numpy as np
from contextlib import ExitStack
import concourse.bass as bass
import concourse.mybir as mybir
from concourse.bass_utils import run_bass_kernel_spmd

F32 = mybir.dt.float32
BF16 = mybir.dt.bfloat16
AF = mybir.ActivationFunctionType
ALU = mybir.AluOpType

D = 1024
EPS = 1e-6
NEG = -30000.0
C_QSB, C_KSB, C_VSB, C_GSB, C_CQ, C_CKV, C_KR, C_GM, C_GL = 0, 512, 1024, 1536, 2048, 2432, 2688, 2720, 3232

ENGS = ["pe", "act", "dve", "pool", "sp"]
NDS = 16


class Dummy:
    def __getitem__(self, k):
        return self

    def __getattr__(self, k):
        return self

    def __call__(self, *a, **k):
        return self


DUMMY = Dummy()


class Tok:
    __slots__ = ("eng", "idx", "dma", "clock", "dclock")

    def __init__(self, eng, idx, dma, clock, dclock):
        self.eng = eng
        self.idx = idx
        self.dma = dma
        self.clock = clock
        self.dclock = dclock


class Res:
    __slots__ = ("w", "rs", "excl")

    def __init__(self, excl=False):
        self.w = None
        self.rs = []
        self.excl = excl


class Prog:
    def __init__(self, nc, needed=None):
        self.nc = nc
        self.dry = nc is None
        self.needed = needed
        self.count = {e: 0 for e in ENGS}
        self.sig = {e: 0 for e in ENGS}
        self.sigval = {}
        self.ndma = 0
        self.ndk = {"h": 0, "s": 0}
        self.dtoks = {"h": [], "s": []}
        self.known = {e: {f: 0 for f in ENGS} for e in ENGS}
        self.dknown = {e: {} for e in ENGS}
        self.used = set()
        self.res = {}
        self.nwaits = 0
        if not self.dry:
            self.engobj = {"pe": nc.tensor, "act": nc.scalar, "dve": nc.vector,
                           "pool": nc.gpsimd, "sp": nc.sync}
            self.sems = {}

    def alloc_sems(self, stack):
        if self.dry:
            return
        for e in ENGS:
            self.sems[e] = stack.enter_context(self.nc.semaphore("s_" + e))
        self.dsem = {"h": [], "s": []}
        for i in range(NDS):
            self.dsem["h"].append(stack.enter_context(self.nc.semaphore("d_%d" % i)))
            self.dsem["s"].append(stack.enter_context(self.nc.semaphore("q_%d" % i)))

    def R(self, name):
        r = self.res.get(name)
        if r is None:
            excl = isinstance(name, str) and name.startswith("ps")
            r = Res(excl)
            self.res[name] = r
        return r

    def _knows(self, e, t):
        if t.dma:
            return self.dknown[e].get((t.dma, t.idx % NDS), -1) >= t.idx
        return self.known[e][t.eng] >= t.idx + 1

    def _merge(self, e, t):
        k = self.known[e]
        for f, v in t.clock.items():
            if v > k[f]:
                k[f] = v
        dk = self.dknown[e]
        for s, v in t.dclock.items():
            if v > dk.get(s, -1):
                dk[s] = v
        if t.dma:
            s = (t.dma, t.idx % NDS)
            if t.idx > dk.get(s, -1):
                dk[s] = t.idx
        else:
            if t.idx + 1 > k[t.eng]:
                k[t.eng] = t.idx + 1

    def _wait(self, e, t):
        if self._knows(e, t):
            return
        self.nwaits += 1
        self.used.add(("dma" + t.dma, t.idx) if t.dma else (t.eng, t.idx))
        if not self.dry:
            if t.dma:
                sem = self.dsem[t.dma][t.idx % NDS]
                val = 16 * (t.idx // NDS + 1)
            else:
                sem = self.sems[t.eng]
                val = self.sigval[(t.eng, t.idx)]
            self.engobj[e].wait_ge(sem, val)
        self._merge(e, t)

    def op(self, eng, fn, reads=(), writes=(), dma=False):
        deps = []
        rl = [self.R(n) for n in reads]
        wl = [self.R(n) for n in writes]
        for r in rl:
            if r.w is not None:
                deps.append((r.w, "raw"))
            if r.excl:
                for t in r.rs:
                    deps.append((t, "rr"))
        for r in wl:
            if r.w is not None:
                deps.append((r.w, "waw"))
            for t in r.rs:
                deps.append((t, "war"))
        best = {}
        for t, kind in deps:
            if (not t.dma) and (not dma) and t.eng == eng and (eng == "pe" or kind == "rr"):
                continue
            key = ("dma" + t.dma, t.idx % NDS) if t.dma else ("eng", t.eng)
            b = best.get(key)
            if b is None or t.idx > b.idx:
                best[key] = t
        for key in sorted(best.keys()):
            self._wait(eng, best[key])
        if dma:
            kind = "s" if eng == "pool" else "h"
            idx = self.ndk[kind]
            if idx >= NDS:
                self._wait(eng, self.dtoks[kind][idx - NDS])
            self.ndk[kind] += 1
            self.ndma += 1
            tok = Tok(eng, idx, kind, dict(self.known[eng]), dict(self.dknown[eng]))
            self.dtoks[kind].append(tok)
            if not self.dry:
                ins = fn(self.engobj[eng])
                ins.then_inc(self.dsem[kind][idx % NDS], 16)
        else:
            idx = self.count[eng]
            self.count[eng] += 1
            tok = Tok(eng, idx, None, dict(self.known[eng]), dict(self.dknown[eng]))
            if not self.dry:
                ins = fn(self.engobj[eng])
                if (eng, idx) in self.needed:
                    self.sig[eng] += 1
                    self.sigval[(eng, idx)] = self.sig[eng]
                    ins.then_inc(self.sems[eng], 1)
        for r in rl:
            r.rs.append(tok)
        for r in wl:
            r.w = tok
            r.rs = []
        return tok

    def wait_all(self, eng):
        toks = []
        for r in self.res.values():
            if r.w is not None:
                toks.append(r.w)
            toks.extend(r.rs)
        best = {}
        for t in toks:
            if (not t.dma) and t.eng == eng:
                continue
            key = ("dma" + t.dma, t.idx % NDS) if t.dma else ("eng", t.eng)
            b = best.get(key)
            if b is None or t.idx > b.idx:
                best[key] = t
        for key in sorted(best.keys()):
            self._wait(eng, best[key])

    def barrier(self):
        for e in ENGS:
            self.wait_all(e)


def build(P, S, debug=False):
    nc = P.nc
    dry = P.dry
    NT = S // 512
    NB = S // 128
    SO = S // 4
    NQ = SO // 512
    NBO = SO // 128

    def dram(name, shape, dt, kind):
        if dry:
            return DUMMY
        return nc.dram_tensor(name, list(shape), dt, kind=kind).ap()

    xf = dram("xf", [S, D], F32, "ExternalInput")
    xo = dram("xo", [SO, D], F32, "ExternalInput")
    w_in = dram("w_in", [D, 5280], F32, "ExternalInput")
    gin_d = dram("gin", [128, 8], F32, "ExternalInput")
    bg_d = dram("bg", [128, 16], F32, "ExternalInput")
    gq_d = dram("gq", [128, 3], F32, "ExternalInput")
    wq_d = dram("wq", [384, 768], F32, "ExternalInput")
    gkv_d = dram("gkv", [128, 2], F32, "ExternalInput")
    wkv_d = dram("wkv", [256, 1024], F32, "ExternalInput")
    wosb_d = dram("wosb", [512, 1024], F32, "ExternalInput")
    womla_d = dram("womla", [512, 1024], F32, "ExternalInput")
    wout_d = dram("wout", [1024, 1024], F32, "ExternalInput")
    gf_d = dram("gf", [128, 1024], F32, "ExternalInput")
    cosf_d = dram("cosf", [128, NB, 16], F32, "ExternalInput")
    sinf_d = dram("sinf", [128, NB, 16], F32, "ExternalInput")
    coso_d = dram("coso", [128, NBO, 4, 16], F32, "ExternalInput")
    sino_d = dram("sino", [128, NBO, 4, 16], F32, "ExternalInput")
    cst_d = dram("cst", [128, 5, 128], F32, "ExternalInput")
    msb_d = dram("msb", [128, 4, 128], F32, "ExternalInput")
    mmla_d = dram("mmla", [128, 4, 128], F32, "ExternalInput")
    esel_d = dram("esel", [128, 128], F32, "ExternalInput")
    out_d = dram("out", [SO, D], F32, "ExternalOutput")
    skind = "ExternalOutput" if debug else "Internal"
    kTsb_d = dram("kTsb", [4, 128, S], BF16, skind)
    vsb_d = dram("vsb", [S, 512], BF16, skind)
    kfT_d = dram("kfT", [8, 96, S], BF16, skind)
    vmla_d = dram("vmla", [S, 512], BF16, skind)
    hoT_d = dram("hoT", [128, 8, SO], BF16, skind)

    root = ExitStack()
    P.alloc_sems(root)

    ARENA_BYTES = 206 * 1024
    arena = None if dry else root.enter_context(nc.sbuf_tensor("arena", [128, ARENA_BYTES // 2], BF16))
    freelist = [[0, ARENA_BYTES]]
    peak = [0]

    def a_free(blk):
        freelist.append(list(blk))
        freelist.sort()
        i = 0
        while i + 1 < len(freelist):
            if freelist[i][0] + freelist[i][1] == freelist[i + 1][0]:
                freelist[i][1] += freelist[i + 1][1]
                del freelist[i + 1]
            else:
                i += 1

    def sb(stack, name, shape, dt):
        isz = 4 if dt == F32 else 2
        n = 1
        for d in shape[1:]:
            n *= d
        nb = (n * isz + 63) // 64 * 64
        for fb in freelist:
            if fb[1] >= nb:
                off = fb[0]
                fb[0] += nb
                fb[1] -= nb
                break
        else:
            raise RuntimeError("SBUF arena full allocating %s (%d B); free=%s" % (name, nb, freelist))
        if fb[1] == 0:
            freelist.remove(fb)
        stack.callback(a_free, (off, nb))
        used = ARENA_BYTES - sum(f[1] for f in freelist)
        peak[0] = max(peak[0], used)
        if dry:
            return DUMMY
        v = arena[0:shape[0], off // 2: off // 2 + n * isz // 2]
        if dt == F32:
            v = v.bitcast(F32)
        if len(shape) == 2:
            return v
        names = " ".join("d%d" % i for i in range(1, len(shape)))
        kw = {"d%d" % i: shape[i] for i in range(1, len(shape) - 1)}
        return v.rearrange("p (%s) -> p %s" % (names, names), **kw)

    pf = []
    pbv = []
    for i in range(8):
        if dry:
            pf.append(DUMMY)
            pbv.append(DUMMY)
        else:
            t = root.enter_context(nc.psum_tensor("pb%d" % i, [128, 512], F32))
            pf.append(t)
            pbv.append(t.bitcast(BF16))
    PS = ["ps%d" % i for i in range(8)]

    cstb = sb(root, "cstb", [128, 5, 128], BF16)
    mskb = sb(root, "mskb", [128, 8, 128], BF16)
    eselb = sb(root, "eselb", [128, 128], BF16)
    scst = ExitStack()
    cstf = sb(scst, "cstf", [128, 5, 128], F32)
    mskf = sb(scst, "mskf", [128, 8, 128], F32)
    eself = sb(scst, "eself", [128, 128], F32)
    gin = sb(root, "gin_s", [128, 8], F32)
    bg = sb(root, "bg_s", [128, 16], F32)
    gq = sb(root, "gq_s", [128, 3], F32)
    gkv = sb(root, "gkv_s", [128, 2], F32)
    junk = sb(root, "junk", [128, 1024], BF16)

    P.op("sp", lambda e: e.dma_start(out=cstf[:], in_=cst_d), writes=["cstf"], dma=True)
    P.op("sp", lambda e: e.dma_start(out=mskf[:, 0:4, :], in_=msb_d), writes=["mskf"], dma=True)
    P.op("sp", lambda e: e.dma_start(out=mskf[:, 4:8, :], in_=mmla_d), writes=["mskf2"], dma=True)
    P.op("sp", lambda e: e.dma_start(out=eself[:], in_=esel_d), writes=["eself"], dma=True)
    P.op("sp", lambda e: e.dma_start(out=gin[:], in_=gin_d), writes=["gin"], dma=True)
    P.op("sp", lambda e: e.dma_start(out=bg[:], in_=bg_d), writes=["bg"], dma=True)
    P.op("sp", lambda e: e.dma_start(out=gq[:], in_=gq_d), writes=["gq"], dma=True)
    P.op("sp", lambda e: e.dma_start(out=gkv[:], in_=gkv_d), writes=["gkv"], dma=True)
    P.op("dve", lambda e: e.tensor_copy(out=cstb[:], in_=cstf[:]), reads=["cstf"], writes=["cstb"])
    P.op("dve", lambda e: e.tensor_copy(out=mskb[:], in_=mskf[:]), reads=["mskf", "mskf2"], writes=["mskb"])
    P.op("dve", lambda e: e.tensor_copy(out=eselb[:], in_=eself[:]), reads=["eself"], writes=["eselb"])
    identb = cstb[:, 0, :]
    trib = cstb[:, 1, :]
    monesb = cstb[:, 2, :]
    onesb = cstb[:, 3, :]
    swapb = cstb[:, 4, :]
    P.barrier()
    scst.close()

    wgl_s = dram("wgl_s", [128, 8, 2048], BF16, "Internal")
    wos_s = dram("wos_s", [128, 4, 1024], BF16, "Internal")
    wom_s = dram("wom_s", [128, 4, 1024], BF16, "Internal")
    wout_s = dram("wout_s", [128, 8, 1024], BF16, "Internal")
    spp = ExitStack()
    pst = [sb(spp, "pst%d" % i, [128, 2048], F32) for i in range(2)]
    pob = [sb(spp, "pob%d" % i, [128, 2048], BF16) for i in range(2)]
    pcnt = [0]

    def prep_piece(src2d, nk, c0, c1, g=None, dst_fn=None, dst_name=None, dram_dst=None, dram_name=None):
        slot = []

        def setup():
            i = pcnt[0] % 2
            pcnt[0] += 1
            slot.append(i)
            n = c1 - c0
            st = pst[i][:, 0:nk * n].rearrange("p (k n) -> p k n", k=nk)
            P.op("sp", lambda e: e.dma_start(out=st, in_=src2d[:, c0:c1].rearrange("(k p) n -> p k n", p=128)),
                 writes=["pst%d" % i], dma=True)

        def cast(k):
            i = slot[0]
            n = c1 - c0
            st = pst[i][:, 0:nk * n].rearrange("p (k n) -> p k n", k=nk)
            ob = pob[i][:, 0:nk * n].rearrange("p (k n) -> p k n", k=nk)
            if dram_dst is not None:
                o = ob[:, k, :]
                oname = "pob%d" % i
            else:
                o = dst_fn(k, c0, c1)
                oname = dst_name
            if g is not None:
                P.op("dve", lambda e: e.tensor_scalar(out=o, in0=st[:, k, :], scalar1=g[:, k:k + 1],
                                                      scalar2=None, op0=ALU.mult),
                     reads=["pst%d" % i, "gin", "gq", "gkv"], writes=[oname])
            else:
                P.op("act", lambda e: e.copy(out=o, in_=st[:, k, :]), reads=["pst%d" % i], writes=[oname])

        def store():
            i = slot[0]
            n = c1 - c0
            ob = pob[i][:, 0:nk * n].rearrange("p (k n) -> p k n", k=nk)
            P.op("pool", lambda e: e.dma_start(out=dram_dst[:, :, c0:c1], in_=ob), reads=["pob%d" % i],
                 writes=[dram_name], dma=True)
        tasks = [setup] + [(lambda k=k: cast(k)) for k in range(nk)]
        if dram_dst is not None:
            tasks.append(store)
        return tasks

    def prep_tasks(src2d, nk, ncols, **kw):
        out = []
        for c0 in range(0, ncols, 256):
            out += prep_piece(src2d, nk, c0, min(c0 + 256, ncols), **kw)
        return out

    def make_front(xsrc, xbuf, ss8, inv8, hnb, hTb, tag, tbanks):
        def load(ti):
            for a in range(4):
                n = ti * 4 + a
                xb = xbuf[a]
                P.op("sp", lambda e, xb=xb, n=n: e.dma_start(out=xb[:], in_=xsrc[n * 128:(n + 1) * 128, :]),
                     writes=["x%s%d" % (tag, a)], dma=True)

        def f1(ti, a):
            p = ti % 2
            xb = xbuf[a]
            xn = "x%s%d" % (tag, a)
            c = p * 4 + a
            sn = "ss%s%d" % (tag, c)
            P.op("act", lambda e: e.activation(out=junk[:], in_=xb[:], func=AF.Square, accum_out=ss8[:, c:c + 1]),
                 reads=[xn], writes=["junk", sn])
            P.op("act", lambda e: e.activation(out=inv8[:, c:c + 1], in_=ss8[:, c:c + 1], func=AF.Ln,
                                               scale=1.0 / D, bias=EPS), reads=[sn], writes=[sn + "i"])
            P.op("act", lambda e: e.activation(out=inv8[:, c:c + 1], in_=inv8[:, c:c + 1], func=AF.Exp,
                                               scale=-0.5), reads=[sn + "i"], writes=[sn + "i"])
            P.op("dve", lambda e: e.tensor_scalar(out=hnb[p][:, a, :], in0=xb[:], scalar1=inv8[:, c:c + 1],
                                                  scalar2=None, op0=ALU.mult),
                 reads=[xn, sn + "i"], writes=["hn%s%d_%d" % (tag, p, a)])

        def f2(ti, a):
            p = ti % 2
            hT = hTb[p]
            bk = tbanks[a % len(tbanks)]
            hname = "hn%s%d_%d" % (tag, p, a)
            for k in range(8):
                P.op("pe", lambda e, k=k: e.transpose(out=pbv[bk][:, k * 128:(k + 1) * 128],
                                                      in_=hnb[p][:, a, k * 128:(k + 1) * 128], identity=identb),
                     reads=[hname, "cstb"], writes=[PS[bk]])
            src = pbv[bk][:, :].rearrange("p (k t) -> p k t", k=8)
            dst = hT[:, :, a * 128:(a + 1) * 128]
            if a % 2 == 0:
                P.op("act", lambda e: e.copy(out=dst, in_=src), reads=[PS[bk]], writes=["hT%s%d_%d" % (tag, p, a)])
            else:
                P.op("dve", lambda e: e.tensor_copy(out=dst, in_=src), reads=[PS[bk]], writes=["hT%s%d_%d" % (tag, p, a)])
        return load, f1, f2

    evc = [0]

    def evac(src, dst, reads, writes, scale=None, func=None, bias=None, eng=None):
        if func is not None or scale is not None or bias is not None:
            kw = {}
            if scale is not None:
                kw["scale"] = scale
            if bias is not None:
                kw["bias"] = bias
            f = func if func is not None else AF.Copy
            P.op("act", lambda e: e.activation(out=dst, in_=src, func=f, **kw), reads=reads, writes=writes)
            return
        if eng is None:
            eng = "act" if evc[0] % 2 == 0 else "dve"
            evc[0] += 1
        if eng == "act":
            P.op("act", lambda e: e.copy(out=dst, in_=src), reads=reads, writes=writes)
        else:
            P.op("dve", lambda e: e.tensor_copy(out=dst, in_=src), reads=reads, writes=writes)

    def rms_small(src_fn, ncols, ssq, invs, name):
        for a in range(4):
            P.op("act", lambda e, a=a: e.activation(out=junk[:, 0:ncols], in_=src_fn(a), func=AF.Square,
                                                    accum_out=ssq[:, a:a + 1]),
                 reads=[name], writes=["junk", name + "_ss"])
        P.op("act", lambda e: e.activation(out=invs[:], in_=ssq[:], func=AF.Ln, scale=1.0 / ncols, bias=EPS),
             reads=[name + "_ss"], writes=[name + "_inv"])
        P.op("act", lambda e: e.activation(out=invs[:], in_=invs[:], func=AF.Exp, scale=-0.5),
             reads=[name + "_inv"], writes=[name + "_inv"])

    swb = ExitStack()
    WBq = sb(swb, "WBq", [128, 8, 512], BF16)
    WBg = sb(swb, "WBg", [128, 8, 512], BF16)
    WBc = sb(swb, "WBc", [128, 8, 384], BF16)
    WBm = sb(swb, "WBm", [128, 8, 512], BF16)
    WQ = sb(swb, "WQ", [128, 3, 768], BF16)
    later = []
    later += prep_tasks(w_in[:, C_QSB:C_QSB + 512], 8, 512, g=gin, dst_fn=lambda k, a, b: WBq[:, k, a:b], dst_name="WBq")
    later += prep_tasks(w_in[:, C_GSB:C_GSB + 512], 8, 512, g=gin, dst_fn=lambda k, a, b: WBg[:, k, a:b], dst_name="WBg")
    later += prep_tasks(w_in[:, C_CQ:C_CQ + 384], 8, 384, g=gin, dst_fn=lambda k, a, b: WBc[:, k, a:b], dst_name="WBc")
    later += prep_tasks(w_in[:, C_GM:C_GM + 512], 8, 512, g=gin, dst_fn=lambda k, a, b: WBm[:, k, a:b], dst_name="WBm")
    later += prep_tasks(wq_d, 3, 768, g=gq, dst_fn=lambda k, a, b: WQ[:, k, a:b], dst_name="WQ")
    later += prep_tasks(w_in[:, C_GL:C_GL + 2048], 8, 2048, g=gin, dram_dst=wgl_s, dram_name="wgl_s")
    later += prep_tasks(wosb_d, 4, 1024, dram_dst=wos_s, dram_name="wos_s")
    later += prep_tasks(womla_d, 4, 1024, dram_dst=wom_s, dram_name="wom_s")
    later += prep_tasks(wout_d, 8, 1024, dram_dst=wout_s, dram_name="wout_s")
    per_group = (len(later) + NT * 24 - 1) // (NT * 24)

    def sprinkle():
        for _ in range(per_group):
            if later:
                later.pop(0)()

    with ExitStack() as sa:
        WAk = sb(sa, "WAk", [128, 8, 512], BF16)
        WAv = sb(sa, "WAv", [128, 8, 512], BF16)
        WAc = sb(sa, "WAc", [128, 8, 288], BF16)
        WKVK = sb(sa, "WKVK", [128, 2, 8, 128], BF16)
        WKVV = sb(sa, "WKVV", [128, 2, 512], BF16)
        xbuf = [sb(sa, "xA%d" % i, [128, 1024], F32) for i in range(4)]
        ss8 = sb(sa, "ss8A", [128, 8], F32)
        inv8 = sb(sa, "inv8A", [128, 8], F32)
        hnb = [sb(sa, "hnA%d" % i, [128, 4, 1024], BF16) for i in range(2)]
        hTb = [sb(sa, "hTA%d" % i, [128, 8, 512], BF16) for i in range(2)]
        loadA, f1A, f2A = make_front(xf, xbuf, ss8, inv8, hnb, hTb, "A", [0, 1, 6, 7])
        kst = [sb(sa, "kst%d" % i, [128, 4, 512], BF16) for i in range(2)]
        vst = [sb(sa, "vst%d" % i, [128, 4, 512], BF16) for i in range(2)]
        ckvs = sb(sa, "ckvs", [128, 4, 288], F32)
        ssk = sb(sa, "ssk", [128, 4], F32)
        invk = sb(sa, "invk", [128, 4], F32)
        kvn = sb(sa, "kvn", [128, 4, 256], BF16)
        kr = sb(sa, "kr", [128, 4, 32], BF16)
        rt1 = sb(sa, "rt1", [128, 4, 16], F32)
        rt2 = sb(sa, "rt2", [128, 4, 16], F32)
        rt3 = sb(sa, "rt3", [128, 4, 16], F32)
        rt4 = sb(sa, "rt4", [128, 4, 16], F32)
        kvnT = [sb(sa, "kvnT%d" % i, [128, 2, 512], BF16) for i in range(2)]
        krT = [sb(sa, "krT%d" % i, [128, 512], BF16) for i in range(2)]
        for i in range(2):
            P.op("pool", lambda e, i=i: e.memset(krT[i][32:64, :], 0.0), writes=["krT%d" % i])
            P.op("pool", lambda e, i=i: e.memset(krT[i][64:128, :], 0.0), writes=["krT%d" % i])
        kfst = [sb(sa, "kfst%d" % i, [96, 8, 512], BF16) for i in range(2)]
        vmst = [sb(sa, "vmst%d" % i, [128, 4, 512], BF16) for i in range(2)]
        cosf = sb(sa, "cosf", [128, NB, 16], F32)
        sinf = sb(sa, "sinf", [128, NB, 16], F32)
        P.op("sp", lambda e: e.dma_start(out=cosf[:], in_=cosf_d), writes=["cosf"], dma=True)
        P.op("sp", lambda e: e.dma_start(out=sinf[:], in_=sinf_d), writes=["sinf"], dma=True)

        xc = [0]
        pj = [0]

        def pbank():
            b = 2 + (pj[0] % 4)
            pj[0] += 1
            return b

        loadA(0)
        for a in range(4):
            f1A(0, a)
        for t in prep_tasks(w_in[:, C_KSB:C_KSB + 512], 8, 512, g=gin, dst_fn=lambda k, a, b: WAk[:, k, a:b], dst_name="WAk"):
            t()
        for t in prep_tasks(w_in[:, C_VSB:C_VSB + 512], 8, 512, g=gin, dst_fn=lambda k, a, b: WAv[:, k, a:b], dst_name="WAv"):
            t()
        for t in prep_tasks(w_in[:, C_CKV:C_CKV + 288], 8, 288, g=gin, dst_fn=lambda k, a, b: WAc[:, k, a:b], dst_name="WAc"):
            t()
        i = pcnt[0] % 2
        pcnt[0] += 1
        stv = pst[i]
        P.op("sp", lambda e: e.dma_start(out=stv[:, 0:2048].rearrange("p (k n) -> p k n", k=2),
                                         in_=wkv_d.rearrange("(k p) n -> p k n", p=128)),
             writes=["pst%d" % i], dma=True)
        P.op("dve", lambda e: e.memset(WKVK[:], 0.0), writes=["WKVK"])
        for rc in range(2):
            src = stv[:, rc * 1024:(rc + 1) * 1024].rearrange("p (h c) -> p h c", h=8)
            P.op("dve", lambda e, rc=rc, src=src: e.tensor_scalar(out=WKVK[:, rc, :, 0:64], in0=src[:, :, 0:64],
                                                                 scalar1=gkv[:, rc:rc + 1], scalar2=None,
                                                                 op0=ALU.mult),
                 reads=["pst%d" % i, "gkv"], writes=["WKVK"])
            P.op("dve", lambda e, rc=rc, src=src: e.tensor_scalar(
                out=WKVV[:, rc, :].rearrange("p (h c) -> p h c", h=8), in0=src[:, :, 64:128],
                scalar1=gkv[:, rc:rc + 1], scalar2=None, op0=ALU.mult),
                 reads=["pst%d" % i, "gkv"], writes=["WKVV"])
        for a in range(4):
            f2A(0, a)
        for ti in range(NT):
            bi = ti % 2
            hT = hTb[bi]
            hTn = "hTA%d" % bi
            if ti + 1 < NT:
                loadA(ti + 1)
            for a in range(4):
                b = pbank()
                for k in range(8):
                    P.op("pe", lambda e, b=b, k=k, a=a: e.matmul(pf[b][:, 0:288], lhsT=hT[:, k, a * 128:(a + 1) * 128],
                                                                 rhs=WAc[:, k, :], start=(k == 0), stop=(k == 7)),
                         reads=["WAc", "%s_%d" % (hTn, a)], writes=[PS[b]])
                evac(pf[b][:, 0:288], ckvs[:, a, :], [PS[b]], ["ckvs"])
                sprinkle()
            rms_small(lambda a: ckvs[:, a, 0:256], 256, ssk, invk, "ckvs")
            for a in range(4):
                P.op("dve", lambda e, a=a: e.tensor_scalar(out=kvn[:, a, :], in0=ckvs[:, a, 0:256],
                                                          scalar1=invk[:, a:a + 1], scalar2=None, op0=ALU.mult),
                     reads=["ckvs", "ckvs_inv"], writes=["kvn"])
            cs = cosf[:, ti * 4:(ti + 1) * 4, :]
            sn = sinf[:, ti * 4:(ti + 1) * 4, :]
            x1 = ckvs[:, :, 256:272]
            x2 = ckvs[:, :, 272:288]
            P.op("dve", lambda e: e.tensor_tensor(out=rt1[:], in0=x1, in1=cs, op=ALU.mult), reads=["ckvs", "cosf"], writes=["rt1"])
            P.op("dve", lambda e: e.tensor_tensor(out=rt2[:], in0=x2, in1=sn, op=ALU.mult), reads=["ckvs", "sinf"], writes=["rt2"])
            P.op("dve", lambda e: e.tensor_tensor(out=kr[:, :, 0:16], in0=rt1[:], in1=rt2[:], op=ALU.subtract),
                 reads=["rt1", "rt2"], writes=["kr1"])
            P.op("dve", lambda e: e.tensor_tensor(out=rt3[:], in0=x1, in1=sn, op=ALU.mult), reads=["ckvs", "sinf"], writes=["rt3"])
            P.op("dve", lambda e: e.tensor_tensor(out=rt4[:], in0=x2, in1=cs, op=ALU.mult), reads=["ckvs", "cosf"], writes=["rt4"])
            P.op("dve", lambda e: e.tensor_tensor(out=kr[:, :, 16:32], in0=rt3[:], in1=rt4[:], op=ALU.add),
                 reads=["rt3", "rt4"], writes=["kr2"])
            for hp in range(4):
                b = pbank()
                for k in range(8):
                    P.op("pe", lambda e, b=b, k=k, hp=hp: e.matmul(pf[b][:, :], lhsT=WAk[:, k, hp * 128:(hp + 1) * 128],
                                                                  rhs=hT[:, k, :], start=(k == 0), stop=(k == 7)),
                         reads=["WAk"] + ["%s_%d" % (hTn, q) for q in range(4)], writes=[PS[b]])
                evac(pf[b][:, :], kst[bi][:, hp, :], [PS[b]], ["kst%d" % bi])
                sprinkle()
                if ti + 1 < NT:
                    f1A(ti + 1, hp)
            P.op("pool", lambda e, ti=ti, bi=bi: e.dma_start(
                out=kTsb_d[:, :, ti * 512:(ti + 1) * 512].rearrange("h p s -> p h s"), in_=kst[bi][:]),
                 reads=["kst%d" % bi], writes=["kTsb_d"], dma=True)
            for a in range(4):
                for rc in range(2):
                    P.op("pe", lambda e, a=a, rc=rc: e.transpose(
                        out=pbv[6][:, rc * 512 + a * 128: rc * 512 + (a + 1) * 128],
                        in_=kvn[:, a, rc * 128:(rc + 1) * 128], identity=identb),
                         reads=["kvn", "cstb"], writes=[PS[6]])
                P.op("pe", lambda e, a=a: e.transpose(out=pbv[7][0:32, a * 128:(a + 1) * 128], in_=kr[:, a, :],
                                                      identity=identb),
                     reads=["kr1", "kr2", "cstb"], writes=[PS[7]])
            evac(pbv[6][:, :].rearrange("p (k t) -> p k t", k=2), kvnT[bi][:], [PS[6]], ["kvnT%d" % bi])
            evac(pbv[7][0:32, 0:512], krT[bi][0:32, :], [PS[7]], ["krT%d" % bi])
            for a in range(4):
                b = pbank()
                for k in range(8):
                    P.op("pe", lambda e, b=b, k=k, a=a: e.matmul(pf[b][:, :], lhsT=hT[:, k, a * 128:(a + 1) * 128],
                                                                 rhs=WAv[:, k, :], start=(k == 0), stop=(k == 7)),
                         reads=["WAv", "%s_%d" % (hTn, a)], writes=[PS[b]])
                evac(pf[b][:, :], vst[bi][:, a, :], [PS[b]], ["vst%d" % bi])
                sprinkle()
            P.op("pool", lambda e, ti=ti, bi=bi: e.dma_start(
                out=vsb_d[ti * 512:(ti + 1) * 512, :].rearrange("(a p) c -> p a c", p=128), in_=vst[bi][:]),
                 reads=["vst%d" % bi], writes=["vsb_d"], dma=True)
            for h in range(8):
                b = pbank()
                P.op("pe", lambda e, b=b, h=h: e.matmul(pf[b][:, :], lhsT=WKVK[:, 0, h, :], rhs=kvnT[bi][:, 0, :],
                                                        start=True, stop=False),
                     reads=["WKVK", "kvnT%d" % bi], writes=[PS[b]])
                P.op("pe", lambda e, b=b, h=h: e.matmul(pf[b][:, :], lhsT=WKVK[:, 1, h, :], rhs=kvnT[bi][:, 1, :],
                                                        start=False, stop=False),
                     reads=["WKVK", "kvnT%d" % bi], writes=[PS[b]])
                P.op("pe", lambda e, b=b: e.matmul(pf[b][:, :], lhsT=eselb[:, :], rhs=krT[bi][:, :],
                                                   start=False, stop=True),
                     reads=["eselb", "krT%d" % bi], writes=[PS[b]])
                evac(pf[b][0:96, :], kfst[bi][:, h, :], [PS[b]], ["kfst%d" % bi])
                sprinkle()
            P.op("pool", lambda e, ti=ti, bi=bi: e.dma_start(
                out=kfT_d[:, :, ti * 512:(ti + 1) * 512].rearrange("h p s -> p h s"), in_=kfst[bi][:]),
                 reads=["kfst%d" % bi], writes=["kfT_d"], dma=True)
            for a in range(4):
                b = pbank()
                for rc in range(2):
                    P.op("pe", lambda e, b=b, rc=rc, a=a: e.matmul(pf[b][:, :], lhsT=kvnT[bi][:, rc, a * 128:(a + 1) * 128],
                                                                   rhs=WKVV[:, rc, :], start=(rc == 0), stop=(rc == 1)),
                         reads=["WKVV", "kvnT%d" % bi], writes=[PS[b]])
                evac(pf[b][:, :], vmst[bi][:, a, :], [PS[b]], ["vmst%d" % bi])
                sprinkle()
            P.op("pool", lambda e, ti=ti, bi=bi: e.dma_start(
                out=vmla_d[ti * 512:(ti + 1) * 512, :].rearrange("(a p) c -> p a c", p=128), in_=vmst[bi][:]),
                 reads=["vmst%d" % bi], writes=["vmla_d"], dma=True)
            if ti + 1 < NT:
                for a in range(4):
                    f2A(ti + 1, a)
        while later:
            later.pop(0)()
        P.barrier()
    spp.close()

    SGsb = sb(root, "SGsb", [128, 4, SO], BF16)
    SGmla = sb(root, "SGmla", [128, 4, SO], BF16)
    sq = ExitStack()
    QTsb = sb(sq, "QTsb", [128, 4, SO], BF16)
    QFT = sb(sq, "QFT", [128, 8, SO], BF16)
    P.op("pool", lambda e: e.memset(QFT[64:128, :, :], 0.0), writes=["QFT"])

    with ExitStack() as sbk:
        xbuf = [sb(sbk, "xB%d" % i, [128, 1024], F32) for i in range(4)]
        ss8 = sb(sbk, "ss8B", [128, 8], F32)
        inv8 = sb(sbk, "inv8B", [128, 8], F32)
        hnb = [sb(sbk, "hnB%d" % i, [128, 4, 1024], BF16) for i in range(2)]
        hTb = [sb(sbk, "hTB%d" % i, [128, 8, 512], BF16) for i in range(2)]
        loadB, f1B, f2B = make_front(xo, xbuf, ss8, inv8, hnb, hTb, "B", [0, 1, 6, 7])
        cqs = sb(sbk, "cqs", [128, 4, 384], F32)
        ssq_ = sb(sbk, "ssq", [128, 4], F32)
        invq = sb(sbk, "invq", [128, 4], F32)
        cqn = sb(sbk, "cqn", [128, 4, 384], BF16)
        cqnT = sb(sbk, "cqnT", [128, 3, 512], BF16)
        qf = [sb(sbk, "qf%d" % i, [128, 8, 96], BF16) for i in range(2)]
        qt1 = sb(sbk, "qt1", [128, 4, 16], F32)
        qt2 = sb(sbk, "qt2", [128, 4, 16], F32)
        qt3 = sb(sbk, "qt3", [128, 4, 16], F32)
        qt4 = sb(sbk, "qt4", [128, 4, 16], F32)
        coso = sb(sbk, "coso", [128, NBO, 4, 16], F32)
        sino = sb(sbk, "sino", [128, NBO, 4, 16], F32)
        P.op("sp", lambda e: e.dma_start(out=coso[:], in_=coso_d), writes=["coso"], dma=True)
        P.op("sp", lambda e: e.dma_start(out=sino[:], in_=sino_d), writes=["sino"], dma=True)

        xc = [0]
        pj = [0]

        def pbank():
            b = 2 + (pj[0] % 4)
            pj[0] += 1
            return b

        loadB(0)
        for a in range(4):
            f1B(0, a)
        for a in range(4):
            f2B(0, a)
        for tq in range(NQ):
            bi = tq % 2
            hT = hTb[bi]
            hTn = "hTB%d" % bi
            cols = slice(tq * 512, (tq + 1) * 512)
            if tq + 1 < NQ:
                loadB(tq + 1)
            P.op("pool", lambda e, tq=tq, hT=hT: e.dma_start(out=hoT_d[:, :, tq * 512:(tq + 1) * 512], in_=hT[:]),
                 reads=["%s_%d" % (hTn, q) for q in range(4)], writes=["hoT_d"], dma=True)
            for a in range(4):
                b = pbank()
                for k in range(8):
                    P.op("pe", lambda e, b=b, k=k, a=a: e.matmul(pf[b][:, 0:384], lhsT=hT[:, k, a * 128:(a + 1) * 128],
                                                                 rhs=WBc[:, k, :], start=(k == 0), stop=(k == 7)),
                         reads=["WBc", "%s_%d" % (hTn, a)], writes=[PS[b]])
                evac(pf[b][:, 0:384], cqs[:, a, :], [PS[b]], ["cqs"])
            rms_small(lambda a: cqs[:, a, :], 384, ssq_, invq, "cqs")
            for a in range(4):
                P.op("dve", lambda e, a=a: e.tensor_scalar(out=cqn[:, a, :], in0=cqs[:, a, :], scalar1=invq[:, a:a + 1],
                                                          scalar2=None, op0=ALU.mult),
                     reads=["cqs", "cqs_inv"], writes=["cqn"])
            for hp in range(4):
                b = pbank()
                for k in range(8):
                    P.op("pe", lambda e, b=b, k=k, hp=hp: e.matmul(pf[b][:, :], lhsT=WBq[:, k, hp * 128:(hp + 1) * 128],
                                                                  rhs=hT[:, k, :], start=(k == 0), stop=(k == 7)),
                         reads=["WBq"] + ["%s_%d" % (hTn, q) for q in range(4)], writes=[PS[b]])
                P.op("dve", lambda e, b=b, hp=hp, cols=cols: e.tensor_scalar(out=QTsb[:, hp, cols], in0=pf[b][:, :],
                                                                            scalar1=0.125, scalar2=None, op0=ALU.mult),
                     reads=[PS[b]], writes=["QTsb"])
                if tq + 1 < NQ:
                    f1B(tq + 1, hp)
            for (W, Wn, dstT, dn) in ((WBg, "WBg", SGsb, "SGsb"), (WBm, "WBm", SGmla, "SGmla")):
                for hp in range(4):
                    b = pbank()
                    for k in range(8):
                        P.op("pe", lambda e, b=b, k=k, hp=hp, W=W: e.matmul(pf[b][:, :], lhsT=W[:, k, hp * 128:(hp + 1) * 128],
                                                                           rhs=hT[:, k, :], start=(k == 0), stop=(k == 7)),
                             reads=[Wn] + ["%s_%d" % (hTn, q) for q in range(4)], writes=[PS[b]])
                    P.op("act", lambda e, b=b, hp=hp, dstT=dstT, cols=cols: e.activation(out=dstT[:, hp, cols], in_=pf[b][:, :],
                                                                                        func=AF.Silu),
                         reads=[PS[b]], writes=[dn])
            for a in range(4):
                for rc in range(3):
                    bk = 6 if rc < 2 else 7
                    off = (rc % 2) * 512 + a * 128
                    P.op("pe", lambda e, a=a, rc=rc, bk=bk, off=off: e.transpose(
                        out=pbv[bk][:, off:off + 128], in_=cqn[:, a, rc * 128:(rc + 1) * 128], identity=identb),
                         reads=["cqn", "cstb"], writes=[PS[bk]])
            evac(pbv[6][:, :].rearrange("p (k t) -> p k t", k=2), cqnT[:, 0:2, :], [PS[6]], ["cqnT"])
            evac(pbv[7][:, 0:512], cqnT[:, 2, :], [PS[7]], ["cqnT"])
            def qX(a, tq=tq):
                n = tq * 4 + a
                qfb = qf[a % 2]
                qfn = "qf%d" % (a % 2)
                for half in range(2):
                    b = pbank()
                    for rc in range(3):
                        P.op("pe", lambda e, b=b, rc=rc, a=a, half=half: e.matmul(
                            pf[b][:, 0:384], lhsT=cqnT[:, rc, a * 128:(a + 1) * 128],
                            rhs=WQ[:, rc, half * 384:(half + 1) * 384], start=(rc == 0), stop=(rc == 2)),
                             reads=["WQ", "cqnT"], writes=[PS[b]])
                    pv = pf[b][:, 0:384].rearrange("p (h c) -> p h c", h=4)
                    dst = qfb[:, half * 4:(half + 1) * 4, :]
                    x1 = pv[:, :, 64:80]
                    x2 = pv[:, :, 80:96]
                    cs = coso[:, n, :, :]
                    sn = sino[:, n, :, :]
                    P.op("act", lambda e, pv=pv, dst=dst: e.copy(out=dst[:, :, 0:64], in_=pv[:, :, 0:64]),
                         reads=[PS[b]], writes=[qfn])
                    P.op("dve", lambda e, x1=x1, cs=cs: e.tensor_tensor(out=qt1[:], in0=x1, in1=cs, op=ALU.mult),
                         reads=[PS[b], "coso"], writes=["qt1"])
                    P.op("dve", lambda e, x2=x2, sn=sn: e.tensor_tensor(out=qt2[:], in0=x2, in1=sn, op=ALU.mult),
                         reads=[PS[b], "sino"], writes=["qt2"])
                    P.op("dve", lambda e, dst=dst: e.tensor_tensor(out=dst[:, :, 64:80], in0=qt1[:], in1=qt2[:], op=ALU.subtract),
                         reads=["qt1", "qt2"], writes=[qfn])
                    P.op("dve", lambda e, x1=x1, sn=sn: e.tensor_tensor(out=qt3[:], in0=x1, in1=sn, op=ALU.mult),
                         reads=[PS[b], "sino"], writes=["qt3"])
                    P.op("dve", lambda e, x2=x2, cs=cs: e.tensor_tensor(out=qt4[:], in0=x2, in1=cs, op=ALU.mult),
                         reads=[PS[b], "coso"], writes=["qt4"])
                    P.op("dve", lambda e, dst=dst: e.tensor_tensor(out=dst[:, :, 80:96], in0=qt3[:], in1=qt4[:], op=ALU.add),
                         reads=["qt3", "qt4"], writes=[qfn])

            def qY(a, tq=tq):
                qfb = qf[a % 2]
                qfn = "qf%d" % (a % 2)
                bk = a % 2
                for h in range(8):
                    P.op("pe", lambda e, h=h, bk=bk, qfb=qfb: e.transpose(out=pbv[bk][0:96, h * 128:(h + 1) * 128],
                                                                         in_=qfb[:, h, :], identity=identb),
                         reads=[qfn, "cstb"], writes=[PS[bk]])
                c0 = tq * 512 + a * 128
                evac(pbv[bk][0:96, :].rearrange("p (h t) -> p h t", h=8), QFT[0:96, :, c0:c0 + 128], [PS[bk]], ["QFT"])

            qX(0)
            for a in range(4):
                if a + 1 < 4:
                    qX(a + 1)
                qY(a)
            if tq + 1 < NQ:
                for a in range(4):
                    f2B(tq + 1, a)
        P.barrier()
    swb.close()

    if debug:
        dq1 = dram("dbg_qtsb", [128, 4, SO], BF16, "ExternalOutput")
        dq2 = dram("dbg_qft", [96, 8, SO], BF16, "ExternalOutput")
        dq3 = dram("dbg_sgsb", [128, 4, SO], BF16, "ExternalOutput")
        dq4 = dram("dbg_sgmla", [128, 4, SO], BF16, "ExternalOutput")
        P.op("sp", lambda e: e.dma_start(out=dq1, in_=QTsb[:]), reads=["QTsb"], writes=["dq1"], dma=True)
        P.op("sp", lambda e: e.dma_start(out=dq2, in_=QFT[0:96]), reads=["QFT"], writes=["dq2"], dma=True)
        P.op("sp", lambda e: e.dma_start(out=dq3, in_=SGsb[:]), reads=["SGsb"], writes=["dq3"], dma=True)
        P.op("sp", lambda e: e.dma_start(out=dq4, in_=SGmla[:]), reads=["SGmla"], writes=["dq4"], dma=True)
        P.barrier()

    def blocks(I):
        out = []
        for kb in range(16 * I + 15, -1, -1):
            m = kb - 16 * I
            if m >= 0:
                out.append((kb, 128 * (m // 4) + 32 * (m % 4), m % 4))
            else:
                out.append((kb, 0, None))
        return out

    with ExitStack() as sc:
        NCH = 4
        CB = NB // NCH
        KT = sb(sc, "KT", [128, S], BF16)
        VV = sb(sc, "VV", [128, NB, 128], BF16)
        KF = sb(sc, "KF", [128, S], BF16)
        VMx = [sb(sc, "VMe", [128, NB, 128], BF16), sb(sc, "VMo", [128, NB, 128], BF16)]
        rhi = sb(sc, "rhi", [128, 512], BF16)
        rlo = sb(sc, "rlo", [128, 512], BF16)
        ocp = sb(sc, "ocp", [128, 512], F32)
        eb = [sb(sc, "eb%d" % i, [128, 512], F32) for i in range(2)]
        spb = [sb(sc, "spb%d" % i, [128, 512], BF16) for i in range(3)]
        wb = [sb(sc, "wb%d" % i, [128, 512], BF16) for i in range(3)]
        ssum = [sb(sc, "ssum%d" % i, [128, 512], BF16) for i in range(3)]
        pbuf = [sb(sc, "pbuf%d" % i, [128, 512], BF16) for i in range(3)]
        rec = sb(sc, "rec", [128, 512], F32)
        otmp = sb(sc, "otmp", [128, 512], F32)
        SCALE = float(96 ** -0.5)
        B_SSB = [0, 1, 2, 3]
        B_SML = [4]
        B_OSB = 5
        B_OML = 6
        B_DML = 7

        def ld_kt(hp, c):
            P.op("sp", lambda e: e.dma_start(out=KT[:, c * CB * 128:(c + 1) * CB * 128],
                                             in_=kTsb_d[hp][:, c * CB * 128:(c + 1) * CB * 128]),
                 reads=["kTsb_d"], writes=["KTc%d" % c], dma=True)

        def ld_vv(hp, c):
            P.op("sp", lambda e: e.dma_start(
                out=VV[:, c * CB:(c + 1) * CB, :],
                in_=vsb_d[c * CB * 128:(c + 1) * CB * 128, hp * 128:(hp + 1) * 128].rearrange("(n p) c -> p n c", p=128)),
                 reads=["vsb_d"], writes=["VVc%d" % c], dma=True)

        def ld_kf(h, c):
            P.op("sp", lambda e: e.dma_start(out=KF[0:96, c * CB * 128:(c + 1) * CB * 128],
                                             in_=kfT_d[h][:, c * CB * 128:(c + 1) * CB * 128]),
                 reads=["kfT_d"], writes=["KFc%d" % c], dma=True)

        def ld_vm(h, c):
            par = h % 2
            vo = 0 if par == 0 else 64
            P.op("sp", lambda e: e.dma_start(
                out=VMx[par][:, c * CB:(c + 1) * CB, vo:vo + 64],
                in_=vmla_d[c * CB * 128:(c + 1) * CB * 128, h * 64:(h + 1) * 64].rearrange("(n p) c -> p n c", p=128)),
                 reads=["vmla_d"], writes=["VM%dc%d" % (par, c)], dma=True)

        jobs = []
        gidx = -1
        for h in range(8):
            for I in range(NQ - 1, -1, -1):
                bl = blocks(I)
                for j, (kb, c0, mi) in enumerate(bl):
                    if j == 0:
                        gidx += 1
                    jobs.append(dict(hp=h // 2, h=h, I=I, kb=kb, c0=c0, mi=mi, first=(j == 0), last=(j == len(bl) - 1),
                                     g=gidx, k=j, ch=kb // CB))
        nj = len(jobs)
        pc0 = 512
        for jb in jobs:
            if jb["first"]:
                pc0 = 512
            jb["pc0"] = pc0
            pc0 = jb["c0"]
        trig = {}
        PD = 5
        for idx, jb in enumerate(jobs):
            nxt = jobs[idx + 1] if idx + 1 < nj else None
            h, c = jb["h"], jb["ch"]
            if c * CB // 16 == jb["I"] or True:
                pass
        lastread = {}
        for idx, jb in enumerate(jobs):
            lastread[(jb["h"], jb["ch"])] = idx
        for (h, c), idx in lastread.items():
            trig.setdefault(idx + PD, []).append((h, c))

        P.op("dve", lambda e: e.memset(VMx[0][:, :, 64:128], 1.0), writes=["VM0c%d" % c for c in range(NCH)])
        P.op("dve", lambda e: e.memset(VMx[1][:, :, 0:64], 1.0), writes=["VM1c%d" % c for c in range(NCH)])
        P.op("pool", lambda e: e.memset(KF[64:128, :], 0.0), writes=["KFc%d" % c for c in range(NCH)])
        P.op("dve", lambda e: e.memset(rhi[:], 0.0), writes=["rhi"])
        P.op("dve", lambda e: e.memset(rlo[:], 0.0), writes=["rlo"])
        for c in range(NCH - 1, -1, -1):
            ld_kt(0, c)
            ld_vv(0, c)
            ld_kf(0, c)
            ld_vm(0, c)
        for c in range(NCH - 1, -1, -1):
            ld_vm(1, c)

        def sb1(j, jb):
            hp, h, I, kb, c0, mi, ch = jb["hp"], jb["h"], jb["I"], jb["kb"], jb["c0"], jb["mi"], jb["ch"]
            po = (h % 2) * 64
            bk = B_SSB[j % 4]
            kt = KT[po:po + 64, kb * 128:(kb + 1) * 128]
            qt = QTsb[po:po + 64, hp, I * 512 + c0:(I + 1) * 512]
            P.op("pe", lambda e: e.matmul(pf[bk][:, c0:512], lhsT=kt, rhs=qt, start=True, stop=(mi is None)),
                 reads=["KTc%d" % ch, "QTsb"], writes=[PS[bk]])
            if mi is not None:
                mo = 32 * mi
                P.op("pe", lambda e: e.matmul(pf[bk][:, c0:c0 + 128 - mo], lhsT=identb, rhs=mskb[:, mi, mo:128], start=False, stop=True),
                     reads=["cstb", "mskb"], writes=[PS[bk]])
            P.op("act", lambda e: e.activation(out=eb[j % 2][:, c0:512], in_=pf[bk][:, c0:512], func=AF.Exp),
                 reads=[PS[bk]], writes=["eb%d" % (j % 2)])

        def sb2(j, jb):
            c0 = jb["c0"]
            s3 = j % 3
            P.op("act", lambda e: e.activation(out=spb[s3][:, c0:512], in_=eb[j % 2][:, c0:512], func=AF.Ln, bias=1.0),
                 reads=["eb%d" % (j % 2)], writes=["spb%d" % s3])
            if not jb["last"]:
                rb = j % 3
                wbf = (j + 1) % 3
                if jb["first"]:
                    P.op("dve", lambda e: e.tensor_copy(out=ssum[wbf][:, c0:512], in_=spb[s3][:, c0:512]),
                         reads=["spb%d" % s3], writes=["ssum%d" % wbf])
                else:
                    pc0 = jb["pc0"]
                    P.op("dve", lambda e: e.tensor_tensor(out=ssum[wbf][:, pc0:512], in0=ssum[rb][:, pc0:512],
                                                          in1=spb[s3][:, pc0:512], op=ALU.add),
                         reads=["ssum%d" % rb, "spb%d" % s3], writes=["ssum%d" % wbf])
                    if pc0 > c0:
                        P.op("dve", lambda e: e.tensor_copy(out=ssum[wbf][:, c0:pc0], in_=spb[s3][:, c0:pc0]),
                             reads=["spb%d" % s3], writes=["ssum%d" % wbf])

        def sb3(j, jb):
            c0 = jb["c0"]
            ab = B_SSB[j % 4]
            s3 = j % 3
            gpar = j % 3
            if not jb["first"]:
                pc0 = jb["pc0"]
                P.op("pe", lambda e: e.matmul(pf[ab][:, pc0:512], lhsT=monesb, rhs=ssum[gpar][:, pc0:512],
                                              start=False, stop=False, skip_group_check=True),
                     reads=["cstb", "ssum%d" % gpar], writes=[PS[ab]])
            P.op("pe", lambda e: e.matmul(pf[ab][:, c0:512], lhsT=trib, rhs=spb[s3][:, c0:512],
                                          start=False, stop=True, skip_group_check=True),
                 reads=["cstb", "spb%d" % s3], writes=[PS[ab]])
            P.op("act", lambda e: e.activation(out=wb[s3][:, c0:512], in_=pf[ab][:, c0:512], func=AF.Exp),
                 reads=[PS[ab]], writes=["wb%d" % s3])

        def sb4(j, jb):
            hp, h, I, kb, c0, ch = jb["hp"], jb["h"], jb["I"], jb["kb"], jb["c0"], jb["ch"]
            s3 = j % 3
            ob = B_OSB
            po = (h % 2) * 64
            if jb["first"]:
                P.op("pe", lambda e: e.matmul(pf[ob][:, 0:512], lhsT=VV[:, kb, :], rhs=wb[s3][:, 0:512],
                                              start=True, stop=jb["last"], skip_group_check=True),
                     reads=["VVc%d" % ch, "wb%d" % s3], writes=[PS[ob]])
            else:
                P.op("pe", lambda e: e.matmul(pf[ob][:, c0:512], lhsT=VV[:, kb, :], rhs=wb[s3][:, c0:512],
                                              start=False, stop=jb["last"], skip_group_check=True),
                     reads=["VVc%d" % ch, "wb%d" % s3], writes=[PS[ob]])
            if jb["last"]:
                cols = slice(I * 512, (I + 1) * 512)
                P.op("dve", lambda e: e.tensor_tensor(out=SGsb[po:po + 64, hp, cols], in0=pf[ob][po:po + 64, :],
                                                      in1=SGsb[po:po + 64, hp, cols], op=ALU.mult),
                     reads=[PS[ob], "SGsb"], writes=["SGsb"])

        def ml1(j, jb):
            hp, h, I, kb, c0, mi, ch = jb["hp"], jb["h"], jb["I"], jb["kb"], jb["c0"], jb["mi"], jb["ch"]
            bk = B_SML[0]
            s3 = j % 3
            P.op("pe", lambda e: e.matmul(pf[bk][:, c0:512], lhsT=KF[:, kb * 128:(kb + 1) * 128],
                                          rhs=QFT[:, h, I * 512 + c0:(I + 1) * 512], start=True, stop=(mi is None)),
                 reads=["KFc%d" % ch, "QFT"], writes=[PS[bk]])
            if mi is not None:
                mo = 32 * mi
                P.op("pe", lambda e: e.matmul(pf[bk][:, c0:c0 + 128 - mo], lhsT=identb, rhs=mskb[:, 4 + mi, mo:128], start=False, stop=True),
                     reads=["cstb", "mskb"], writes=[PS[bk]])
            P.op("act", lambda e: e.activation(out=pbuf[s3][:, c0:512], in_=pf[bk][:, c0:512], func=AF.Exp, scale=SCALE),
                 reads=[PS[bk]], writes=["pbuf%d" % s3])

        def ml2(j, jb):
            hp, h, I, kb, c0, ch = jb["hp"], jb["h"], jb["I"], jb["kb"], jb["c0"], jb["ch"]
            s3 = j % 3
            ob = B_OML
            db = B_DML
            par = h % 2
            po = par * 64
            dq = 64 - po
            if jb["first"]:
                P.op("pe", lambda e: e.matmul(pf[ob][:, 0:512], lhsT=VMx[par][:, kb, :], rhs=pbuf[s3][:, 0:512],
                                              start=True, stop=jb["last"], skip_group_check=True),
                     reads=["VM%dc%d" % (par, ch), "pbuf%d" % s3], writes=[PS[ob]])
            else:
                P.op("pe", lambda e: e.matmul(pf[ob][:, c0:512], lhsT=VMx[par][:, kb, :], rhs=pbuf[s3][:, c0:512],
                                              start=False, stop=jb["last"], skip_group_check=True),
                     reads=["VM%dc%d" % (par, ch), "pbuf%d" % s3], writes=[PS[ob]])
            if jb["last"]:
                cols = slice(I * 512, (I + 1) * 512)
                P.op("dve", lambda e: e.tensor_copy(out=ocp[:, :], in_=pf[ob][:, :]), reads=[PS[ob]], writes=["ocp"])
                def fin1(q):
                    if q == 0:
                        P.op("act", lambda e: e.activation(out=rec[dq:dq + 64, :], in_=ocp[dq:dq + 64, :], func=AF.Ln),
                             reads=["ocp"], writes=["rec"])
                    else:
                        P.op("act", lambda e: e.activation(out=rec[dq:dq + 64, :], in_=rec[dq:dq + 64, :], func=AF.Exp,
                                                           scale=-1.0),
                             reads=["rec"], writes=["rec"])

                def fin1b():
                    P.op("dve", lambda e: e.tensor_copy(out=rhi[dq:dq + 64, :], in_=rec[dq:dq + 64, :]),
                         reads=["rec"], writes=["rhi"])
                    P.op("dve", lambda e: e.tensor_tensor(out=rlo[dq:dq + 64, :], in0=rec[dq:dq + 64, :],
                                                          in1=rhi[dq:dq + 64, :], op=ALU.subtract),
                         reads=["rec", "rhi"], writes=["rlo"])
                deferred.setdefault(cur[0] + 2, []).append(lambda: fin1(0))
                deferred.setdefault(cur[0] + 3, []).append(lambda: fin1(1))
                deferred.setdefault(cur[0] + 4, []).append(fin1b)
                def fin2():
                    P.op("pe", lambda e: e.matmul(pf[db][:, :], lhsT=swapb, rhs=rhi[:, :], start=True, stop=False),
                         reads=["cstb", "rhi"], writes=[PS[db]])
                    P.op("pe", lambda e: e.matmul(pf[db][:, :], lhsT=swapb, rhs=rlo[:, :], start=False, stop=True),
                         reads=["cstb", "rlo"], writes=[PS[db]])
                    P.op("dve", lambda e: e.tensor_tensor(out=otmp[po:po + 64, :], in0=pf[db][po:po + 64, :],
                                                          in1=ocp[po:po + 64, :], op=ALU.mult),
                         reads=[PS[db], "ocp"], writes=["otmp"])
                    P.op("dve", lambda e: e.tensor_tensor(out=SGmla[po:po + 64, hp, cols], in0=otmp[po:po + 64, :],
                                                          in1=SGmla[po:po + 64, hp, cols], op=ALU.mult),
                         reads=["otmp", "SGmla"], writes=["SGmla"])
                deferred.setdefault(cur[0] + 7, []).append(fin2)

        deferred = {}
        cur = [0]
        for j, jb in enumerate(jobs):
            if jb["first"] and jb["c0"] > 0:
                def zp(j=j, c0=jb["c0"]):
                    P.op("pool", lambda e: e.memset(pbuf[j % 3][:, 0:c0], 0.0), writes=["pbuf%d" % (j % 3)])

                def zw(j=j, c0=jb["c0"]):
                    P.op("pool", lambda e: e.memset(wb[j % 3][:, 0:c0], 0.0), writes=["wb%d" % (j % 3)])
                deferred.setdefault(max(j - 1, -1), []).append(zp)
                deferred.setdefault(j + 1, []).append(zw)
        for f in deferred.pop(-1, []):
            f()
        for step in range(nj + 3 + PD):
            cur[0] = step
            for (h, c) in trig.get(step, []):
                if h + 1 < 8:
                    ld_kf(h + 1, c)
                if h + 2 < 8:
                    ld_vm(h + 2, c)
                if h % 2 == 1 and h // 2 + 1 < 4:
                    ld_kt(h // 2 + 1, c)
                    ld_vv(h // 2 + 1, c)
            if step < nj:
                sb1(step, jobs[step])
                ml1(step, jobs[step])
            if 0 <= step - 1 < nj:
                sb2(step - 1, jobs[step - 1])
            if 0 <= step - 2 < nj:
                sb3(step - 2, jobs[step - 2])
            if 0 <= step - 1 < nj:
                ml2(step - 1, jobs[step - 1])
            if 0 <= step - 3 < nj:
                sb4(step - 3, jobs[step - 3])
            for f in deferred.pop(step, []):
                f()
        for k in sorted(deferred):
            for f in deferred[k]:
                f()
        P.barrier()

    if debug:
        dq5 = dram("dbg_ogsb", [128, 4, SO], BF16, "ExternalOutput")
        dq6 = dram("dbg_ogmla", [128, 4, SO], BF16, "ExternalOutput")
        P.op("sp", lambda e: e.dma_start(out=dq5, in_=SGsb[:]), reads=["SGsb"], writes=["dq5"], dma=True)
        P.op("sp", lambda e: e.dma_start(out=dq6, in_=SGmla[:]), reads=["SGmla"], writes=["dq6"], dma=True)
        P.barrier()
    sq.close()

    with ExitStack() as se:
        WGL = sb(se, "WGL", [128, 8, 2048], BF16)
        WOS = sb(se, "WOS", [128, 4, 1024], BF16)
        WOM = sb(se, "WOM", [128, 4, 1024], BF16)
        WOUT = sb(se, "WOUT", [128, 8, 1024], BF16)
        P.op("sp", lambda e: e.dma_start(out=WGL[:], in_=wgl_s), reads=["wgl_s"], writes=["WGL"], dma=True)
        P.op("sp", lambda e: e.dma_start(out=WOS[:], in_=wos_s), reads=["wos_s"], writes=["WOS"], dma=True)
        P.op("sp", lambda e: e.dma_start(out=WOM[:], in_=wom_s), reads=["wom_s"], writes=["WOM"], dma=True)
        P.op("sp", lambda e: e.dma_start(out=WOUT[:], in_=wout_s), reads=["wout_s"], writes=["WOUT"], dma=True)
        hoT = [sb(se, "hoT%d" % i, [128, 8, 512], BF16) for i in range(2)]
        G = sb(se, "G", [128, 16, 512], BF16)
        mg = [sb(se, "mg%d" % i, [128, 8, 512], BF16) for i in range(2)]
        mt1 = [sb(se, "mt1_%d" % i, [128, 512], F32) for i in range(2)]
        mt2 = [sb(se, "mt2_%d" % i, [128, 512], F32) for i in range(2)]
        xob = [sb(se, "xob%d" % i, [128, 1024], F32) for i in range(4)]
        resb = [sb(se, "resb%d" % i, [128, 1024], F32) for i in range(2)]
        gfs = sb(se, "gfs", [128, 1024], F32)
        ssf = sb(se, "ssf", [128, 2], F32)
        invf = sb(se, "invf", [128, 2], F32)
        pend = []
        P.op("sp", lambda e: e.dma_start(out=gfs[:], in_=gf_d), writes=["gfs"], dma=True)
        pj = [0]

        def ld_hoT(tq):
            hT = hoT[tq % 2]
            P.op("sp", lambda e: e.dma_start(out=hT[:], in_=hoT_d[:, :, tq * 512:(tq + 1) * 512]),
                 reads=["hoT_d"], writes=["hoT%d" % (tq % 2)], dma=True)

        def e_gl(tq):
            hT = hoT[tq % 2]
            hTn = "hoT%d" % (tq % 2)
            for mt in range(16):
                b = pj[0] % 2
                pj[0] += 1
                for k in range(8):
                    P.op("pe", lambda e, b=b, k=k, mt=mt: e.matmul(pf[b][:, :], lhsT=WGL[:, k, mt * 128:(mt + 1) * 128],
                                                                  rhs=hT[:, k, :], start=(k == 0), stop=(k == 7)),
                         reads=["WGL", hTn], writes=[PS[b]])
                P.op("act", lambda e, b=b, mt=mt: e.activation(out=G[:, mt, :], in_=pf[b][:, :], func=AF.Sigmoid,
                                                               bias=bg[:, mt:mt + 1]),
                     reads=[PS[b], "bg"], writes=["G"])

        def e_y(tq):
            cols = slice(tq * 512, (tq + 1) * 512)
            mgb = mg[tq % 2]
            for et in range(8):
                b1 = 2 + (et % 2) * 2
                b2 = b1 + 1
                m1 = mt1[et % 2]
                m2 = mt2[et % 2]
                for hp in range(4):
                    P.op("pe", lambda e, b1=b1, hp=hp, et=et: e.matmul(pf[b1][:, :], lhsT=WOS[:, hp, et * 128:(et + 1) * 128],
                                                                      rhs=SGsb[:, hp, cols], start=(hp == 0), stop=(hp == 3)),
                         reads=["WOS", "SGsb"], writes=[PS[b1]])
                for hp in range(4):
                    P.op("pe", lambda e, b2=b2, hp=hp, et=et: e.matmul(pf[b2][:, :], lhsT=WOM[:, hp, et * 128:(et + 1) * 128],
                                                                      rhs=SGmla[:, hp, cols], start=(hp == 0), stop=(hp == 3)),
                         reads=["WOM", "SGmla"], writes=[PS[b2]])
                P.op("dve", lambda e, b1=b1, et=et, m1=m1: e.tensor_tensor(out=m1[:], in0=pf[b1][:, :], in1=G[:, et, :], op=ALU.mult),
                     reads=[PS[b1], "G"], writes=["mt1_%d" % (et % 2)])
                P.op("dve", lambda e, b2=b2, et=et, m2=m2: e.tensor_tensor(out=m2[:], in0=pf[b2][:, :], in1=G[:, 8 + et, :], op=ALU.mult),
                     reads=[PS[b2], "G"], writes=["mt2_%d" % (et % 2)])
                P.op("pool", lambda e, et=et, m1=m1, m2=m2, mgb=mgb: e.tensor_tensor(out=mgb[:, et, :], in0=m1[:], in1=m2[:], op=ALU.add),
                     reads=["mt1_%d" % (et % 2), "mt2_%d" % (et % 2)], writes=["mg%d" % (tq % 2)])

        def ld_x(tq):
            for a in range(4):
                n = tq * 4 + a
                P.op("sp", lambda e, n=n, a=a: e.dma_start(out=xob[a][:], in_=xo[n * 128:(n + 1) * 128, :]),
                     writes=["xob%d" % a], dma=True)

        def e_out(tq):
            mgb = mg[tq % 2]
            for a in range(4):
                n = tq * 4 + a
                xi = n % 2
                for half in range(2):
                    b = 6 + half
                    for k in range(8):
                        P.op("pe", lambda e, b=b, k=k, a=a, half=half: e.matmul(
                            pf[b][:, :], lhsT=mgb[:, k, a * 128:(a + 1) * 128], rhs=WOUT[:, k, half * 512:(half + 1) * 512],
                            start=(k == 0), stop=(k == 7)),
                             reads=["mg%d" % (tq % 2), "WOUT"], writes=[PS[b]])
                    P.op("dve", lambda e, b=b, xi=xi, a=a, half=half: e.tensor_tensor(
                        out=resb[xi][:, half * 512:(half + 1) * 512], in0=pf[b][:, :],
                        in1=xob[a][:, half * 512:(half + 1) * 512], op=ALU.add),
                         reads=[PS[b], "xob%d" % a], writes=["resb%d" % xi])
                P.op("act", lambda e, xi=xi: e.activation(out=junk[:], in_=resb[xi][:], func=AF.Square,
                                                          accum_out=ssf[:, xi:xi + 1]),
                     reads=["resb%d" % xi], writes=["junk", "ssf%d" % xi])
                P.op("act", lambda e, xi=xi: e.activation(out=invf[:, xi:xi + 1], in_=ssf[:, xi:xi + 1], func=AF.Ln,
                                                          scale=1.0 / D, bias=EPS),
                     reads=["ssf%d" % xi], writes=["invf%d" % xi])
                P.op("act", lambda e, xi=xi: e.activation(out=invf[:, xi:xi + 1], in_=invf[:, xi:xi + 1], func=AF.Exp,
                                                          scale=-0.5),
                     reads=["invf%d" % xi], writes=["invf%d" % xi])

                def tail(n=n, xi=xi):
                    P.op("dve", lambda e: e.scalar_tensor_tensor(out=resb[xi][:], in0=resb[xi][:], scalar=invf[:, xi:xi + 1],
                                                                 in1=gfs[:], op0=ALU.mult, op1=ALU.mult),
                         reads=["resb%d" % xi, "invf%d" % xi, "gfs"], writes=["resb%d" % xi])
                    P.op("sp", lambda e: e.dma_start(out=out_d[n * 128:(n + 1) * 128, :], in_=resb[xi][:]),
                         reads=["resb%d" % xi], writes=["out_d"], dma=True)
                if pend:
                    pend.pop(0)()
                pend.append(tail)

        ld_hoT(0)
        for tq in range(NQ + 1):
            if tq + 1 < NQ:
                ld_hoT(tq + 1)
            if tq >= 1:
                ld_x(tq - 1)
            if tq < NQ:
                e_gl(tq)
            if tq >= 1:
                e_out(tq - 1)
            if tq < NQ:
                e_y(tq)
        while pend:
            pend.pop(0)()
        P.barrier()
    root.close()


def host_consts(S, c):
    NB = S // 128
    SO = S // 4
    NBO = SO // 128
    half = 16
    inv_freq = (np.float32(10000.0) ** (-np.arange(half, dtype=np.float32) / np.float32(half))).astype(np.float32)
    pos = np.arange(S, dtype=np.float32)
    ang = (pos[:, None] * inv_freq[None, :]).astype(np.float32)
    cosf = np.cos(ang).astype(np.float32)
    sinf = np.sin(ang).astype(np.float32)
    cf = np.ascontiguousarray(cosf.reshape(NB, 128, 16).transpose(1, 0, 2))
    sf = np.ascontiguousarray(sinf.reshape(NB, 128, 16).transpose(1, 0, 2))
    co = cosf[c::4].reshape(NBO, 128, 16).transpose(1, 0, 2)
    so = sinf[c::4].reshape(NBO, 128, 16).transpose(1, 0, 2)
    co4 = np.ascontiguousarray(np.broadcast_to(co[:, :, None, :], (128, NBO, 4, 16))).astype(np.float32)
    so4 = np.ascontiguousarray(np.broadcast_to(so[:, :, None, :], (128, NBO, 4, 16))).astype(np.float32)
    ident = np.eye(128, dtype=np.float32)
    jj = np.arange(128)
    tri = -(jj[:, None] >= jj[None, :]).astype(np.float32)
    swap = np.zeros((128, 128), np.float32)
    swap[(jj + 64) % 128, jj] = 1.0
    cst = np.ascontiguousarray(np.stack([ident, tri, -np.ones((128, 128), np.float32),
                                         np.ones((128, 128), np.float32), swap], axis=1))
    ss = np.arange(128)[:, None]
    qq = np.arange(128)[None, :]
    msb = np.zeros((128, 4, 128), np.float32)
    mmla = np.zeros((128, 4, 128), np.float32)
    for m in range(4):
        msb[:, m, :] = np.where(128 * m + ss < 4 * qq + c, 0.0, NEG)
        mmla[:, m, :] = np.where(2 * m + ss // 64 <= qq // 16, 0.0, NEG)
    esel = np.zeros((128, 128), np.float32)
    esel[np.arange(32), 64 + np.arange(32)] = 1.0
    return dict(cosf=cf, sinf=sf, coso=co4, sino=so4, cst=cst, msb=msb, mmla=mmla, esel=esel)


def make_in_maps(S, x, norm_in_g, w_in, b_gate, q_norm_g, w_q_up, kv_norm_g, w_kv_up,
                 w_o_sb, w_o_mla, w_out, norm_f_g):
    f = lambda a: np.ascontiguousarray(np.asarray(a, dtype=np.float32))
    B = x.shape[0]
    shared = dict(
        w_in=f(w_in[0]),
        gin=f(np.asarray(norm_in_g[0]).reshape(8, 128).T),
        bg=f(np.asarray(b_gate[0]).reshape(16, 128).T),
        gq=f(np.asarray(q_norm_g[0]).reshape(3, 128).T),
        wq=f(w_q_up[0]),
        gkv=f(np.asarray(kv_norm_g[0]).reshape(2, 128).T),
        wkv=f(w_kv_up[0]),
        wosb=f(w_o_sb[0]), womla=f(w_o_mla[0]), wout=f(w_out[0]),
        gf=f(np.broadcast_to(np.asarray(norm_f_g)[None, :], (128, 1024))),
    )
    consts = [host_consts(S, c) for c in range(4)]
    maps = []
    for b in range(B):
        xb = f(x[b])
        for c in range(4):
            m = dict(shared)
            m.update(consts[c])
            m["xf"] = xb
            m["xo"] = f(xb[c::4])
            maps.append(m)
    return maps


_CACHE = {}


def get_program(S, debug=False):
    key = (S, debug)
    if key not in _CACHE:
        P0 = Prog(None)
        build(P0, S, debug)
        nc = bass.Bass("TRN2", target_bir_lowering=False)
        P1 = Prog(nc, needed=P0.used)
        build(P1, S, debug)
        _CACHE[key] = nc
    return _CACHE[key]


def run(S, inputs, debug=False):
    x = np.asarray(inputs["x"])
    B = x.shape[0]
    maps = make_in_maps(S, **{k: np.asarray(v) for k, v in inputs.items()})
    nc = get_program(S, debug)
    ncores = 4 * B
    res = run_bass_kernel_spmd(nc, maps, core_ids=list(range(ncores)))
    out = np.empty((B, S, D), np.float32)
    for b in range(B):
        for c in range(4):
            out[b, c::4, :] = res.results[b * 4 + c]["out"]
    return out, res


def kernel(x, norm_in_g, w_in, b_gate, q_norm_g, w_q_up, kv_norm_g, w_kv_up,
           w_o_sb, w_o_mla, w_out, norm_f_g):
    inputs = dict(x=x, norm_in_g=norm_in_g, w_in=w_in, b_gate=b_gate, q_norm_g=q_norm_g, w_q_up=w_q_up,
                  kv_norm_g=kv_norm_g, w_kv_up=w_kv_up, w_o_sb=w_o_sb, w_o_mla=w_o_mla, w_out=w_out,
                  norm_f_g=norm_f_g)
    S = np.asarray(x).shape[1]
    out, _ = run(S, inputs)
    return out
```

```python
import numpy as np
from contextlib import ExitStack
import concourse.bass as bass
import concourse.mybir as mybir
from concourse.bass_utils import run_bass_kernel_spmd

F32 = mybir.dt.float32
BF16 = mybir.dt.bfloat16
AF = mybir.ActivationFunctionType
ALU = mybir.AluOpType

D = 1024
EPS = 1e-6
NEG = -30000.0
C_QSB, C_KSB, C_VSB, C_GSB, C_CQ, C_CKV, C_KR, C_GM, C_GL = 0, 512, 1024, 1536, 2048, 2432, 2688, 2720, 3232

ENGS = ["pe", "act", "dve", "pool", "sp"]
NDS = 16


class Dummy:
    def __getitem__(self, k):
        return self

    def __getattr__(self, k):
        return self

    def __call__(self, *a, **k):
        return self


DUMMY = Dummy()


class Tok:
    __slots__ = ("eng", "idx", "dma", "clock", "dclock")

    def __init__(self, eng, idx, dma, clock, dclock):
        self.eng = eng
        self.idx = idx
        self.dma = dma
        self.clock = clock
        self.dclock = dclock


class Res:
    __slots__ = ("w", "rs", "excl")

    def __init__(self, excl=False):
        self.w = None
        self.rs = []
        self.excl = excl


class Prog:
    def __init__(self, nc, needed=None):
        self.nc = nc
        self.dry = nc is None
        self.needed = needed
        self.count = {e: 0 for e in ENGS}
        self.sig = {e: 0 for e in ENGS}
        self.sigval = {}
        self.ndma = 0
        self.ndk = {"h": 0, "s": 0}
        self.dtoks = {"h": [], "s": []}
        self.known = {e: {f: 0 for f in ENGS} for e in ENGS}
        self.dknown = {e: {} for e in ENGS}
        self.used = set()
        self.res = {}
        self.nwaits = 0
        if not self.dry:
            self.engobj = {"pe": nc.tensor, "act": nc.scalar, "dve": nc.vector,
                           "pool": nc.gpsimd, "sp": nc.sync}
            self.sems = {}

    def alloc_sems(self, stack):
        if self.dry:
            return
        for e in ENGS:
            self.sems[e] = stack.enter_context(self.nc.semaphore("s_" + e))
        self.dsem = {"h": [], "s": []}
        for i in range(NDS):
            self.dsem["h"].append(stack.enter_context(self.nc.semaphore("d_%d" % i)))
            self.dsem["s"].append(stack.enter_context(self.nc.semaphore("q_%d" % i)))

    def R(self, name):
        r = self.res.get(name)
        if r is None:
            excl = isinstance(name, str) and name.startswith("ps")
            r = Res(excl)
            self.res[name] = r
        return r

    def _knows(self, e, t):
        if t.dma:
            return self.dknown[e].get((t.dma, t.idx % NDS), -1) >= t.idx
        return self.known[e][t.eng] >= t.idx + 1

    def _merge(self, e, t):
        k = self.known[e]
        for f, v in t.clock.items():
            if v > k[f]:
                k[f] = v
        dk = self.dknown[e]
        for s, v in t.dclock.items():
            if v > dk.get(s, -1):
                dk[s] = v
        if t.dma:
            s = (t.dma, t.idx % NDS)
            if t.idx > dk.get(s, -1):
                dk[s] = t.idx
        else:
            if t.idx + 1 > k[t.eng]:
                k[t.eng] = t.idx + 1

    def _wait(self, e, t):
        if self._knows(e, t):
            return
        self.nwaits += 1
        self.used.add(("dma" + t.dma, t.idx) if t.dma else (t.eng, t.idx))
        if not self.dry:
            if t.dma:
                sem = self.dsem[t.dma][t.idx % NDS]
                val = 16 * (t.idx // NDS + 1)
            else:
                sem = self.sems[t.eng]
                val = self.sigval[(t.eng, t.idx)]
            self.engobj[e].wait_ge(sem, val)
        self._merge(e, t)

    def op(self, eng, fn, reads=(), writes=(), dma=False):
        deps = []
        rl = [self.R(n) for n in reads]
        wl = [self.R(n) for n in writes]
        for r in rl:
            if r.w is not None:
                deps.append((r.w, "raw"))
            if r.excl:
                for t in r.rs:
                    deps.append((t, "rr"))
        for r in wl:
            if r.w is not None:
                deps.append((r.w, "waw"))
            for t in r.rs:
                deps.append((t, "war"))
        best = {}
        for t, kind in deps:
            if (not t.dma) and (not dma) and t.eng == eng and (eng == "pe" or kind == "rr"):
                continue
            key = ("dma" + t.dma, t.idx % NDS) if t.dma else ("eng", t.eng)
            b = best.get(key)
            if b is None or t.idx > b.idx:
                best[key] = t
        for key in sorted(best.keys()):
            self._wait(eng, best[key])
        if dma:
            kind = "s" if eng == "pool" else "h"
            idx = self.ndk[kind]
            if idx >= NDS:
                self._wait(eng, self.dtoks[kind][idx - NDS])
            self.ndk[kind] += 1
            self.ndma += 1
            tok = Tok(eng, idx, kind, dict(self.known[eng]), dict(self.dknown[eng]))
            self.dtoks[kind].append(tok)
            if not self.dry:
                ins = fn(self.engobj[eng])
                ins.then_inc(self.dsem[kind][idx % NDS], 16)
        else:
            idx = self.count[eng]
            self.count[eng] += 1
            tok = Tok(eng, idx, None, dict(self.known[eng]), dict(self.dknown[eng]))
            if not self.dry:
                ins = fn(self.engobj[eng])
                if (eng, idx) in self.needed:
                    self.sig[eng] += 1
                    self.sigval[(eng, idx)] = self.sig[eng]
                    ins.then_inc(self.sems[eng], 1)
        for r in rl:
            r.rs.append(tok)
        for r in wl:
            r.w = tok
            r.rs = []
        return tok

    def wait_all(self, eng):
        toks = []
        for r in self.res.values():
            if r.w is not None:
                toks.append(r.w)
            toks.extend(r.rs)
        best = {}
        for t in toks:
            if (not t.dma) and t.eng == eng:
                continue
            key = ("dma" + t.dma, t.idx % NDS) if t.dma else ("eng", t.eng)
            b = best.get(key)
            if b is None or t.idx > b.idx:
                best[key] = t
        for key in sorted(best.keys()):
            self._wait(eng, best[key])

    def barrier(self):
        for e in ENGS:
            self.wait_all(e)


def build(P, S, debug=False):
    nc = P.nc
    dry = P.dry
    NT = S // 512
    NB = S // 128
    SO = S // 4
    NQ = SO // 512
    NBO = SO // 128

    def dram(name, shape, dt, kind):
        if dry:
            return DUMMY
        return nc.dram_tensor(name, list(shape), dt, kind=kind).ap()

    xf = dram("xf", [S, D], F32, "ExternalInput")
    xo = dram("xo", [SO, D], F32, "ExternalInput")
    w_in = dram("w_in", [D, 5280], F32, "ExternalInput")
    gin_d = dram("gin", [128, 8], F32, "ExternalInput")
    bg_d = dram("bg", [128, 16], F32, "ExternalInput")
    gq_d = dram("gq", [128, 3], F32, "ExternalInput")
    wq_d = dram("wq", [384, 768], F32, "ExternalInput")
    gkv_d = dram("gkv", [128, 2], F32, "ExternalInput")
    wkv_d = dram("wkv", [256, 1024], F32, "ExternalInput")
    wosb_d = dram("wosb", [512, 1024], F32, "ExternalInput")
    womla_d = dram("womla", [512, 1024], F32, "ExternalInput")
    wout_d = dram("wout", [1024, 1024], F32, "ExternalInput")
    gf_d = dram("gf", [128, 1024], F32, "ExternalInput")
    cosf_d = dram("cosf", [128, NB, 16], F32, "ExternalInput")
    sinf_d = dram("sinf", [128, NB, 16], F32, "ExternalInput")
    coso_d = dram("coso", [128, NBO, 4, 16], F32, "ExternalInput")
    sino_d = dram("sino", [128, NBO, 4, 16], F32, "ExternalInput")
    cst_d = dram("cst", [128, 5, 128], F32, "ExternalInput")
    msb_d = dram("msb", [128, 4, 128], F32, "ExternalInput")
    mmla_d = dram("mmla", [128, 4, 128], F32, "ExternalInput")
    esel_d = dram("esel", [128, 128], F32, "ExternalInput")
    out_d = dram("out", [SO, D], F32, "ExternalOutput")
    skind = "ExternalOutput" if debug else "Internal"
    kTsb_d = dram("kTsb", [4, 128, S], BF16, skind)
    vsb_d = dram("vsb", [S, 512], BF16, skind)
    kfT_d = dram("kfT", [8, 96, S], BF16, skind)
    vmla_d = dram("vmla", [S, 512], BF16, skind)
    hoT_d = dram("hoT", [128, 8, SO], BF16, skind)

    root = ExitStack()
    P.alloc_sems(root)

    ARENA_BYTES = 206 * 1024
    arena = None if dry else root.enter_context(nc.sbuf_tensor("arena", [128, ARENA_BYTES // 2], BF16))
    freelist = [[0, ARENA_BYTES]]
    peak = [0]

    def a_free(blk):
        freelist.append(list(blk))
        freelist.sort()
        i = 0
        while i + 1 < len(freelist):
            if freelist[i][0] + freelist[i][1] == freelist[i + 1][0]:
                freelist[i][1] += freelist[i + 1][1]
                del freelist[i + 1]
            else:
                i += 1

    def sb(stack, name, shape, dt):
        isz = 4 if dt == F32 else 2
        n = 1
        for d in shape[1:]:
            n *= d
        nb = (n * isz + 63) // 64 * 64
        for fb in freelist:
            if fb[1] >= nb:
                off = fb[0]
                fb[0] += nb
                fb[1] -= nb
                break
        else:
            raise RuntimeError("SBUF arena full allocating %s (%d B); free=%s" % (name, nb, freelist))
        if fb[1] == 0:
            freelist.remove(fb)
        stack.callback(a_free, (off, nb))
        used = ARENA_BYTES - sum(f[1] for f in freelist)
        peak[0] = max(peak[0], used)
        if dry:
            return DUMMY
        v = arena[0:shape[0], off // 2: off // 2 + n * isz // 2]
        if dt == F32:
            v = v.bitcast(F32)
        if len(shape) == 2:
            return v
        names = " ".join("d%d" % i for i in range(1, len(shape)))
        kw = {"d%d" % i: shape[i] for i in range(1, len(shape) - 1)}
        return v.rearrange("p (%s) -> p %s" % (names, names), **kw)

    pf = []
    pbv = []
    for i in range(8):
        if dry:
            pf.append(DUMMY)
            pbv.append(DUMMY)
        else:
            t = root.enter_context(nc.psum_tensor("pb%d" % i, [128, 512], F32))
            pf.append(t)
            pbv.append(t.bitcast(BF16))
    PS = ["ps%d" % i for i in range(8)]

    cstb = sb(root, "cstb", [128, 5, 128], BF16)
    mskb = sb(root, "mskb", [128, 8, 128], BF16)
    eselb = sb(root, "eselb", [128, 128], BF16)
    scst = ExitStack()
    cstf = sb(scst, "cstf", [128, 5, 128], F32)
    mskf = sb(scst, "mskf", [128, 8, 128], F32)
    eself = sb(scst, "eself", [128, 128], F32)
    gin = sb(root, "gin_s", [128, 8], F32)
    bg = sb(root, "bg_s", [128, 16], F32)
    gq = sb(root, "gq_s", [128, 3], F32)
    gkv = sb(root, "gkv_s", [128, 2], F32)
    junk = sb(root, "junk", [128, 1024], BF16)

    P.op("sp", lambda e: e.dma_start(out=cstf[:], in_=cst_d), writes=["cstf"], dma=True)
    P.op("sp", lambda e: e.dma_start(out=mskf[:, 0:4, :], in_=msb_d), writes=["mskf"], dma=True)
    P.op("sp", lambda e: e.dma_start(out=mskf[:, 4:8, :], in_=mmla_d), writes=["mskf2"], dma=True)
    P.op("sp", lambda e: e.dma_start(out=eself[:], in_=esel_d), writes=["eself"], dma=True)
    P.op("sp", lambda e: e.dma_start(out=gin[:], in_=gin_d), writes=["gin"], dma=True)
    P.op("sp", lambda e: e.dma_start(out=bg[:], in_=bg_d), writes=["bg"], dma=True)
    P.op("sp", lambda e: e.dma_start(out=gq[:], in_=gq_d), writes=["gq"], dma=True)
    P.op("sp", lambda e: e.dma_start(out=gkv[:], in_=gkv_d), writes=["gkv"], dma=True)
    P.op("dve", lambda e: e.tensor_copy(out=cstb[:], in_=cstf[:]), reads=["cstf"], writes=["cstb"])
    P.op("dve", lambda e: e.tensor_copy(out=mskb[:], in_=mskf[:]), reads=["mskf", "mskf2"], writes=["mskb"])
    P.op("dve", lambda e: e.tensor_copy(out=eselb[:], in_=eself[:]), reads=["eself"], writes=["eselb"])
    identb = cstb[:, 0, :]
    trib = cstb[:, 1, :]
    monesb = cstb[:, 2, :]
    onesb = cstb[:, 3, :]
    swapb = cstb[:, 4, :]
    P.barrier()
    scst.close()

    wgl_s = dram("wgl_s", [128, 8, 2048], BF16, "Internal")
    wos_s = dram("wos_s", [128, 4, 1024], BF16, "Internal")
    wom_s = dram("wom_s", [128, 4, 1024], BF16, "Internal")
    wout_s = dram("wout_s", [128, 8, 1024], BF16, "Internal")
    spp = ExitStack()
    pst = [sb(spp, "pst%d" % i, [128, 2048], F32) for i in range(2)]
    pob = [sb(spp, "pob%d" % i, [128, 2048], BF16) for i in range(2)]
    pcnt = [0]

    def prep_piece(src2d, nk, c0, c1, g=None, dst_fn=None, dst_name=None, dram_dst=None, dram_name=None):
        slot = []

        def setup():
            i = pcnt[0] % 2
            pcnt[0] += 1
            slot.append(i)
            n = c1 - c0
            st = pst[i][:, 0:nk * n].rearrange("p (k n) -> p k n", k=nk)
            P.op("sp", lambda e: e.dma_start(out=st, in_=src2d[:, c0:c1].rearrange("(k p) n -> p k n", p=128)),
                 writes=["pst%d" % i], dma=True)

        def cast(k):
            i = slot[0]
            n = c1 - c0
            st = pst[i][:, 0:nk * n].rearrange("p (k n) -> p k n", k=nk)
            ob = pob[i][:, 0:nk * n].rearrange("p (k n) -> p k n", k=nk)
            if dram_dst is not None:
                o = ob[:, k, :]
                oname = "pob%d" % i
            else:
                o = dst_fn(k, c0, c1)
                oname = dst_name
            if g is not None:
                P.op("dve", lambda e: e.tensor_scalar(out=o, in0=st[:, k, :], scalar1=g[:, k:k + 1],
                                                      scalar2=None, op0=ALU.mult),
                     reads=["pst%d" % i, "gin", "gq", "gkv"], writes=[oname])
            else:
                P.op("act", lambda e: e.copy(out=o, in_=st[:, k, :]), reads=["pst%d" % i], writes=[oname])

        def store():
            i = slot[0]
            n = c1 - c0
            ob = pob[i][:, 0:nk * n].rearrange("p (k n) -> p k n", k=nk)
            P.op("pool", lambda e: e.dma_start(out=dram_dst[:, :, c0:c1], in_=ob), reads=["pob%d" % i],
                 writes=[dram_name], dma=True)
        tasks = [setup] + [(lambda k=k: cast(k)) for k in range(nk)]
        if dram_dst is not None:
            tasks.append(store)
        return tasks

    def prep_tasks(src2d, nk, ncols, **kw):
        out = []
        for c0 in range(0, ncols, 256):
            out += prep_piece(src2d, nk, c0, min(c0 + 256, ncols), **kw)
        return out

    def make_front(xsrc, xbuf, ss8, inv8, hnb, hTb, tag, tbanks):
        def load(ti):
            for a in range(4):
                n = ti * 4 + a
                xb = xbuf[a]
                P.op("sp", lambda e, xb=xb, n=n: e.dma_start(out=xb[:], in_=xsrc[n * 128:(n + 1) * 128, :]),
                     writes=["x%s%d" % (tag, a)], dma=True)

        def f1(ti, a):
            p = ti % 2
            xb = xbuf[a]
            xn = "x%s%d" % (tag, a)
            c = p * 4 + a
            sn = "ss%s%d" % (tag, c)
            P.op("act", lambda e: e.activation(out=junk[:], in_=xb[:], func=AF.Square, accum_out=ss8[:, c:c + 1]),
                 reads=[xn], writes=["junk", sn])
            P.op("act", lambda e: e.activation(out=inv8[:, c:c + 1], in_=ss8[:, c:c + 1], func=AF.Ln,
                                               scale=1.0 / D, bias=EPS), reads=[sn], writes=[sn + "i"])
            P.op("act", lambda e: e.activation(out=inv8[:, c:c + 1], in_=inv8[:, c:c + 1], func=AF.Exp,
                                               scale=-0.5), reads=[sn + "i"], writes=[sn + "i"])
            P.op("dve", lambda e: e.tensor_scalar(out=hnb[p][:, a, :], in0=xb[:], scalar1=inv8[:, c:c + 1],
                                                  scalar2=None, op0=ALU.mult),
                 reads=[xn, sn + "i"], writes=["hn%s%d_%d" % (tag, p, a)])

        def f2(ti, a):
            p = ti % 2
            hT = hTb[p]
            bk = tbanks[a % len(tbanks)]
            hname = "hn%s%d_%d" % (tag, p, a)
            for k in range(8):
                P.op("pe", lambda e, k=k: e.transpose(out=pbv[bk][:, k * 128:(k + 1) * 128],
                                                      in_=hnb[p][:, a, k * 128:(k + 1) * 128], identity=identb),
                     reads=[hname, "cstb"], writes=[PS[bk]])
            src = pbv[bk][:, :].rearrange("p (k t) -> p k t", k=8)
            dst = hT[:, :, a * 128:(a + 1) * 128]
            if a % 2 == 0:
                P.op("act", lambda e: e.copy(out=dst, in_=src), reads=[PS[bk]], writes=["hT%s%d_%d" % (tag, p, a)])
            else:
                P.op("dve", lambda e: e.tensor_copy(out=dst, in_=src), reads=[PS[bk]], writes=["hT%s%d_%d" % (tag, p, a)])
        return load, f1, f2

    evc = [0]

    def evac(src, dst, reads, writes, scale=None, func=None, bias=None, eng=None):
        if func is not None or scale is not None or bias is not None:
            kw = {}
            if scale is not None:
                kw["scale"] = scale
            if bias is not None:
                kw["bias"] = bias
            f = func if func is not None else AF.Copy
            P.op("act", lambda e: e.activation(out=dst, in_=src, func=f, **kw), reads=reads, writes=writes)
            return
        if eng is None:
            eng = "act" if evc[0] % 2 == 0 else "dve"
            evc[0] += 1
        if eng == "act":
            P.op("act", lambda e: e.copy(out=dst, in_=src), reads=reads, writes=writes)
        else:
            P.op("dve", lambda e: e.tensor_copy(out=dst, in_=src), reads=reads, writes=writes)

    def rms_small(src_fn, ncols, ssq, invs, name):
        for a in range(4):
            P.op("act", lambda e, a=a: e.activation(out=junk[:, 0:ncols], in_=src_fn(a), func=AF.Square,
                                                    accum_out=ssq[:, a:a + 1]),
                 reads=[name], writes=["junk", name + "_ss"])
        P.op("act", lambda e: e.activation(out=invs[:], in_=ssq[:], func=AF.Ln, scale=1.0 / ncols, bias=EPS),
             reads=[name + "_ss"], writes=[name + "_inv"])
        P.op("act", lambda e: e.activation(out=invs[:], in_=invs[:], func=AF.Exp, scale=-0.5),
             reads=[name + "_inv"], writes=[name + "_inv"])

    swb = ExitStack()
    WBq = sb(swb, "WBq", [128, 8, 512], BF16)
    WBg = sb(swb, "WBg", [128, 8, 512], BF16)
    WBc = sb(swb, "WBc", [128, 8, 384], BF16)
    WBm = sb(swb, "WBm", [128, 8, 512], BF16)
    WQ = sb(swb, "WQ", [128, 3, 768], BF16)
    later = []
    later += prep_tasks(w_in[:, C_QSB:C_QSB + 512], 8, 512, g=gin, dst_fn=lambda k, a, b: WBq[:, k, a:b], dst_name="WBq")
    later += prep_tasks(w_in[:, C_GSB:C_GSB + 512], 8, 512, g=gin, dst_fn=lambda k, a, b: WBg[:, k, a:b], dst_name="WBg")
    later += prep_tasks(w_in[:, C_CQ:C_CQ + 384], 8, 384, g=gin, dst_fn=lambda k, a, b: WBc[:, k, a:b], dst_name="WBc")
    later += prep_tasks(w_in[:, C_GM:C_GM + 512], 8, 512, g=gin, dst_fn=lambda k, a, b: WBm[:, k, a:b], dst_name="WBm")
    later += prep_tasks(wq_d, 3, 768, g=gq, dst_fn=lambda k, a, b: WQ[:, k, a:b], dst_name="WQ")
    later += prep_tasks(w_in[:, C_GL:C_GL + 2048], 8, 2048, g=gin, dram_dst=wgl_s, dram_name="wgl_s")
    later += prep_tasks(wosb_d, 4, 1024, dram_dst=wos_s, dram_name="wos_s")
    later += prep_tasks(womla_d, 4, 1024, dram_dst=wom_s, dram_name="wom_s")
    later += prep_tasks(wout_d, 8, 1024, dram_dst=wout_s, dram_name="wout_s")
    per_group = (len(later) + NT * 24 - 1) // (NT * 24)

    def sprinkle():
        for _ in range(per_group):
            if later:
                later.pop(0)()

    with ExitStack() as sa:
        WAk = sb(sa, "WAk", [128, 8, 512], BF16)
        WAv = sb(sa, "WAv", [128, 8, 512], BF16)
        WAc = sb(sa, "WAc", [128, 8, 288], BF16)
        WKVK = sb(sa, "WKVK", [128, 2, 8, 128], BF16)
        WKVV = sb(sa, "WKVV", [128, 2, 512], BF16)
        xbuf = [sb(sa, "xA%d" % i, [128, 1024], F32) for i in range(4)]
        ss8 = sb(sa, "ss8A", [128, 8], F32)
        inv8 = sb(sa, "inv8A", [128, 8], F32)
        hnb = [sb(sa, "hnA%d" % i, [128, 4, 1024], BF16) for i in range(2)]
        hTb = [sb(sa, "hTA%d" % i, [128, 8, 512], BF16) for i in range(2)]
        loadA, f1A, f2A = make_front(xf, xbuf, ss8, inv8, hnb, hTb, "A", [0, 1, 6, 7])
        kst = [sb(sa, "kst%d" % i, [128, 4, 512], BF16) for i in range(2)]
        vst = [sb(sa, "vst%d" % i, [128, 4, 512], BF16) for i in range(2)]
        ckvs = sb(sa, "ckvs", [128, 4, 288], F32)
        ssk = sb(sa, "ssk", [128, 4], F32)
        invk = sb(sa, "invk", [128, 4], F32)
        kvn = sb(sa, "kvn", [128, 4, 256], BF16)
        kr = sb(sa, "kr", [128, 4, 32], BF16)
        rt1 = sb(sa, "rt1", [128, 4, 16], F32)
        rt2 = sb(sa, "rt2", [128, 4, 16], F32)
        rt3 = sb(sa, "rt3", [128, 4, 16], F32)
        rt4 = sb(sa, "rt4", [128, 4, 16], F32)
        kvnT = [sb(sa, "kvnT%d" % i, [128, 2, 512], BF16) for i in range(2)]
        krT = [sb(sa, "krT%d" % i, [128, 512], BF16) for i in range(2)]
        for i in range(2):
            P.op("pool", lambda e, i=i: e.memset(krT[i][32:64, :], 0.0), writes=["krT%d" % i])
            P.op("pool", lambda e, i=i: e.memset(krT[i][64:128, :], 0.0), writes=["krT%d" % i])
        kfst = [sb(sa, "kfst%d" % i, [96, 8, 512], BF16) for i in range(2)]
        vmst = [sb(sa, "vmst%d" % i, [128, 4, 512], BF16) for i in range(2)]
        cosf = sb(sa, "cosf", [128, NB, 16], F32)
        sinf = sb(sa, "sinf", [128, NB, 16], F32)
        P.op("sp", lambda e: e.dma_start(out=cosf[:], in_=cosf_d), writes=["cosf"], dma=True)
        P.op("sp", lambda e: e.dma_start(out=sinf[:], in_=sinf_d), writes=["sinf"], dma=True)

        xc = [0]
        pj = [0]

        def pbank():
            b = 2 + (pj[0] % 4)
            pj[0] += 1
            return b

        loadA(0)
        for a in range(4):
            f1A(0, a)
        for t in prep_tasks(w_in[:, C_KSB:C_KSB + 512], 8, 512, g=gin, dst_fn=lambda k, a, b: WAk[:, k, a:b], dst_name="WAk"):
            t()
        for t in prep_tasks(w_in[:, C_VSB:C_VSB + 512], 8, 512, g=gin, dst_fn=lambda k, a, b: WAv[:, k, a:b], dst_name="WAv"):
            t()
        for t in prep_tasks(w_in[:, C_CKV:C_CKV + 288], 8, 288, g=gin, dst_fn=lambda k, a, b: WAc[:, k, a:b], dst_name="WAc"):
            t()
        i = pcnt[0] % 2
        pcnt[0] += 1
        stv = pst[i]
        P.op("sp", lambda e: e.dma_start(out=stv[:, 0:2048].rearrange("p (k n) -> p k n", k=2),
                                         in_=wkv_d.rearrange("(k p) n -> p k n", p=128)),
             writes=["pst%d" % i], dma=True)
        P.op("dve", lambda e: e.memset(WKVK[:], 0.0), writes=["WKVK"])
        for rc in range(2):
            src = stv[:, rc * 1024:(rc + 1) * 1024].rearrange("p (h c) -> p h c", h=8)
            P.op("dve", lambda e, rc=rc, src=src: e.tensor_scalar(out=WKVK[:, rc, :, 0:64], in0=src[:, :, 0:64],
                                                                 scalar1=gkv[:, rc:rc + 1], scalar2=None,
                                                                 op0=ALU.mult),
                 reads=["pst%d" % i, "gkv"], writes=["WKVK"])
            P.op("dve", lambda e, rc=rc, src=src: e.tensor_scalar(
                out=WKVV[:, rc, :].rearrange("p (h c) -> p h c", h=8), in0=src[:, :, 64:128],
                scalar1=gkv[:, rc:rc + 1], scalar2=None, op0=ALU.mult),
                 reads=["pst%d" % i, "gkv"], writes=["WKVV"])
        for a in range(4):
            f2A(0, a)
        for ti in range(NT):
            bi = ti % 2
            hT = hTb[bi]
            hTn = "hTA%d" % bi
            if ti + 1 < NT:
                loadA(ti + 1)
            for a in range(4):
                b = pbank()
                for k in range(8):
                    P.op("pe", lambda e, b=b, k=k, a=a: e.matmul(pf[b][:, 0:288], lhsT=hT[:, k, a * 128:(a + 1) * 128],
                                                                 rhs=WAc[:, k, :], start=(k == 0), stop=(k == 7)),
                         reads=["WAc", "%s_%d" % (hTn, a)], writes=[PS[b]])
                evac(pf[b][:, 0:288], ckvs[:, a, :], [PS[b]], ["ckvs"])
                sprinkle()
            rms_small(lambda a: ckvs[:, a, 0:256], 256, ssk, invk, "ckvs")
            for a in range(4):
                P.op("dve", lambda e, a=a: e.tensor_scalar(out=kvn[:, a, :], in0=ckvs[:, a, 0:256],
                                                          scalar1=invk[:, a:a + 1], scalar2=None, op0=ALU.mult),
                     reads=["ckvs", "ckvs_inv"], writes=["kvn"])
            cs = cosf[:, ti * 4:(ti + 1) * 4, :]
            sn = sinf[:, ti * 4:(ti + 1) * 4, :]
            x1 = ckvs[:, :, 256:272]
            x2 = ckvs[:, :, 272:288]
            P.op("dve", lambda e: e.tensor_tensor(out=rt1[:], in0=x1, in1=cs, op=ALU.mult), reads=["ckvs", "cosf"], writes=["rt1"])
            P.op("dve", lambda e: e.tensor_tensor(out=rt2[:], in0=x2, in1=sn, op=ALU.mult), reads=["ckvs", "sinf"], writes=["rt2"])
            P.op("dve", lambda e: e.tensor_tensor(out=kr[:, :, 0:16], in0=rt1[:], in1=rt2[:], op=ALU.subtract),
                 reads=["rt1", "rt2"], writes=["kr1"])
            P.op("dve", lambda e: e.tensor_tensor(out=rt3[:], in0=x1, in1=sn, op=ALU.mult), reads=["ckvs", "sinf"], writes=["rt3"])
            P.op("dve", lambda e: e.tensor_tensor(out=rt4[:], in0=x2, in1=cs, op=ALU.mult), reads=["ckvs", "cosf"], writes=["rt4"])
            P.op("dve", lambda e: e.tensor_tensor(out=kr[:, :, 16:32], in0=rt3[:], in1=rt4[:], op=ALU.add),
                 reads=["rt3", "rt4"], writes=["kr2"])
            for hp in range(4):
                b = pbank()
                for k in range(8):
                    P.op("pe", lambda e, b=b, k=k, hp=hp: e.matmul(pf[b][:, :], lhsT=WAk[:, k, hp * 128:(hp + 1) * 128],
                                                                  rhs=hT[:, k, :], start=(k == 0), stop=(k == 7)),
                         reads=["WAk"] + ["%s_%d" % (hTn, q) for q in range(4)], writes=[PS[b]])
                evac(pf[b][:, :], kst[bi][:, hp, :], [PS[b]], ["kst%d" % bi])
                sprinkle()
                if ti + 1 < NT:
                    f1A(ti + 1, hp)
            P.op("pool", lambda e, ti=ti, bi=bi: e.dma_start(
                out=kTsb_d[:, :, ti * 512:(ti + 1) * 512].rearrange("h p s -> p h s"), in_=kst[bi][:]),
                 reads=["kst%d" % bi], writes=["kTsb_d"], dma=True)
            for a in range(4):
                for rc in range(2):
                    P.op("pe", lambda e, a=a, rc=rc: e.transpose(
                        out=pbv[6][:, rc * 512 + a * 128: rc * 512 + (a + 1) * 128],
                        in_=kvn[:, a, rc * 128:(rc + 1) * 128], identity=identb),
                         reads=["kvn", "cstb"], writes=[PS[6]])
                P.op("pe", lambda e, a=a: e.transpose(out=pbv[7][0:32, a * 128:(a + 1) * 128], in_=kr[:, a, :],
                                                      identity=identb),
                     reads=["kr1", "kr2", "cstb"], writes=[PS[7]])
            evac(pbv[6][:, :].rearrange("p (k t) -> p k t", k=2), kvnT[bi][:], [PS[6]], ["kvnT%d" % bi])
            evac(pbv[7][0:32, 0:512], krT[bi][0:32, :], [PS[7]], ["krT%d" % bi])
            for a in range(4):
                b = pbank()
                for k in range(8):
                    P.op("pe", lambda e, b=b, k=k, a=a: e.matmul(pf[b][:, :], lhsT=hT[:, k, a * 128:(a + 1) * 128],
                                                                 rhs=WAv[:, k, :], start=(k == 0), stop=(k == 7)),
                         reads=["WAv", "%s_%d" % (hTn, a)], writes=[PS[b]])
                evac(pf[b][:, :], vst[bi][:, a, :], [PS[b]], ["vst%d" % bi])
                sprinkle()
            P.op("pool", lambda e, ti=ti, bi=bi: e.dma_start(
                out=vsb_d[ti * 512:(ti + 1) * 512, :].rearrange("(a p) c -> p a c", p=128), in_=vst[bi][:]),
                 reads=["vst%d" % bi], writes=["vsb_d"], dma=True)
            for h in range(8):
                b = pbank()
                P.op("pe", lambda e, b=b, h=h: e.matmul(pf[b][:, :], lhsT=WKVK[:, 0, h, :], rhs=kvnT[bi][:, 0, :],
                                                        start=True, stop=False),
                     reads=["WKVK", "kvnT%d" % bi], writes=[PS[b]])
                P.op("pe", lambda e, b=b, h=h: e.matmul(pf[b][:, :], lhsT=WKVK[:, 1, h, :], rhs=kvnT[bi][:, 1, :],
                                                        start=False, stop=False),
                     reads=["WKVK", "kvnT%d" % bi], writes=[PS[b]])
                P.op("pe", lambda e, b=b: e.matmul(pf[b][:, :], lhsT=eselb[:, :], rhs=krT[bi][:, :],
                                                   start=False, stop=True),
                     reads=["eselb", "krT%d" % bi], writes=[PS[b]])
                evac(pf[b][0:96, :], kfst[bi][:, h, :], [PS[b]], ["kfst%d" % bi])
                sprinkle()
            P.op("pool", lambda e, ti=ti, bi=bi: e.dma_start(
                out=kfT_d[:, :, ti * 512:(ti + 1) * 512].rearrange("h p s -> p h s"), in_=kfst[bi][:]),
                 reads=["kfst%d" % bi], writes=["kfT_d"], dma=True)
            for a in range(4):
                b = pbank()
                for rc in range(2):
                    P.op("pe", lambda e, b=b, rc=rc, a=a: e.matmul(pf[b][:, :], lhsT=kvnT[bi][:, rc, a * 128:(a + 1) * 128],
                                                                   rhs=WKVV[:, rc, :], start=(rc == 0), stop=(rc == 1)),
                         reads=["WKVV", "kvnT%d" % bi], writes=[PS[b]])
                evac(pf[b][:, :], vmst[bi][:, a, :], [PS[b]], ["vmst%d" % bi])
                sprinkle()
            P.op("pool", lambda e, ti=ti, bi=bi: e.dma_start(
                out=vmla_d[ti * 512:(ti + 1) * 512, :].rearrange("(a p) c -> p a c", p=128), in_=vmst[bi][:]),
                 reads=["vmst%d" % bi], writes=["vmla_d"], dma=True)
            if ti + 1 < NT:
                for a in range(4):
                    f2A(ti + 1, a)
        while later:
            later.pop(0)()
        P.barrier()
    spp.close()

    SGsb = sb(root, "SGsb", [128, 4, SO], BF16)
    SGmla = sb(root, "SGmla", [128, 4, SO], BF16)
    sq = ExitStack()
    QTsb = sb(sq, "QTsb", [128, 4, SO], BF16)
    QFT = sb(sq, "QFT", [128, 8, SO], BF16)
    P.op("pool", lambda e: e.memset(QFT[64:128, :, :], 0.0), writes=["QFT"])

    with ExitStack() as sbk:
        xbuf = [sb(sbk, "xB%d" % i, [128, 1024], F32) for i in range(4)]
        ss8 = sb(sbk, "ss8B", [128, 8], F32)
        inv8 = sb(sbk, "inv8B", [128, 8], F32)
        hnb = [sb(sbk, "hnB%d" % i, [128, 4, 1024], BF16) for i in range(2)]
        hTb = [sb(sbk, "hTB%d" % i, [128, 8, 512], BF16) for i in range(2)]
        loadB, f1B, f2B = make_front(xo, xbuf, ss8, inv8, hnb, hTb, "B", [0, 1, 6, 7])
        cqs = sb(sbk, "cqs", [128, 4, 384], F32)
        ssq_ = sb(sbk, "ssq", [128, 4], F32)
        invq = sb(sbk, "invq", [128, 4], F32)
        cqn = sb(sbk, "cqn", [128, 4, 384], BF16)
        cqnTb = [sb(sbk, "cqnT%d" % i, [128, 3, 512], BF16) for i in range(2)]
        qtasks = []
        qf = [sb(sbk, "qf%d" % i, [128, 8, 96], BF16) for i in range(2)]
        qt1 = sb(sbk, "qt1", [128, 4, 16], F32)
        qt2 = sb(sbk, "qt2", [128, 4, 16], F32)
        qt3 = sb(sbk, "qt3", [128, 4, 16], F32)
        qt4 = sb(sbk, "qt4", [128, 4, 16], F32)
        coso = sb(sbk, "coso", [128, NBO, 4, 16], F32)
        sino = sb(sbk, "sino", [128, NBO, 4, 16], F32)
        P.op("sp", lambda e: e.dma_start(out=coso[:], in_=coso_d), writes=["coso"], dma=True)
        P.op("sp", lambda e: e.dma_start(out=sino[:], in_=sino_d), writes=["sino"], dma=True)

        xc = [0]
        pj = [0]

        def pbank():
            b = 2 + (pj[0] % 4)
            pj[0] += 1
            return b

        loadB(0)
        for a in range(4):
            f1B(0, a)
        for a in range(4):
            f2B(0, a)
        for tq in range(NQ):
            bi = tq % 2
            hT = hTb[bi]
            hTn = "hTB%d" % bi
            cols = slice(tq * 512, (tq + 1) * 512)
            if tq + 1 < NQ:
                loadB(tq + 1)
            P.op("pool", lambda e, tq=tq, hT=hT: e.dma_start(out=hoT_d[:, :, tq * 512:(tq + 1) * 512], in_=hT[:]),
                 reads=["%s_%d" % (hTn, q) for q in range(4)], writes=["hoT_d"], dma=True)
            for a in range(4):
                b = pbank()
                for k in range(8):
                    P.op("pe", lambda e, b=b, k=k, a=a: e.matmul(pf[b][:, 0:384], lhsT=hT[:, k, a * 128:(a + 1) * 128],
                                                                 rhs=WBc[:, k, :], start=(k == 0), stop=(k == 7)),
                         reads=["WBc", "%s_%d" % (hTn, a)], writes=[PS[b]])
                evac(pf[b][:, 0:384], cqs[:, a, :], [PS[b]], ["cqs"])
                if qtasks:
                    qtasks.pop(0)()
            rms_small(lambda a: cqs[:, a, :], 384, ssq_, invq, "cqs")
            for a in range(4):
                P.op("dve", lambda e, a=a: e.tensor_scalar(out=cqn[:, a, :], in0=cqs[:, a, :], scalar1=invq[:, a:a + 1],
                                                          scalar2=None, op0=ALU.mult),
                     reads=["cqs", "cqs_inv"], writes=["cqn"])
            for hp in range(4):
                b = pbank()
                for k in range(8):
                    P.op("pe", lambda e, b=b, k=k, hp=hp: e.matmul(pf[b][:, :], lhsT=WBq[:, k, hp * 128:(hp + 1) * 128],
                                                                  rhs=hT[:, k, :], start=(k == 0), stop=(k == 7)),
                         reads=["WBq"] + ["%s_%d" % (hTn, q) for q in range(4)], writes=[PS[b]])
                P.op("dve", lambda e, b=b, hp=hp, cols=cols: e.tensor_scalar(out=QTsb[:, hp, cols], in0=pf[b][:, :],
                                                                            scalar1=0.125, scalar2=None, op0=ALU.mult),
                     reads=[PS[b]], writes=["QTsb"])
                if tq + 1 < NQ:
                    f1B(tq + 1, hp)
                if qtasks:
                    qtasks.pop(0)()
            for (W, Wn, dstT, dn) in ((WBg, "WBg", SGsb, "SGsb"), (WBm, "WBm", SGmla, "SGmla")):
                for hp in range(4):
                    b = pbank()
                    for k in range(8):
                        P.op("pe", lambda e, b=b, k=k, hp=hp, W=W: e.matmul(pf[b][:, :], lhsT=W[:, k, hp * 128:(hp + 1) * 128],
                                                                           rhs=hT[:, k, :], start=(k == 0), stop=(k == 7)),
                             reads=[Wn] + ["%s_%d" % (hTn, q) for q in range(4)], writes=[PS[b]])
                    P.op("act", lambda e, b=b, hp=hp, dstT=dstT, cols=cols: e.activation(out=dstT[:, hp, cols], in_=pf[b][:, :],
                                                                                        func=AF.Silu),
                         reads=[PS[b]], writes=[dn])
                    if qtasks:
                        qtasks.pop(0)()
            for a in range(4):
                for rc in range(3):
                    bk = 6 if rc < 2 else 7
                    off = (rc % 2) * 512 + a * 128
                    P.op("pe", lambda e, a=a, rc=rc, bk=bk, off=off: e.transpose(
                        out=pbv[bk][:, off:off + 128], in_=cqn[:, a, rc * 128:(rc + 1) * 128], identity=identb),
                         reads=["cqn", "cstb"], writes=[PS[bk]])
            cqnT = cqnTb[tq % 2]
            cqn_name = "cqnT%d" % (tq % 2)
            evac(pbv[6][:, :].rearrange("p (k t) -> p k t", k=2), cqnT[:, 0:2, :], [PS[6]], [cqn_name])
            evac(pbv[7][:, 0:512], cqnT[:, 2, :], [PS[7]], [cqn_name])
            def qX(a, tq=tq, cqnT=cqnT, cqn_name=cqn_name):
                n = tq * 4 + a
                qfb = qf[a % 2]
                qfn = "qf%d" % (a % 2)
                for half in range(2):
                    b = pbank()
                    for rc in range(3):
                        P.op("pe", lambda e, b=b, rc=rc, a=a, half=half: e.matmul(
                            pf[b][:, 0:384], lhsT=cqnT[:, rc, a * 128:(a + 1) * 128],
                            rhs=WQ[:, rc, half * 384:(half + 1) * 384], start=(rc == 0), stop=(rc == 2)),
                             reads=["WQ", cqn_name], writes=[PS[b]])
                    pv = pf[b][:, 0:384].rearrange("p (h c) -> p h c", h=4)
                    dst = qfb[:, half * 4:(half + 1) * 4, :]
                    x1 = pv[:, :, 64:80]
                    x2 = pv[:, :, 80:96]
                    cs = coso[:, n, :, :]
                    sn = sino[:, n, :, :]
                    P.op("act", lambda e, pv=pv, dst=dst: e.copy(out=dst[:, :, 0:64], in_=pv[:, :, 0:64]),
                         reads=[PS[b]], writes=[qfn])
                    P.op("dve", lambda e, x1=x1, cs=cs: e.tensor_tensor(out=qt1[:], in0=x1, in1=cs, op=ALU.mult),
                         reads=[PS[b], "coso"], writes=["qt1"])
                    P.op("dve", lambda e, x2=x2, sn=sn: e.tensor_tensor(out=qt2[:], in0=x2, in1=sn, op=ALU.mult),
                         reads=[PS[b], "sino"], writes=["qt2"])
                    P.op("dve", lambda e, dst=dst: e.tensor_tensor(out=dst[:, :, 64:80], in0=qt1[:], in1=qt2[:], op=ALU.subtract),
                         reads=["qt1", "qt2"], writes=[qfn])
                    P.op("dve", lambda e, x1=x1, sn=sn: e.tensor_tensor(out=qt3[:], in0=x1, in1=sn, op=ALU.mult),
                         reads=[PS[b], "sino"], writes=["qt3"])
                    P.op("dve", lambda e, x2=x2, cs=cs: e.tensor_tensor(out=qt4[:], in0=x2, in1=cs, op=ALU.mult),
                         reads=[PS[b], "coso"], writes=["qt4"])
                    P.op("dve", lambda e, dst=dst: e.tensor_tensor(out=dst[:, :, 80:96], in0=qt3[:], in1=qt4[:], op=ALU.add),
                         reads=["qt3", "qt4"], writes=[qfn])

            def qY(a, tq=tq):
                qfb = qf[a % 2]
                qfn = "qf%d" % (a % 2)
                bk = a % 2
                for h in range(8):
                    P.op("pe", lambda e, h=h, bk=bk, qfb=qfb: e.transpose(out=pbv[bk][0:96, h * 128:(h + 1) * 128],
                                                                         in_=qfb[:, h, :], identity=identb),
                         reads=[qfn, "cstb"], writes=[PS[bk]])
                c0 = tq * 512 + a * 128
                evac(pbv[bk][0:96, :].rearrange("p (h t) -> p h t", h=8), QFT[0:96, :, c0:c0 + 128], [PS[bk]], ["QFT"])

            qtasks.append(lambda qX=qX: qX(0))
            for a in range(4):
                if a + 1 < 4:
                    qtasks.append(lambda qX=qX, a=a: qX(a + 1))
                qtasks.append(lambda qY=qY, a=a: qY(a))
            if tq + 1 < NQ:
                for a in range(4):
                    f2B(tq + 1, a)
        while qtasks:
            qtasks.pop(0)()
        P.barrier()
    swb.close()

    if debug:
        dq1 = dram("dbg_qtsb", [128, 4, SO], BF16, "ExternalOutput")
        dq2 = dram("dbg_qft", [96, 8, SO], BF16, "ExternalOutput")
        dq3 = dram("dbg_sgsb", [128, 4, SO], BF16, "ExternalOutput")
        dq4 = dram("dbg_sgmla", [128, 4, SO], BF16, "ExternalOutput")
        P.op("sp", lambda e: e.dma_start(out=dq1, in_=QTsb[:]), reads=["QTsb"], writes=["dq1"], dma=True)
        P.op("sp", lambda e: e.dma_start(out=dq2, in_=QFT[0:96]), reads=["QFT"], writes=["dq2"], dma=True)
        P.op("sp", lambda e: e.dma_start(out=dq3, in_=SGsb[:]), reads=["SGsb"], writes=["dq3"], dma=True)
        P.op("sp", lambda e: e.dma_start(out=dq4, in_=SGmla[:]), reads=["SGmla"], writes=["dq4"], dma=True)
        P.barrier()

    def blocks(I):
        out = []
        for kb in range(16 * I + 15, -1, -1):
            m = kb - 16 * I
            if m >= 0:
                out.append((kb, 128 * (m // 4) + 32 * (m % 4), m % 4))
            else:
                out.append((kb, 0, None))
        return out

    with ExitStack() as sc:
        NCH = 4
        CB = NB // NCH
        KT = sb(sc, "KT", [128, S], BF16)
        VV = sb(sc, "VV", [128, NB, 128], BF16)
        KF = sb(sc, "KF", [128, S], BF16)
        VMx = [sb(sc, "VMe", [128, NB, 128], BF16), sb(sc, "VMo", [128, NB, 128], BF16)]
        rhi = sb(sc, "rhi", [128, 512], BF16)
        rlo = sb(sc, "rlo", [128, 512], BF16)
        ocp = sb(sc, "ocp", [128, 512], F32)
        eb = [sb(sc, "eb%d" % i, [128, 512], F32) for i in range(2)]
        spb = [sb(sc, "spb%d" % i, [128, 512], BF16) for i in range(3)]
        wb = [sb(sc, "wb%d" % i, [128, 512], BF16) for i in range(3)]
        ssum = [sb(sc, "ssum%d" % i, [128, 512], BF16) for i in range(3)]
        pbuf = [sb(sc, "pbuf%d" % i, [128, 512], BF16) for i in range(3)]
        rec = sb(sc, "rec", [128, 512], F32)
        otmp = sb(sc, "otmp", [128, 512], F32)
        SCALE = float(96 ** -0.5)
        B_SSB = [0, 1, 2, 3]
        B_SML = [4]
        B_OSB = 5
        B_OML = 6
        B_DML = 7

        def ld_kt(hp, c):
            P.op("sp", lambda e: e.dma_start(out=KT[:, c * CB * 128:(c + 1) * CB * 128],
                                             in_=kTsb_d[hp][:, c * CB * 128:(c + 1) * CB * 128]),
                 reads=["kTsb_d"], writes=["KTc%d" % c], dma=True)

        def ld_vv(hp, c):
            P.op("sp", lambda e: e.dma_start(
                out=VV[:, c * CB:(c + 1) * CB, :],
                in_=vsb_d[c * CB * 128:(c + 1) * CB * 128, hp * 128:(hp + 1) * 128].rearrange("(n p) c -> p n c", p=128)),
                 reads=["vsb_d"], writes=["VVc%d" % c], dma=True)

        def ld_kf(h, c):
            P.op("sp", lambda e: e.dma_start(out=KF[0:96, c * CB * 128:(c + 1) * CB * 128],
                                             in_=kfT_d[h][:, c * CB * 128:(c + 1) * CB * 128]),
                 reads=["kfT_d"], writes=["KFc%d" % c], dma=True)

        def ld_vm(h, c):
            par = h % 2
            vo = 0 if par == 0 else 64
            P.op("sp", lambda e: e.dma_start(
                out=VMx[par][:, c * CB:(c + 1) * CB, vo:vo + 64],
                in_=vmla_d[c * CB * 128:(c + 1) * CB * 128, h * 64:(h + 1) * 64].rearrange("(n p) c -> p n c", p=128)),
                 reads=["vmla_d"], writes=["VM%dc%d" % (par, c)], dma=True)

        jobs = []
        gidx = -1
        for h in range(8):
            for I in range(NQ - 1, -1, -1):
                bl = blocks(I)
                for j, (kb, c0, mi) in enumerate(bl):
                    if j == 0:
                        gidx += 1
                    jobs.append(dict(hp=h // 2, h=h, I=I, kb=kb, c0=c0, mi=mi, first=(j == 0), last=(j == len(bl) - 1),
                                     g=gidx, k=j, ch=kb // CB))
        nj = len(jobs)
        pc0 = 512
        for jb in jobs:
            if jb["first"]:
                pc0 = 512
            jb["pc0"] = pc0
            pc0 = jb["c0"]
        trig = {}
        PD = 5
        for idx, jb in enumerate(jobs):
            nxt = jobs[idx + 1] if idx + 1 < nj else None
            h, c = jb["h"], jb["ch"]
            if c * CB // 16 == jb["I"] or True:
                pass
        lastread = {}
        for idx, jb in enumerate(jobs):
            lastread[(jb["h"], jb["ch"])] = idx
        for (h, c), idx in lastread.items():
            trig.setdefault(idx + PD, []).append((h, c))

        P.op("dve", lambda e: e.memset(VMx[0][:, :, 64:128], 1.0), writes=["VM0c%d" % c for c in range(NCH)])
        P.op("dve", lambda e: e.memset(VMx[1][:, :, 0:64], 1.0), writes=["VM1c%d" % c for c in range(NCH)])
        P.op("pool", lambda e: e.memset(KF[64:128, :], 0.0), writes=["KFc%d" % c for c in range(NCH)])
        P.op("dve", lambda e: e.memset(rhi[:], 0.0), writes=["rhi"])
        P.op("dve", lambda e: e.memset(rlo[:], 0.0), writes=["rlo"])
        for c in range(NCH - 1, -1, -1):
            ld_kt(0, c)
            ld_vv(0, c)
            ld_kf(0, c)
            ld_vm(0, c)
        for c in range(NCH - 1, -1, -1):
            ld_vm(1, c)

        def sb1(j, jb):
            hp, h, I, kb, c0, mi, ch = jb["hp"], jb["h"], jb["I"], jb["kb"], jb["c0"], jb["mi"], jb["ch"]
            po = (h % 2) * 64
            bk = B_SSB[j % 4]
            kt = KT[po:po + 64, kb * 128:(kb + 1) * 128]
            qt = QTsb[po:po + 64, hp, I * 512 + c0:(I + 1) * 512]
            P.op("pe", lambda e: e.matmul(pf[bk][:, c0:512], lhsT=kt, rhs=qt, start=True, stop=(mi is None)),
                 reads=["KTc%d" % ch, "QTsb"], writes=[PS[bk]])
            if mi is not None:
                mo = 32 * mi
                P.op("pe", lambda e: e.matmul(pf[bk][:, c0:c0 + 128 - mo], lhsT=identb, rhs=mskb[:, mi, mo:128], start=False, stop=True),
                     reads=["cstb", "mskb"], writes=[PS[bk]])
            P.op("act", lambda e: e.activation(out=eb[j % 2][:, c0:512], in_=pf[bk][:, c0:512], func=AF.Exp),
                 reads=[PS[bk]], writes=["eb%d" % (j % 2)])

        def sb2(j, jb):
            c0 = jb["c0"]
            s3 = j % 3
            P.op("act", lambda e: e.activation(out=spb[s3][:, c0:512], in_=eb[j % 2][:, c0:512], func=AF.Ln, bias=1.0),
                 reads=["eb%d" % (j % 2)], writes=["spb%d" % s3])
            if not jb["last"]:
                rb = j % 3
                wbf = (j + 1) % 3
                if jb["first"]:
                    P.op("dve", lambda e: e.tensor_copy(out=ssum[wbf][:, c0:512], in_=spb[s3][:, c0:512]),
                         reads=["spb%d" % s3], writes=["ssum%d" % wbf])
                else:
                    pc0 = jb["pc0"]
                    P.op("dve", lambda e: e.tensor_tensor(out=ssum[wbf][:, pc0:512], in0=ssum[rb][:, pc0:512],
                                                          in1=spb[s3][:, pc0:512], op=ALU.add),
                         reads=["ssum%d" % rb, "spb%d" % s3], writes=["ssum%d" % wbf])
                    if pc0 > c0:
                        P.op("dve", lambda e: e.tensor_copy(out=ssum[wbf][:, c0:pc0], in_=spb[s3][:, c0:pc0]),
                             reads=["spb%d" % s3], writes=["ssum%d" % wbf])

        def sb3(j, jb):
            c0 = jb["c0"]
            ab = B_SSB[j % 4]
            s3 = j % 3
            gpar = j % 3
            if not jb["first"]:
                pc0 = jb["pc0"]
                P.op("pe", lambda e: e.matmul(pf[ab][:, pc0:512], lhsT=monesb, rhs=ssum[gpar][:, pc0:512],
                                              start=False, stop=False, skip_group_check=True),
                     reads=["cstb", "ssum%d" % gpar], writes=[PS[ab]])
            P.op("pe", lambda e: e.matmul(pf[ab][:, c0:512], lhsT=trib, rhs=spb[s3][:, c0:512],
                                          start=False, stop=True, skip_group_check=True),
                 reads=["cstb", "spb%d" % s3], writes=[PS[ab]])
            P.op("act", lambda e: e.activation(out=wb[s3][:, c0:512], in_=pf[ab][:, c0:512], func=AF.Exp),
                 reads=[PS[ab]], writes=["wb%d" % s3])

        def sb4(j, jb):
            hp, h, I, kb, c0, ch = jb["hp"], jb["h"], jb["I"], jb["kb"], jb["c0"], jb["ch"]
            s3 = j % 3
            ob = B_OSB
            po = (h % 2) * 64
            if jb["first"]:
                P.op("pe", lambda e: e.matmul(pf[ob][:, 0:512], lhsT=VV[:, kb, :], rhs=wb[s3][:, 0:512],
                                              start=True, stop=jb["last"], skip_group_check=True),
                     reads=["VVc%d" % ch, "wb%d" % s3], writes=[PS[ob]])
            else:
                P.op("pe", lambda e: e.matmul(pf[ob][:, c0:512], lhsT=VV[:, kb, :], rhs=wb[s3][:, c0:512],
                                              start=False, stop=jb["last"], skip_group_check=True),
                     reads=["VVc%d" % ch, "wb%d" % s3], writes=[PS[ob]])
            if jb["last"]:
                cols = slice(I * 512, (I + 1) * 512)
                P.op("dve", lambda e: e.tensor_tensor(out=SGsb[po:po + 64, hp, cols], in0=pf[ob][po:po + 64, :],
                                                      in1=SGsb[po:po + 64, hp, cols], op=ALU.mult),
                     reads=[PS[ob], "SGsb"], writes=["SGsb"])

        def ml1(j, jb):
            hp, h, I, kb, c0, mi, ch = jb["hp"], jb["h"], jb["I"], jb["kb"], jb["c0"], jb["mi"], jb["ch"]
            bk = B_SML[0]
            s3 = j % 3
            P.op("pe", lambda e: e.matmul(pf[bk][:, c0:512], lhsT=KF[:, kb * 128:(kb + 1) * 128],
                                          rhs=QFT[:, h, I * 512 + c0:(I + 1) * 512], start=True, stop=(mi is None)),
                 reads=["KFc%d" % ch, "QFT"], writes=[PS[bk]])
            if mi is not None:
                mo = 32 * mi
                P.op("pe", lambda e: e.matmul(pf[bk][:, c0:c0 + 128 - mo], lhsT=identb, rhs=mskb[:, 4 + mi, mo:128], start=False, stop=True),
                     reads=["cstb", "mskb"], writes=[PS[bk]])
            P.op("act", lambda e: e.activation(out=pbuf[s3][:, c0:512], in_=pf[bk][:, c0:512], func=AF.Exp, scale=SCALE),
                 reads=[PS[bk]], writes=["pbuf%d" % s3])

        def ml2(j, jb):
            hp, h, I, kb, c0, ch = jb["hp"], jb["h"], jb["I"], jb["kb"], jb["c0"], jb["ch"]
            s3 = j % 3
            ob = B_OML
            db = B_DML
            par = h % 2
            po = par * 64
            dq = 64 - po
            if jb["first"]:
                P.op("pe", lambda e: e.matmul(pf[ob][:, 0:512], lhsT=VMx[par][:, kb, :], rhs=pbuf[s3][:, 0:512],
                                              start=True, stop=jb["last"], skip_group_check=True),
                     reads=["VM%dc%d" % (par, ch), "pbuf%d" % s3], writes=[PS[ob]])
            else:
                P.op("pe", lambda e: e.matmul(pf[ob][:, c0:512], lhsT=VMx[par][:, kb, :], rhs=pbuf[s3][:, c0:512],
                                              start=False, stop=jb["last"], skip_group_check=True),
                     reads=["VM%dc%d" % (par, ch), "pbuf%d" % s3], writes=[PS[ob]])
            if jb["last"]:
                cols = slice(I * 512, (I + 1) * 512)
                P.op("dve", lambda e: e.tensor_copy(out=ocp[:, :], in_=pf[ob][:, :]), reads=[PS[ob]], writes=["ocp"])
                def fin1(q):
                    if q == 0:
                        P.op("act", lambda e: e.activation(out=rec[dq:dq + 64, :], in_=ocp[dq:dq + 64, :], func=AF.Ln),
                             reads=["ocp"], writes=["rec"])
                    else:
                        P.op("act", lambda e: e.activation(out=rec[dq:dq + 64, :], in_=rec[dq:dq + 64, :], func=AF.Exp,
                                                           scale=-1.0),
                             reads=["rec"], writes=["rec"])

                def fin1b():
                    P.op("dve", lambda e: e.tensor_copy(out=rhi[dq:dq + 64, :], in_=rec[dq:dq + 64, :]),
                         reads=["rec"], writes=["rhi"])
                    P.op("dve", lambda e: e.tensor_tensor(out=rlo[dq:dq + 64, :], in0=rec[dq:dq + 64, :],
                                                          in1=rhi[dq:dq + 64, :], op=ALU.subtract),
                         reads=["rec", "rhi"], writes=["rlo"])
                deferred.setdefault(cur[0] + 2, []).append(lambda: fin1(0))
                deferred.setdefault(cur[0] + 3, []).append(lambda: fin1(1))
                deferred.setdefault(cur[0] + 4, []).append(fin1b)
                def fin2():
                    P.op("pe", lambda e: e.matmul(pf[db][:, :], lhsT=swapb, rhs=rhi[:, :], start=True, stop=False),
                         reads=["cstb", "rhi"], writes=[PS[db]])
                    P.op("pe", lambda e: e.matmul(pf[db][:, :], lhsT=swapb, rhs=rlo[:, :], start=False, stop=True),
                         reads=["cstb", "rlo"], writes=[PS[db]])
                    P.op("dve", lambda e: e.tensor_tensor(out=otmp[po:po + 64, :], in0=pf[db][po:po + 64, :],
                                                          in1=ocp[po:po + 64, :], op=ALU.mult),
                         reads=[PS[db], "ocp"], writes=["otmp"])
                    P.op("dve", lambda e: e.tensor_tensor(out=SGmla[po:po + 64, hp, cols], in0=otmp[po:po + 64, :],
                                                          in1=SGmla[po:po + 64, hp, cols], op=ALU.mult),
                         reads=["otmp", "SGmla"], writes=["SGmla"])
                deferred.setdefault(cur[0] + 7, []).append(fin2)

        deferred = {}
        cur = [0]
        for j, jb in enumerate(jobs):
            if jb["first"] and jb["c0"] > 0:
                def zp(j=j, c0=jb["c0"]):
                    P.op("pool", lambda e: e.memset(pbuf[j % 3][:, 0:c0], 0.0), writes=["pbuf%d" % (j % 3)])

                def zw(j=j, c0=jb["c0"]):
                    P.op("pool", lambda e: e.memset(wb[j % 3][:, 0:c0], 0.0), writes=["wb%d" % (j % 3)])
                deferred.setdefault(max(j - 1, -1), []).append(zp)
                deferred.setdefault(j + 1, []).append(zw)
        for f in deferred.pop(-1, []):
            f()
        for step in range(nj + 3 + PD):
            cur[0] = step
            for (h, c) in trig.get(step, []):
                if h + 1 < 8:
                    ld_kf(h + 1, c)
                if h + 2 < 8:
                    ld_vm(h + 2, c)
                if h % 2 == 1 and h // 2 + 1 < 4:
                    ld_kt(h // 2 + 1, c)
                    ld_vv(h // 2 + 1, c)
            if step < nj:
                sb1(step, jobs[step])
                ml1(step, jobs[step])
            if 0 <= step - 1 < nj:
                sb2(step - 1, jobs[step - 1])
            if 0 <= step - 2 < nj:
                sb3(step - 2, jobs[step - 2])
            if 0 <= step - 1 < nj:
                ml2(step - 1, jobs[step - 1])
            if 0 <= step - 3 < nj:
                sb4(step - 3, jobs[step - 3])
            for f in deferred.pop(step, []):
                f()
        for k in sorted(deferred):
            for f in deferred[k]:
                f()
        P.barrier()

    if debug:
        dq5 = dram("dbg_ogsb", [128, 4, SO], BF16, "ExternalOutput")
        dq6 = dram("dbg_ogmla", [128, 4, SO], BF16, "ExternalOutput")
        P.op("sp", lambda e: e.dma_start(out=dq5, in_=SGsb[:]), reads=["SGsb"], writes=["dq5"], dma=True)
        P.op("sp", lambda e: e.dma_start(out=dq6, in_=SGmla[:]), reads=["SGmla"], writes=["dq6"], dma=True)
        P.barrier()
    sq.close()

    with ExitStack() as se:
        WGL = sb(se, "WGL", [128, 8, 2048], BF16)
        WOS = sb(se, "WOS", [128, 4, 1024], BF16)
        WOM = sb(se, "WOM", [128, 4, 1024], BF16)
        WOUT = sb(se, "WOUT", [128, 8, 1024], BF16)
        P.op("sp", lambda e: e.dma_start(out=WGL[:], in_=wgl_s), reads=["wgl_s"], writes=["WGL"], dma=True)
        P.op("sp", lambda e: e.dma_start(out=WOS[:], in_=wos_s), reads=["wos_s"], writes=["WOS"], dma=True)
        P.op("sp", lambda e: e.dma_start(out=WOM[:], in_=wom_s), reads=["wom_s"], writes=["WOM"], dma=True)
        P.op("sp", lambda e: e.dma_start(out=WOUT[:], in_=wout_s), reads=["wout_s"], writes=["WOUT"], dma=True)
        hoT = [sb(se, "hoT%d" % i, [128, 8, 512], BF16) for i in range(2)]
        G = sb(se, "G", [128, 16, 512], BF16)
        mg = [sb(se, "mg%d" % i, [128, 8, 512], BF16) for i in range(2)]
        mt1 = [sb(se, "mt1_%d" % i, [128, 512], F32) for i in range(2)]
        mt2 = [sb(se, "mt2_%d" % i, [128, 512], F32) for i in range(2)]
        xob = [sb(se, "xob%d" % i, [128, 1024], F32) for i in range(4)]
        resb = [sb(se, "resb%d" % i, [128, 1024], F32) for i in range(2)]
        gfs = sb(se, "gfs", [128, 1024], F32)
        ssf = sb(se, "ssf", [128, 2], F32)
        invf = sb(se, "invf", [128, 2], F32)
        pend = []
        P.op("sp", lambda e: e.dma_start(out=gfs[:], in_=gf_d), writes=["gfs"], dma=True)
        pj = [0]

        def ld_hoT(tq):
            hT = hoT[tq % 2]
            P.op("sp", lambda e: e.dma_start(out=hT[:], in_=hoT_d[:, :, tq * 512:(tq + 1) * 512]),
                 reads=["hoT_d"], writes=["hoT%d" % (tq % 2)], dma=True)

        def e_gl(tq):
            hT = hoT[tq % 2]
            hTn = "hoT%d" % (tq % 2)
            for mt in range(16):
                b = pj[0] % 2
                pj[0] += 1
                for k in range(8):
                    P.op("pe", lambda e, b=b, k=k, mt=mt: e.matmul(pf[b][:, :], lhsT=WGL[:, k, mt * 128:(mt + 1) * 128],
                                                                  rhs=hT[:, k, :], start=(k == 0), stop=(k == 7)),
                         reads=["WGL", hTn], writes=[PS[b]])
                P.op("act", lambda e, b=b, mt=mt: e.activation(out=G[:, mt, :], in_=pf[b][:, :], func=AF.Sigmoid,
                                                               bias=bg[:, mt:mt + 1]),
                     reads=[PS[b], "bg"], writes=["G"])

        def e_y(tq):
            cols = slice(tq * 512, (tq + 1) * 512)
            mgb = mg[tq % 2]
            for et in range(8):
                b1 = 2 + (et % 2) * 2
                b2 = b1 + 1
                m1 = mt1[et % 2]
                m2 = mt2[et % 2]
                for hp in range(4):
                    P.op("pe", lambda e, b1=b1, hp=hp, et=et: e.matmul(pf[b1][:, :], lhsT=WOS[:, hp, et * 128:(et + 1) * 128],
                                                                      rhs=SGsb[:, hp, cols], start=(hp == 0), stop=(hp == 3)),
                         reads=["WOS", "SGsb"], writes=[PS[b1]])
                for hp in range(4):
                    P.op("pe", lambda e, b2=b2, hp=hp, et=et: e.matmul(pf[b2][:, :], lhsT=WOM[:, hp, et * 128:(et + 1) * 128],
                                                                      rhs=SGmla[:, hp, cols], start=(hp == 0), stop=(hp == 3)),
                         reads=["WOM", "SGmla"], writes=[PS[b2]])
                P.op("dve", lambda e, b1=b1, et=et, m1=m1: e.tensor_tensor(out=m1[:], in0=pf[b1][:, :], in1=G[:, et, :], op=ALU.mult),
                     reads=[PS[b1], "G"], writes=["mt1_%d" % (et % 2)])
                P.op("dve", lambda e, b2=b2, et=et, m2=m2: e.tensor_tensor(out=m2[:], in0=pf[b2][:, :], in1=G[:, 8 + et, :], op=ALU.mult),
                     reads=[PS[b2], "G"], writes=["mt2_%d" % (et % 2)])
                P.op("pool", lambda e, et=et, m1=m1, m2=m2, mgb=mgb: e.tensor_tensor(out=mgb[:, et, :], in0=m1[:], in1=m2[:], op=ALU.add),
                     reads=["mt1_%d" % (et % 2), "mt2_%d" % (et % 2)], writes=["mg%d" % (tq % 2)])

        def ld_x(tq):
            for a in range(4):
                n = tq * 4 + a
                P.op("sp", lambda e, n=n, a=a: e.dma_start(out=xob[a][:], in_=xo[n * 128:(n + 1) * 128, :]),
                     writes=["xob%d" % a], dma=True)

        def e_out(tq):
            mgb = mg[tq % 2]
            for a in range(4):
                n = tq * 4 + a
                xi = n % 2
                for half in range(2):
                    b = 6 + half
                    for k in range(8):
                        P.op("pe", lambda e, b=b, k=k, a=a, half=half: e.matmul(
                            pf[b][:, :], lhsT=mgb[:, k, a * 128:(a + 1) * 128], rhs=WOUT[:, k, half * 512:(half + 1) * 512],
                            start=(k == 0), stop=(k == 7)),
                             reads=["mg%d" % (tq % 2), "WOUT"], writes=[PS[b]])
                    P.op("dve", lambda e, b=b, xi=xi, a=a, half=half: e.tensor_tensor(
                        out=resb[xi][:, half * 512:(half + 1) * 512], in0=pf[b][:, :],
                        in1=xob[a][:, half * 512:(half + 1) * 512], op=ALU.add),
                         reads=[PS[b], "xob%d" % a], writes=["resb%d" % xi])
                P.op("act", lambda e, xi=xi: e.activation(out=junk[:], in_=resb[xi][:], func=AF.Square,
                                                          accum_out=ssf[:, xi:xi + 1]),
                     reads=["resb%d" % xi], writes=["junk", "ssf%d" % xi])
                P.op("act", lambda e, xi=xi: e.activation(out=invf[:, xi:xi + 1], in_=ssf[:, xi:xi + 1], func=AF.Ln,
                                                          scale=1.0 / D, bias=EPS),
                     reads=["ssf%d" % xi], writes=["invf%d" % xi])
                P.op("act", lambda e, xi=xi: e.activation(out=invf[:, xi:xi + 1], in_=invf[:, xi:xi + 1], func=AF.Exp,
                                                          scale=-0.5),
                     reads=["invf%d" % xi], writes=["invf%d" % xi])

                def tail(n=n, xi=xi):
                    P.op("dve", lambda e: e.scalar_tensor_tensor(out=resb[xi][:], in0=resb[xi][:], scalar=invf[:, xi:xi + 1],
                                                                 in1=gfs[:], op0=ALU.mult, op1=ALU.mult),
                         reads=["resb%d" % xi, "invf%d" % xi, "gfs"], writes=["resb%d" % xi])
                    P.op("sp", lambda e: e.dma_start(out=out_d[n * 128:(n + 1) * 128, :], in_=resb[xi][:]),
                         reads=["resb%d" % xi], writes=["out_d"], dma=True)
                if pend:
                    pend.pop(0)()
                pend.append(tail)

        ld_hoT(0)
        for tq in range(NQ + 1):
            if tq + 1 < NQ:
                ld_hoT(tq + 1)
            if tq >= 1:
                ld_x(tq - 1)
            if tq < NQ:
                e_gl(tq)
            if tq >= 1:
                e_out(tq - 1)
            if tq < NQ:
                e_y(tq)
        while pend:
            pend.pop(0)()
        P.barrier()
    root.close()


def host_consts(S, c):
    NB = S // 128
    SO = S // 4
    NBO = SO // 128
    half = 16
    inv_freq = (np.float32(10000.0) ** (-np.arange(half, dtype=np.float32) / np.float32(half))).astype(np.float32)
    pos = np.arange(S, dtype=np.float32)
    ang = (pos[:, None] * inv_freq[None, :]).astype(np.float32)
    cosf = np.cos(ang).astype(np.float32)
    sinf = np.sin(ang).astype(np.float32)
    cf = np.ascontiguousarray(cosf.reshape(NB, 128, 16).transpose(1, 0, 2))
    sf = np.ascontiguousarray(sinf.reshape(NB, 128, 16).transpose(1, 0, 2))
    co = cosf[c::4].reshape(NBO, 128, 16).transpose(1, 0, 2)
    so = sinf[c::4].reshape(NBO, 128, 16).transpose(1, 0, 2)
    co4 = np.ascontiguousarray(np.broadcast_to(co[:, :, None, :], (128, NBO, 4, 16))).astype(np.float32)
    so4 = np.ascontiguousarray(np.broadcast_to(so[:, :, None, :], (128, NBO, 4, 16))).astype(np.float32)
    ident = np.eye(128, dtype=np.float32)
    jj = np.arange(128)
    tri = -(jj[:, None] >= jj[None, :]).astype(np.float32)
    swap = np.zeros((128, 128), np.float32)
    swap[(jj + 64) % 128, jj] = 1.0
    cst = np.ascontiguousarray(np.stack([ident, tri, -np.ones((128, 128), np.float32),
                                         np.ones((128, 128), np.float32), swap], axis=1))
    ss = np.arange(128)[:, None]
    qq = np.arange(128)[None, :]
    msb = np.zeros((128, 4, 128), np.float32)
    mmla = np.zeros((128, 4, 128), np.float32)
    for m in range(4):
        msb[:, m, :] = np.where(128 * m + ss < 4 * qq + c, 0.0, NEG)
        mmla[:, m, :] = np.where(2 * m + ss // 64 <= qq // 16, 0.0, NEG)
    esel = np.zeros((128, 128), np.float32)
    esel[np.arange(32), 64 + np.arange(32)] = 1.0
    return dict(cosf=cf, sinf=sf, coso=co4, sino=so4, cst=cst, msb=msb, mmla=mmla, esel=esel)


def make_in_maps(S, x, norm_in_g, w_in, b_gate, q_norm_g, w_q_up, kv_norm_g, w_kv_up,
                 w_o_sb, w_o_mla, w_out, norm_f_g):
    f = lambda a: np.ascontiguousarray(np.asarray(a, dtype=np.float32))
    B = x.shape[0]
    shared = dict(
        w_in=f(w_in[0]),
        gin=f(np.asarray(norm_in_g[0]).reshape(8, 128).T),
        bg=f(np.asarray(b_gate[0]).reshape(16, 128).T),
        gq=f(np.asarray(q_norm_g[0]).reshape(3, 128).T),
        wq=f(w_q_up[0]),
        gkv=f(np.asarray(kv_norm_g[0]).reshape(2, 128).T),
        wkv=f(w_kv_up[0]),
        wosb=f(w_o_sb[0]), womla=f(w_o_mla[0]), wout=f(w_out[0]),
        gf=f(np.broadcast_to(np.asarray(norm_f_g)[None, :], (128, 1024))),
    )
    consts = [host_consts(S, c) for c in range(4)]
    maps = []
    for b in range(B):
        xb = f(x[b])
        for c in range(4):
            m = dict(shared)
            m.update(consts[c])
            m["xf"] = xb
            m["xo"] = f(xb[c::4])
            maps.append(m)
    return maps


_CACHE = {}


def get_program(S, debug=False):
    key = (S, debug)
    if key not in _CACHE:
        P0 = Prog(None)
        build(P0, S, debug)
        nc = bass.Bass("TRN2", target_bir_lowering=False)
        P1 = Prog(nc, needed=P0.used)
        build(P1, S, debug)
        _CACHE[key] = nc
    return _CACHE[key]


def run(S, inputs, debug=False):
    x = np.asarray(inputs["x"])
    B = x.shape[0]
    maps = make_in_maps(S, **{k: np.asarray(v) for k, v in inputs.items()})
    nc = get_program(S, debug)
    ncores = 4 * B
    res = run_bass_kernel_spmd(nc, maps, core_ids=list(range(ncores)))
    out = np.empty((B, S, D), np.float32)
    for b in range(B):
        for c in range(4):
            out[b, c::4, :] = res.results[b * 4 + c]["out"]
    return out, res


def kernel(x, norm_in_g, w_in, b_gate, q_norm_g, w_q_up, kv_norm_g, w_kv_up,
           w_o_sb, w_o_mla, w_out, norm_f_g):
    inputs = dict(x=x, norm_in_g=norm_in_g, w_in=w_in, b_gate=b_gate, q_norm_g=q_norm_g, w_q_up=w_q_up,
                  kv_norm_g=kv_norm_g, w_kv_up=w_kv_up, w_o_sb=w_o_sb, w_o_mla=w_o_mla, w_out=w_out,
                  norm_f_g=norm_f_g)
    S = np.asarray(x).shape[1]
    out, _ = run(S, inputs)
    return out
```
